# Optimizing a Trainium2 kernel written in Bass

```python
import math
import jax, jax.numpy as jnp
from jax import lax
import numpy as np

D_MODEL = 1024
BATCH = 8
SEQ = 2048
DEPTH = 4

HEAD_DIM = 64
HA = 8
KV_A = 2
G_A = HA // KV_A
HB = 4
WA = HA * HEAD_DIM
WB = HB * 2 * HEAD_DIM
WINDOW = 128
BLK = 128
QBLK = 128
N_BUCKETS = 32
MAX_DIST = 128
D_FF = 2816
CONV_WIDTH = 3
EPS = 1e-6
QA_W, KA_W, VA_W = WA, KV_A * HEAD_DIM, KV_A * HEAD_DIM
QB_W, KB_W, VB_W = HB * 2 * HEAD_DIM, HB * 2 * HEAD_DIM, HB * 2 * HEAD_DIM
IN_W = QA_W + KA_W + VA_W + QB_W + KB_W + VB_W

kernel_name = "hybrid_swa_diffattn_convffn_encoder"


def rms_norm(x, g):
    xf = x.astype(jnp.float32)
    y = xf * lax.rsqrt(jnp.mean(xf * xf, axis=-1, keepdims=True) + EPS)
    return (y * g.astype(jnp.float32)).astype(x.dtype)


def t5_bucket(rel):
    half = N_BUCKETS // 2
    max_exact = half // 2
    ret = jnp.where(rel > 0, half, 0)
    n = jnp.abs(rel)
    nf = jnp.maximum(n, 1).astype(jnp.float32)
    large = max_exact + (jnp.log(nf / max_exact) / math.log(MAX_DIST / max_exact)
                         * (half - max_exact)).astype(jnp.int32)
    large = jnp.minimum(large, half - 1)
    return ret + jnp.where(n < max_exact, n, large)


def windowed_gqa(q, k, v, sink, bias_tab):
    B, S = q.shape[0], q.shape[1]
    nb = S // BLK
    qb = q.reshape(B, nb, BLK, KV_A, G_A, HEAD_DIM)

    def band(t):
        tp = jnp.pad(t, ((0, 0), (BLK, BLK), (0, 0), (0, 0))).reshape(B, nb + 2, BLK, KV_A, HEAD_DIM)
        return jnp.concatenate([tp[:, :-2], tp[:, 1:-1], tp[:, 2:]], axis=2)

    kw, vw = band(k), band(v)
    s = jnp.einsum('bnqkgd,bnskd->bnkgqs', qb, kw).astype(jnp.float32) * (HEAD_DIM ** -0.5)
    rel = jnp.arange(3 * BLK)[None, :] - BLK - jnp.arange(BLK)[:, None]
    kpos = (jnp.arange(nb)[:, None] - 1) * BLK + jnp.arange(3 * BLK)[None, :]
    valid = (jnp.abs(rel) <= WINDOW)[None] & ((kpos >= 0) & (kpos < S))[:, None, :]
    bias = bias_tab[t5_bucket(rel)].astype(jnp.float32).transpose(2, 0, 1)
    s = s + bias.reshape(KV_A, G_A, BLK, 3 * BLK)
    s = jnp.where(valid[None, :, None, None], s, -jnp.inf)
    sink_col = jnp.broadcast_to(sink.astype(jnp.float32).reshape(1, 1, KV_A, G_A, 1, 1),
                                s.shape[:-1] + (1,))
    p = jax.nn.softmax(jnp.concatenate([s, sink_col], axis=-1), axis=-1)[..., :-1]
    o = jnp.einsum('bnkgqs,bnskd->bnqkgd', p.astype(v.dtype), vw)
    return o.reshape(B, S, WA)


def diff_attention(q, k, v, lam, lam_init, subln_g, bias_tab):
    B, S = q.shape[0], q.shape[1]
    nb = S // QBLK
    qblocks = q.reshape(B, nb, QBLK, HB, 2, HEAD_DIM).transpose(1, 0, 2, 3, 4, 5)
    kpos = jnp.arange(S)

    def one_block(args):
        qb, i = args
        qpos = i * QBLK + jnp.arange(QBLK)
        bias = bias_tab[t5_bucket(kpos[None, :] - qpos[:, None])].astype(jnp.float32)
        bias = bias.transpose(2, 0, 1)
        s = jnp.einsum('bqhcd,bkhcd->bhcqk', qb, k).astype(jnp.float32) * (HEAD_DIM ** -0.5)
        p = jax.nn.softmax(s + bias[None, :, None], axis=-1)
        a = p[:, :, 0] - lam * p[:, :, 1]
        return jnp.einsum('bhqk,bkhe->bqhe', a.astype(v.dtype), v)

    o = lax.map(one_block, (qblocks, jnp.arange(nb)))
    o = o.transpose(1, 0, 2, 3, 4).reshape(B, S, HB, 2 * HEAD_DIM)
    o = rms_norm(o, subln_g) * (1.0 - lam_init)
    return o.reshape(B, S, WB)


def dwconv_centred(u, w, b):
    S = u.shape[1]
    pad = CONV_WIDTH // 2
    up = jnp.pad(u, ((0, 0), (pad, pad), (0, 0)))
    out = b
    for j in range(CONV_WIDTH):
        out = out + up[:, j:j + S] * w[j]
    return out


def setup_inputs(seed: int = 0) -> dict:
    key = jax.random.key(seed)
    ks = jax.random.split(key, 24)
    f32 = jnp.float32
    nrm = lambda k, shape, s: jax.random.normal(k, shape, f32) * s
    L, D = DEPTH, D_MODEL
    return {
        "x": nrm(ks[0], (BATCH, SEQ, D), 1.0),
        "ln1_g": 1.0 + nrm(ks[1], (L, D), 0.02),
        "w_in": nrm(ks[2], (L, D, IN_W), D ** -0.5),
        "qn_a": 1.0 + nrm(ks[3], (L, HEAD_DIM), 0.02),
        "kn_a": 1.0 + nrm(ks[4], (L, HEAD_DIM), 0.02),
        "sink": nrm(ks[5], (L, HA), 0.5),
        "qn_b": 1.0 + nrm(ks[6], (L, HEAD_DIM), 0.02),
        "kn_b": 1.0 + nrm(ks[7], (L, HEAD_DIM), 0.02),
        "lam_q1": nrm(ks[8], (L, HEAD_DIM), 0.1),
        "lam_k1": nrm(ks[9], (L, HEAD_DIM), 0.1),
        "lam_q2": nrm(ks[10], (L, HEAD_DIM), 0.1),
        "lam_k2": nrm(ks[11], (L, HEAD_DIM), 0.1),
        "subln_g": 1.0 + nrm(ks[12], (L, 2 * HEAD_DIM), 0.02),
        "rel_bias": nrm(ks[13], (N_BUCKETS, HA + HB), 0.5),
        "w_gate": nrm(ks[14], (L, D, 2 * D), D ** -0.5),
        "b_gate": nrm(ks[15], (L, 2 * D), 0.02),
        "w_a_proj": nrm(ks[16], (L, WA, D), WA ** -0.5),
        "w_b_proj": nrm(ks[17], (L, WB, D), WB ** -0.5),
        "w_o": nrm(ks[18], (L, D, D), D ** -0.5),
        "ln2_g": 1.0 + nrm(ks[19], (L, D), 0.02),
        "w_up": nrm(ks[20], (L, D, 2 * D_FF), D ** -0.5),
        "conv_w": nrm(ks[21], (L, CONV_WIDTH, 2 * D_FF), CONV_WIDTH ** -0.5),
        "conv_b": nrm(ks[22], (L, 2 * D_FF), 0.02),
        "w_down": nrm(ks[23], (L, D_FF, D), D_FF ** -0.5),
    }


def reference(x, ln1_g, w_in, qn_a, kn_a, sink, qn_b, kn_b, lam_q1, lam_k1, lam_q2, lam_k2,
              subln_g, rel_bias, w_gate, b_gate, w_a_proj, w_b_proj, w_o, ln2_g, w_up,
              conv_w, conv_b, w_down):
    B, S, D = x.shape
    bias_a, bias_b = rel_bias[:, :HA], rel_bias[:, HA:]
    cuts = np.cumsum([QA_W, KA_W, VA_W, QB_W, KB_W]).tolist()
    for l in range(DEPTH):
        h = rms_norm(x, ln1_g[l])
        z = h @ w_in[l]
        zqa, zka, zva, zqb, zkb, zvb = jnp.split(z, cuts, axis=-1)
        q_a = rms_norm(zqa.reshape(B, S, HA, HEAD_DIM), qn_a[l])
        k_a = rms_norm(zka.reshape(B, S, KV_A, HEAD_DIM), kn_a[l])
        v_a = zva.reshape(B, S, KV_A, HEAD_DIM)
        o_a = windowed_gqa(q_a, k_a, v_a, sink[l], bias_a)
        q_b = rms_norm(zqb.reshape(B, S, HB, 2, HEAD_DIM), qn_b[l])
        k_b = rms_norm(zkb.reshape(B, S, HB, 2, HEAD_DIM), kn_b[l])
        v_b = zvb.reshape(B, S, HB, 2 * HEAD_DIM)
        lam_init = 0.8 - 0.6 * math.exp(-0.3 * l)
        lam = (jnp.exp(jnp.sum(lam_q1[l].astype(jnp.float32) * lam_k1[l].astype(jnp.float32)))
               - jnp.exp(jnp.sum(lam_q2[l].astype(jnp.float32) * lam_k2[l].astype(jnp.float32)))
               + lam_init)
        o_b = diff_attention(q_b, k_b, v_b, lam, lam_init, subln_g[l], bias_b)
        gates = jax.nn.sigmoid(h @ w_gate[l] + b_gate[l])
        g_a, g_b = gates[..., :D], gates[..., D:]
        mix = g_a * (o_a @ w_a_proj[l]) + g_b * (o_b @ w_b_proj[l])
        x = x + mix @ w_o[l]
        h2 = rms_norm(x, ln2_g[l])
        u = dwconv_centred(h2 @ w_up[l], conv_w[l], conv_b[l])
        val, gate = u[..., :D_FF], u[..., D_FF:]
        x = x + (jax.nn.silu(gate) * val) @ w_down[l]
    return x
```

```python
import contextlib
import math
import numpy as np
import concourse.bass as bass
import concourse.mybir as mybir
from concourse.bass_utils import run_bass_kernel_spmd

F32 = mybir.dt.float32
BF16 = mybir.dt.bfloat16
AF = mybir.ActivationFunctionType
ALU = mybir.AluOpType
AX = mybir.AxisListType

D = 1024
SEQ = 2048
DEPTH = 4
NCORES = 8
D_FF = 2816
IN_W = 2304
EPS = 1e-6
NCOL = 212
C_LN1, C_LN2, C_GQA, C_GKA, C_GQB, C_GKB, C_BG, C_CW, C_CB = 0, 8, 16, 17, 18, 19, 20, 36, 168
NBROW = 392
SINK_PERM = [0, 2, 1, 3, 4, 6, 5, 7]
MASK_NEG = -30000.0

EPOCH = 2000
ENGS = ("pe", "act", "dve", "pool", "sp")


class Buf:
    __slots__ = ("w", "r", "excl")

    def __init__(self, r=None):
        self.excl = False
        self.w = None
        self.r = dict(r) if r else {}


class Sched:
    def __init__(self, nc, stack):
        self.nc = nc
        self.stack = stack
        self.ops = {e: [] for e in ENGS}
        self.seq = {e: 0 for e in ENGS}
        self.sems = {}
        self.waited = {e: {} for e in ENGS}
        self.dma_cnt = {}
        self.barrier = {}
        self.enabled = True

    def sem(self, key):
        s = self.sems.get(key)
        if s is None:
            s = self.stack.enter_context(self.nc.semaphore("s_" + "_".join(str(k) for k in key)))
            self.sems[key] = s
        return s

    def newbuf(self):
        return Buf(self.barrier)

    def mark_barrier(self):
        b = {}
        for e in ENGS:
            n = self.seq[e]
            if n > 0:
                b[(e, (n - 1) // EPOCH)] = (n - 1) % EPOCH + 1
        for k, c in self.dma_cnt.items():
            b[k] = 16 * c
        self.barrier = b

    def _deps(self, eng, reads, writes):
        deps = {}
        for b in reads:
            if b.w is not None:
                k, v = b.w
                if deps.get(k, 0) < v:
                    deps[k] = v
        for b in writes:
            if b.w is not None:
                k, v = b.w
                if deps.get(k, 0) < v:
                    deps[k] = v
            for k, v in b.r.items():
                if deps.get(k, 0) < v:
                    deps[k] = v
        out = []
        wd = self.waited[eng]
        for k, v in deps.items():
            if eng == "pe" and k[0] == "pe":
                continue
            if wd.get(k, 0) >= v:
                continue
            wd[k] = v
            out.append((self.sem(k), v))
        return out

    def _commit(self, tok, reads, writes):
        k, v = tok
        for b in writes:
            b.w = tok
            b.r = {}
        for b in reads:
            if b.r.get(k, 0) < v:
                b.r[k] = v

    def op(self, eng, fns, reads=(), writes=()):
        if not self.enabled:
            return
        if not isinstance(fns, (list, tuple)):
            fns = [fns]
        if any(b.excl for b in reads):
            writes = list(writes) + [b for b in reads if b.excl]
            reads = [b for b in reads if not b.excl]
        waits = self._deps(eng, reads, writes)
        n = self.seq[eng]
        self.seq[eng] = n + 1
        key = (eng, n // EPOCH)
        tok = (key, n % EPOCH + 1)
        self.ops[eng].append((waits, fns, self.sem(key), 1))
        self._commit(tok, reads, writes)

    def dma(self, eng, slot, pairs, reads=(), writes=(), **kw):
        if not self.enabled:
            return
        waits = self._deps(eng, reads, writes)
        key = ("dma", slot)
        sem = self.sem(key)
        c = self.dma_cnt.get(key, 0)
        first = True
        for (o, i) in pairs:
            c += 1
            fn = (lambda e, o=o, i=i: e.dma_start(out=o, in_=i, **kw))
            self.ops[eng].append((waits if first else [], [fn], sem, 16))
            first = False
        self.dma_cnt[key] = c
        self._commit((key, 16 * c), reads, writes)

    def wait_all(self, eng, bufs):
        waits = self._deps(eng, bufs, bufs)
        self.ops[eng].append((waits, [], None, 0))

    def emit(self):
        objs = {"pe": "tensor", "act": "scalar", "dve": "vector", "pool": "gpsimd", "sp": "sync"}
        with self.nc.Block() as block:
            for e in ENGS:
                lst = self.ops[e]
                if not lst:
                    continue

                def body(engobj, lst=lst):
                    for waits, fns, sem, inc in lst:
                        for s, v in waits:
                            engobj.wait_ge(s, v)
                        for fn in fns[:-1]:
                            fn(engobj)
                        if fns:
                            fns[-1](engobj).then_inc(sem, inc)

                getattr(block, objs[e])(body)


def MM(out, lhsT, rhs, start=True, stop=True):
    return lambda e: e.matmul(out, lhsT=lhsT, rhs=rhs, start=start, stop=stop, skip_group_check=True)


def ACTF(out, in_, func, **kw):
    return lambda e: e.activation(out=out, in_=in_, func=func, **kw)


def STT(out, in0, scalar, in1, op0, op1):
    return lambda e: e.scalar_tensor_tensor(out=out, in0=in0, scalar=scalar, in1=in1, op0=op0, op1=op1)


def TT(out, in0, in1, op):
    return lambda e: e.tensor_tensor(out=out, in0=in0, in1=in1, op=op)


def TS(out, in0, s1, op0, s2=None, op1=None):
    if op1 is None:
        return lambda e: e.tensor_scalar(out=out, in0=in0, scalar1=s1, scalar2=None, op0=op0)
    return lambda e: e.tensor_scalar(out=out, in0=in0, scalar1=s1, scalar2=s2, op0=op0, op1=op1)


def CP(out, in_):
    return lambda e: e.tensor_copy(out=out, in_=in_)


def RECIP(out, in_):
    return lambda e: e.reciprocal(out=out, in_=in_)


def MEMSET(ap, v):
    return lambda e: e.memset(ap, v)


class _Stop(Exception):
    pass


def build_program(layers, stop=None, dumps=()):
    NL = len(layers)
    dbg_outs = {}
    nc = bass.Bass("TRN2", target_bir_lowering=False)

    def din(name, shape):
        return nc.dram_tensor(name, list(shape), F32, kind="ExternalInput").ap()

    xT_d = din("xT", [D, SEQ])
    w_in_d = din("w_in", [NL, D, IN_W])
    w_gate_d = din("w_gate", [NL, D, 2 * D])
    w_a_d = din("w_a_proj", [NL, 512, D])
    w_b_d = din("w_b_proj", [NL, 512, D])
    w_o_d = din("w_o", [NL, D, D])
    w_up_d = din("w_up", [NL, D, 2 * D_FF])
    w_dn_d = din("w_down", [NL, D_FF, D])
    pcols_d = din("pcols", [128, NL * NCOL])
    brow_d = din("brow", [NL, NBROW])
    cfar_d = din("cfar", [8])
    strip_d = din("stripB", [4, 128, 1152])
    biasA_d = din("biasA", [128, 2 * 3 * 512])
    ident_d = din("ident", [128, 128])
    bones_d = din("bones", [128, 128])
    out_d = nc.dram_tensor("outT", [D, SEQ], F32, kind="ExternalOutput").ap()

    with contextlib.ExitStack() as st:
        S = Sched(nc, st)

        uid = [0]

        def sb(stack, name, shape, dt):
            uid[0] += 1
            return stack.enter_context(nc.sbuf_tensor(f"{name}_{uid[0]}", list(shape), dt))

        chk_cnt = {}

        def chk(name):
            chk_cnt[name] = chk_cnt.get(name, 0) + 1
            if stop is None:
                return
            nm, _, n = stop.partition("#")
            if nm == name and chk_cnt[name] == int(n or 1):
                S.enabled = False

        def dump(name, ap, bufs):
            if name not in dumps or name in dbg_outs:
                return
            shp = list(ap.shape)
            dt_ = nc.dram_tensor("dbg_" + name, shp, ap.dtype, kind="ExternalOutput").ap()
            dbg_outs[name] = dt_
            S.dma("sp", ("dbg", name), [(dt_, ap)], reads=bufs)
            S.wait_all("sp", bufs)

        xT = sb(st, "xT_s", [128, 8, SEQ], F32)
        bx = [[S.newbuf() for _ in range(4)] for _ in range(8)]
        hT = sb(st, "hT_s", [128, 8, SEQ], BF16)
        bh = [[S.newbuf() for _ in range(4)] for _ in range(8)]
        pcols = sb(st, "pcols_s", [128, NL * NCOL], F32); b_pc = S.newbuf()
        brow = sb(st, "brow_s", [128, NBROW], F32); b_brow = S.newbuf()
        cfar = sb(st, "cfar_s", [128, 8], F32); b_cfar = S.newbuf()
        identf = sb(st, "identf", [128, 128], F32); b_idf = S.newbuf()
        bonesf = sb(st, "bonesf", [128, 128], F32); b_bof = S.newbuf()
        ident = sb(st, "ident_b", [128, 128], BF16); b_id = S.newbuf()
        bones = sb(st, "bones_b", [128, 128], BF16); b_bo = S.newbuf()
        ones = sb(st, "ones_b", [128, 128], BF16); b_ones = S.newbuf()
        epsc = sb(st, "epsc", [128, 1], F32); b_eps = S.newbuf()
        small = sb(st, "small", [128, 32], F32); b_small = S.newbuf()
        esink = sb(st, "esink", [128, 8], F32); b_esink = S.newbuf()
        G2 = sb(st, "G2", [128, 128], F32); b_G2 = S.newbuf()
        lamt = sb(st, "lamt", [128, 64], F32); b_lamt = S.newbuf()
        NWS = 4
        wslots = [sb(st, f"wslot{i}", [128, 2048], BF16) for i in range(NWS)]
        wbufs = [S.newbuf() for _ in range(NWS)]
        sqs = [sb(st, f"sq{i}", [128, 512], BF16) for i in range(2)]; b_sqs = [S.newbuf() for _ in range(2)]
        rts = [sb(st, f"rt{i}", [128, 512], F32) for i in range(2)]; b_rts = [S.newbuf() for _ in range(2)]

        pbank = [st.enter_context(nc.psum_tensor(f"pb{i}", [128, 512], F32)) for i in range(8)]
        bbank = [S.newbuf() for _ in range(8)]
        for b_ in bbank:
            b_.excl = True
        rot = {"ps": 0, "sq": 0, "rt": 0, "accA": 0}

        def ps_next():
            i = 3 + rot["ps"] % 5
            rot["ps"] += 1
            return pbank[i], bbank[i]

        def sq_next():
            i = rot["sq"] % 2
            rot["sq"] += 1
            return sqs[i], b_sqs[i]

        def rt_next():
            i = rot["rt"] % 2
            rot["rt"] += 1
            return rts[i], b_rts[i]

        def wview(slot, a, b):
            return slot[:, 0:a * b].rearrange("p (a b) -> p a b", a=a)

        def plan_layer(li):
            P = []

            def incols(c0, n):
                return w_in_d[li, :, c0:c0 + n].rearrange("(c p) n -> p c n", p=128)

            P.append((("qa", li, 0), lambda s: [(wview(s, 8, 256), incols(0, 256))]))
            P.append((("qa", li, 1), lambda s: [(wview(s, 8, 256), incols(256, 256))]))

            def kdup(s):
                v = s[:, 0:2048].rearrange("p (c k d e) -> p c k d e", c=8, k=2, d=2)
                return [(v[:, :, k, dd, :], incols(512 + k * 64, 64)) for k in range(2) for dd in range(2)]

            P.append((("ka", li), kdup))
            P.append((("va", li), lambda s: [(wview(s, 8, 128), incols(640, 128))]))
            for h in range(4):
                def qk(s, h=h):
                    v = wview(s, 8, 256)
                    return [(v[:, :, 0:128], incols(768 + h * 128, 128)),
                            (v[:, :, 128:256], incols(1280 + h * 128, 128))]
                P.append((("qkb", li, h), qk))
                P.append((("vb", li, h), lambda s, h=h: [(wview(s, 8, 128), incols(1792 + h * 128, 128))]))
            for ft in range(8):
                def gate(s, ft=ft):
                    v = wview(s, 8, 256)
                    g = lambda c0: w_gate_d[li, :, c0:c0 + 128].rearrange("(c p) n -> p c n", p=128)
                    return [(v[:, :, 0:128], g(ft * 128)), (v[:, :, 128:256], g(D + ft * 128))]
                P.append((("gate", li, ft), gate))

                def proj(s, ft=ft):
                    v = s[:, 0:1024].rearrange("p (m c n) -> p m c n", m=2, c=4)
                    a = w_a_d[li, :, ft * 128:(ft + 1) * 128].rearrange("(c p) n -> p c n", p=128)
                    b = w_b_d[li, :, ft * 128:(ft + 1) * 128].rearrange("(c p) n -> p c n", p=128)
                    return [(v[:, 0], a), (v[:, 1], b)]
                P.append((("proj", li, ft), proj))
            for f2 in range(4):
                P.append((("wo", li, f2), lambda s, f2=f2: [(wview(s, 8, 256), w_o_d[li, :, f2 * 256:(f2 + 1) * 256].rearrange("(c p) n -> p c n", p=128))]))
            for half in range(2):
                for i in range(11):
                    p = half * 11 + i

                    def up(s, p=p):
                        v = wview(s, 8, 256)
                        u = lambda c0: w_up_d[li, :, c0:c0 + 128].rearrange("(c p) n -> p c n", p=128)
                        return [(v[:, :, 0:128], u(p * 128)), (v[:, :, 128:256], u(D_FF + p * 128))]
                    P.append((("up", li, p), up))
                for fo in range(8):
                    def down(s, half=half, fo=fo):
                        v = wview(s, 11, 128)
                        src = w_dn_d[li, half * 1408:(half + 1) * 1408, fo * 128:(fo + 1) * 128].rearrange("(k p) n -> p k n", p=128)
                        return [(v, src)]
                    P.append((("down", li, half, fo), down))
            return P

        plan = []
        for li in range(NL):
            plan += plan_layer(li)
        wst = {"idx": 0, "issued": 0}
        AHEAD = NWS - 2

        def w_issue_upto(k):
            while wst["issued"] <= min(k, len(plan) - 1):
                t = wst["issued"]
                slot = t % NWS
                S.dma("pool", ("w", slot, plan[t][0][1]), plan[t][1](wslots[slot]), writes=[wbufs[slot]])
                wst["issued"] += 1

        def w_get(key):
            t = wst["idx"]
            assert plan[t][0] == key, (plan[t][0], key)
            w_issue_upto(t + AHEAD)
            wst["idx"] += 1
            return wslots[t % NWS], wbufs[t % NWS]

        for c in range(8):
            S.dma("sp", ("x", c), [(xT[:, c, :], xT_d[c * 128:(c + 1) * 128, :])], writes=bx[c])
        S.dma("sp", "c0", [(pcols[:], pcols_d)], writes=[b_pc])
        S.dma("sp", "c1", [(cfar[:], cfar_d.partition_broadcast(128))], writes=[b_cfar])
        S.dma("sp", "c2", [(identf[:], ident_d)], writes=[b_idf])
        S.dma("sp", "c3", [(bonesf[:], bones_d)], writes=[b_bof])
        S.op("dve", CP(ident[:], identf[:]), [b_idf], [b_id])
        S.op("dve", CP(bones[:], bonesf[:]), [b_bof], [b_bo])
        S.op("pool", MEMSET(ones[:], 1.0 / D), [], [b_ones])
        S.op("pool", MEMSET(epsc[:], EPS), [], [b_eps])
        w_issue_upto(AHEAD - 1)

        def pc(li, col, n=1):
            return pcols[:, li * NCOL + col: li * NCOL + col + n]

        def rstd_from_ms(pm, bpm, scale=1.0):
            rt, brt = rt_next()
            S.op("act", ACTF(rt[:], pm, AF.Ln, bias=epsc[:, 0:1], scale=scale), [bpm, b_eps], [brt])
            S.op("act", ACTF(rt[:], rt[:], AF.Exp, scale=-0.5), [brt], [brt])
            return rt, brt

        def rmsnorm_to_hT(li, gcol0):
            for G in range(4):
                ts = slice(G * 512, (G + 1) * 512)
                pm, bpm = ps_next()
                for c in range(8):
                    sq, bsq = sq_next()
                    S.op("act", ACTF(sq[:], xT[:, c, ts], AF.Square), [bx[c][G]], [bsq])
                    S.op("pe", MM(pm[:], ones[:], sq[:], start=(c == 0), stop=(c == 7)), [b_ones, bsq], [bpm])
                rt, brt = rstd_from_ms(pm[:], bpm)
                for c in range(8):
                    S.op("dve", STT(hT[:, c, ts], xT[:, c, ts], pc(li, gcol0 + c), rt[:], ALU.mult, ALU.mult),
                         [bx[c][G], b_pc, brt], [bh[c][G]])

        def proj_fm(wap_fn, G, kc, rhs_fn, rhs_bufs, wb):
            pz, bpz = ps_next()
            S.op("pe", [MM(pz[:], wap_fn(c), rhs_fn(c), start=(c == 0), stop=(c == kc - 1)) for c in range(kc)],
                 [wb] + rhs_bufs, [bpz])
            return pz, bpz

        def norm64_store(pz, bpz, gcol, dst, bdst):
            sq, bsq = sq_next()
            S.op("act", ACTF(sq[:], pz[:], AF.Square), [bpz], [bsq])
            pm, bpm = ps_next()
            S.op("pe", MM(pm[:], bones[:], sq[:]), [b_bo, bsq], [bpm])
            rt, brt = rstd_from_ms(pm[:], bpm)
            S.op("dve", STT(dst, pz[:], gcol, rt[:], ALU.mult, ALU.mult), [bpz, b_pc, brt], [bdst])

        def hT_rhs(G):
            return (lambda c: hT[:, c, G * 512:(G + 1) * 512]), [bh[c][G] for c in range(8)]

        for li, l in enumerate(layers):
            lam_init = 0.8 - 0.6 * math.exp(-0.3 * l)
            S.dma("sp", "brow", [(brow[:], brow_d[li].partition_broadcast(128))], writes=[b_brow])
            S.op("dve", TT(lamt[:], brow[:, 136:200], brow[:, 200:264], ALU.mult), [b_brow], [b_lamt])
            S.op("dve", lambda e: e.reduce_sum(out=small[:, 0:1], in_=lamt[:], axis=AX.X), [b_lamt], [b_small])
            S.op("dve", TT(lamt[:], brow[:, 264:328], brow[:, 328:392], ALU.mult), [b_brow, b_lamt], [b_lamt])
            S.op("dve", lambda e: e.reduce_sum(out=small[:, 1:2], in_=lamt[:], axis=AX.X), [b_lamt, b_small], [b_small])
            S.op("act", ACTF(small[:, 2:4], small[:, 0:2], AF.Exp), [b_small], [b_small])
            S.op("dve", TT(small[:, 4:5], small[:, 3:4], small[:, 2:3], ALU.subtract), [b_small], [b_small])
            S.op("dve", TS(small[:, 4:5], small[:, 4:5], -lam_init, ALU.add), [b_small], [b_small])
            S.op("act", ACTF(esink[:], brow[:, 128:136], AF.Exp), [b_brow], [b_esink])
            S.op("dve", TS(G2[:], brow[:, 0:128], 1.0 - lam_init, ALU.mult), [b_brow], [b_G2])

            rmsnorm_to_hT(li, C_LN1)
            dump("hT", hT[:], [b for r in bh for b in r])
            chk("N1")

            with contextlib.ExitStack() as st_att:
                S.mark_barrier()
                oT = sb(st_att, "oT", [128, 8, SEQ], BF16)
                boT = [[S.newbuf() for _ in range(4)] for _ in range(8)]
                with contextlib.ExitStack() as st_ab:
                    pTs = [sb(st_ab, f"pT{i}", [128, 512], BF16) for i in range(3)]; b_pTs = [S.newbuf() for _ in range(3)]
                    sbts = [sb(st_ab, f"sbt{i}", [128, 512], F32) for i in range(2)]; b_sbts = [S.newbuf() for _ in range(2)]
                    rot["pT"] = 0; rot["sbt"] = 0

                    def pT_next():
                        i = rot["pT"] % 3
                        rot["pT"] += 1
                        return pTs[i], b_pTs[i]

                    def sbt_next():
                        i = rot["sbt"] % 2
                        rot["sbt"] += 1
                        return sbts[i], b_sbts[i]

                    with contextlib.ExitStack() as st_a:
                        qaT = sb(st_a, "qaT", [128, 4, SEQ], BF16); b_qa = [[S.newbuf() for _ in range(4)] for _ in range(4)]
                        kaT = sb(st_a, "kaT", [128, 2, SEQ], BF16); b_ka = [[S.newbuf() for _ in range(4)] for _ in range(2)]
                        vaA = sb(st_a, "vaA", [128, 16, 2, 66], BF16); b_va = [S.newbuf() for _ in range(16)]
                        biasA = sb(st_a, "biasA_s", [128, 2, 3, 512], F32); b_biasA = S.newbuf()
                        ostA = [sb(st_a, f"ostA{i}", [128, 512], BF16) for i in range(2)]; b_ostA = [[S.newbuf() for _ in range(8)] for _ in range(2)]
                        r4s = sb(st_a, "r4s", [128, 2, 4], F32); b_r4 = [S.newbuf() for _ in range(2)]
                        S.dma("sp", "biasA", [(biasA[:].rearrange("p a b c -> p (a b c)"), biasA_d)], writes=[b_biasA])
                        S.op("pool", MEMSET(vaA[:, :, :, 64:66], 1.0), [], b_va)
                        for half2 in range(2):
                            ws, wb = w_get(("qa", li, half2))
                            wv = wview(ws, 8, 256)
                            for t2 in range(2):
                                ft = half2 * 2 + t2
                                for G in range(4):
                                    rf, rb = hT_rhs(G)
                                    pz, bpz = proj_fm(lambda c: wv[:, c, t2 * 128:(t2 + 1) * 128], G, 8, rf, rb, wb)
                                    norm64_store(pz, bpz, pc(li, C_GQA), qaT[:, ft, G * 512:(G + 1) * 512], b_qa[ft][G])
                        ws, wb = w_get(("ka", li))
                        wv = wview(ws, 8, 256)
                        for kap in range(2):
                            for G in range(4):
                                rf, rb = hT_rhs(G)
                                pz, bpz = proj_fm(lambda c: wv[:, c, kap * 128:(kap + 1) * 128], G, 8, rf, rb, wb)
                                norm64_store(pz, bpz, pc(li, C_GKA), kaT[:, kap, G * 512:(G + 1) * 512], b_ka[kap][G])
                        ws, wb = w_get(("va", li))
                        wv = wview(ws, 8, 128)
                        for t4 in range(4):
                            pv, bpv = ps_next()
                            for tq in range(4):
                                tt = t4 * 4 + tq
                                S.op("pe", [MM(pv[:, tq * 128:(tq + 1) * 128], hT[:, c, tt * 128:(tt + 1) * 128], wv[:, c, :],
                                               start=(c == 0), stop=(c == 7)) for c in range(8)],
                                     [wb] + [bh[c][t4] for c in range(8)], [bpv])
                            S.op("act", ACTF(vaA[:, t4 * 4:(t4 + 1) * 4, :, 0:64].rearrange("p t k e -> p (t k) e"),
                                             pv[:].rearrange("p (a e) -> p a e", e=64), AF.Copy),
                                 [bpv], b_va[t4 * 4:(t4 + 1) * 4])
                        dump("qaT", qaT[:], [b for r in b_qa for b in r])
                        dump("kaT", kaT[:], [b for r in b_ka for b in r])
                        dump("vaA", vaA[:], b_va)
                        chk("Ain")
                        for i in range(16):
                            ost, bost = ostA[i % 2], b_ostA[i % 2]
                            for kap in range(2):
                                js = [j for j in (i - 1, i, i + 1) if 0 <= j < 16]
                                acc, bacc = pbank[rot["accA"] % 3], bbank[rot["accA"] % 3]
                                rot["accA"] += 1
                                for jn, j in enumerate(js):
                                    sbt, bsbt = sbt_next()
                                    for hf in range(2):
                                        pS, bpS = ps_next()
                                        S.op("pe", MM(pS[:, 0:256],
                                                      kaT[hf * 64:(hf + 1) * 64, kap, j * 128:(j + 1) * 128],
                                                      qaT[hf * 64:(hf + 1) * 64, 2 * kap:2 * kap + 2, i * 128:(i + 1) * 128]),
                                             [b_ka[kap][j // 4], b_qa[2 * kap][i // 4], b_qa[2 * kap + 1][i // 4]], [bpS])
                                        S.op("dve", STT(sbt[:, hf * 256:(hf + 1) * 256], pS[:, 0:256], 0.125,
                                                        biasA[:, kap, j - i + 1, hf * 256:(hf + 1) * 256], ALU.mult, ALU.add),
                                             [bpS, b_biasA, bsbt], [bsbt])
                                    chk("A1")
                                    pT, bpT = pT_next()
                                    S.op("act", ACTF(pT[:], sbt[:], AF.Exp), [bsbt], [bpT])
                                    chk("A2")
                                    S.op("pe", [MM(acc[:, cb * 66:(cb + 1) * 66], pT[:, cb * 128:(cb + 1) * 128], vaA[:, j, kap, :],
                                                   start=(jn == 0 and cb == 0), stop=(jn == len(js) - 1))
                                                for cb in range(4)],
                                         [bpT, b_va[j]], [bacc])
                                    chk("A3")
                                accv = acc[:, 0:264].rearrange("p (a e) -> p a e", e=66)
                                r4 = r4s[:, kap, :]
                                S.op("dve", TT(r4, accv[:, :, 64], esink[:, kap * 4:(kap + 1) * 4], ALU.add), [bacc, b_esink], [b_r4[kap]])
                                S.op("dve", RECIP(r4, r4), [b_r4[kap]], [b_r4[kap]])
                                chk("A4")
                                for cb in range(4):
                                    h = 4 * kap + 2 * (cb % 2) + cb // 2
                                    if True:
                                        S.op("act", ACTF(ost[:, h * 64:(h + 1) * 64], accv[:, cb, 0:64], AF.Identity, scale=r4s[:, kap, cb:cb + 1]),
                                             [bacc, b_r4[kap]], [bost[h]])
                                    else:
                                        S.op("dve", TS(ost[:, h * 64:(h + 1) * 64], accv[:, cb, 0:64], r4s[:, kap, cb:cb + 1], ALU.mult),
                                             [bacc, b_r4[kap]], [bost[h]])
                                chk("A4b")
                            chk("A5")
                            ptr, bptr = ps_next()
                            S.op("pe", [MM(ptr[:, ft * 128:(ft + 1) * 128], ost[:, ft * 128:(ft + 1) * 128], ident[:]) for ft in range(4)],
                                 bost + [b_id], [bptr])
                            S.op("act", ACTF(oT[:, 0:4, i * 128:(i + 1) * 128], ptr[:].rearrange("p (a b) -> p a b", a=4), AF.Copy),
                                 [bptr], [boT[ft][i // 4] for ft in range(4)])
                    S.mark_barrier()
                    dump("oTa", oT[:, 0:4, :], [b for r in boT[0:4] for b in r])
                    chk("A")

                    with contextlib.ExitStack() as st_b:
                        qkT = [sb(st_b, f"qkT{i}", [128, 2, SEQ], BF16) for i in range(2)]
                        b_qk = [[[S.newbuf() for _ in range(4)] for _ in range(2)] for _ in range(2)]
                        vB = [sb(st_b, f"vB{i}", [128, 16, 130], BF16) for i in range(2)]
                        b_vB = [[S.newbuf() for _ in range(16)] for _ in range(2)]
                        strips = [sb(st_b, f"strip{i}", [128, 1152], F32) for i in range(2)]; b_strip = [S.newbuf() for _ in range(2)]
                        accS = sb(st_b, "accS", [128, 8, 130], F32); b_accS = S.newbuf()
                        o32 = sb(st_b, "o32", [128, 4, 128], F32); b_o32 = S.newbuf()
                        junk = sb(st_b, "junkB", [128, 128], F32); b_junk = S.newbuf()
                        ostB = [sb(st_b, f"ostB{i}", [128, 512], BF16) for i in range(2)]; b_ostB = [S.newbuf() for _ in range(2)]
                        sm = sb(st_b, "smB", [128, 32], F32); b_sm = S.newbuf()
                        for par in range(2):
                            S.op("pool", MEMSET(vB[par][:, :, 128:130], 1.0), [], b_vB[par])
                        rnd = 0
                        for h in range(4):
                            par = h % 2
                            qk, bqk, vb_, bvb = qkT[par], b_qk[par], vB[par], b_vB[par]
                            S.dma("sp", ("strip", par), [(strips[par][:], strip_d[h])], writes=[b_strip[par]])
                            ws, wb = w_get(("qkb", li, h))
                            wv = wview(ws, 8, 256)
                            for m in range(2):
                                for G in range(4):
                                    rf, rb = hT_rhs(G)
                                    pz, bpz = proj_fm(lambda c: wv[:, c, m * 128:(m + 1) * 128], G, 8, rf, rb, wb)
                                    norm64_store(pz, bpz, pc(li, C_GQB + m), qk[:, m, G * 512:(G + 1) * 512], bqk[m][G])
                            ws, wb = w_get(("vb", li, h))
                            wv = wview(ws, 8, 128)
                            for t4 in range(4):
                                pv, bpv = ps_next()
                                for tq in range(4):
                                    tt = t4 * 4 + tq
                                    S.op("pe", [MM(pv[:, tq * 128:(tq + 1) * 128], hT[:, c, tt * 128:(tt + 1) * 128], wv[:, c, :],
                                                   start=(c == 0), stop=(c == 7)) for c in range(8)],
                                         [wb] + [bh[c][t4] for c in range(8)], [bpv])
                                S.op("act", ACTF(vb_[:, t4 * 4:(t4 + 1) * 4, 0:128], pv[:].rearrange("p (a e) -> p a e", e=128), AF.Copy),
                                     [bpv], bvb[t4 * 4:(t4 + 1) * 4])
                            for G in range(4):
                                started = set()
                                for cm in range(2):
                                    for j in range(16):
                                        pS, bpS = ps_next()
                                        S.op("pe", MM(pS[:], qk[cm * 64:(cm + 1) * 64, 1, j * 128:(j + 1) * 128],
                                                      qk[cm * 64:(cm + 1) * 64, 0, G * 512:(G + 1) * 512]),
                                             [bqk[1][j // 4], bqk[0][G]], [bpS])
                                        dl = j - 4 * G
                                        pT, bpT = pT_next()
                                        if -1 <= dl <= 4:
                                            sbt, bsbt = sbt_next()
                                            S.op("dve", STT(sbt[:], pS[:], 0.125, strips[par][:, (4 - dl) * 128:(4 - dl) * 128 + 512], ALU.mult, ALU.add),
                                                 [bpS, b_strip[par]], [bsbt])
                                            S.op("act", ACTF(pT[:], sbt[:], AF.Exp), [bsbt], [bpT])
                                        else:
                                            col = h if dl < 0 else 4 + h
                                            S.op("act", ACTF(pT[:], pS[:], AF.Exp, scale=0.125, bias=cfar[:, col:col + 1]), [bpS, b_cfar], [bpT])
                                        fns = []
                                        touched = []
                                        for b in range(4):
                                            a = cm * 4 + b
                                            bank, off = a // 3, (a % 3) * 130
                                            fns.append(MM(pbank[bank][:, off:off + 130], pT[:, b * 128:(b + 1) * 128], vb_[:, j, :],
                                                          start=(bank not in started), stop=(j == 15)))
                                            started.add(bank)
                                            if bbank[bank] not in touched:
                                                touched.append(bbank[bank])
                                        S.op("pe", fns, [bpT, bvb[j]], touched)
                                S.op("act", ACTF(accS[:, 0:3, :], pbank[0][:, 0:390].rearrange("p (a e) -> p a e", e=130), AF.Copy), [bbank[0]], [b_accS])
                                S.op("dve", CP(accS[:, 3:6, :], pbank[1][:, 0:390].rearrange("p (a e) -> p a e", e=130)), [bbank[1], b_accS], [b_accS])
                                S.op("act", ACTF(accS[:, 6:8, :], pbank[2][:, 0:260].rearrange("p (a e) -> p a e", e=130), AF.Copy), [bbank[2], b_accS], [b_accS])
                                S.op("dve", RECIP(sm[:, 0:8], accS[:, :, 128]), [b_accS, b_sm], [b_sm])
                                S.op("dve", TS(sm[:, 8:12], sm[:, 4:8], small[:, 4:5], ALU.mult), [b_sm, b_small], [b_sm])
                                for b in range(4):
                                    S.op("act", ACTF(o32[:, b, :], accS[:, b, 0:128], AF.Identity, scale=sm[:, b:b + 1]), [b_accS, b_sm, b_o32], [b_o32])
                                for b in range(4):
                                    S.op("dve", STT(o32[:, b, :], accS[:, 4 + b, 0:128], sm[:, 8 + b:9 + b], o32[:, b, :], ALU.mult, ALU.add),
                                         [b_accS, b_sm, b_o32], [b_o32])
                                for b in range(4):
                                    S.op("act", ACTF(junk[:], o32[:, b, :], AF.Square, accum_out=sm[:, 12 + b:13 + b]), [b_o32, b_junk, b_sm], [b_junk, b_sm])
                                S.op("act", ACTF(sm[:, 16:20], sm[:, 12:16], AF.Ln, bias=epsc[:, 0:1], scale=1.0 / 128), [b_sm, b_eps], [b_sm])
                                S.op("act", ACTF(sm[:, 16:20], sm[:, 16:20], AF.Exp, scale=-0.5), [b_sm], [b_sm])
                                ost, bost = ostB[rnd % 2], b_ostB[rnd % 2]
                                rnd += 1
                                for b in range(4):
                                    S.op("dve", STT(ost[:, b * 128:(b + 1) * 128], o32[:, b, :], sm[:, 16 + b:17 + b], G2[:], ALU.mult, ALU.mult),
                                         [b_o32, b_sm, b_G2], [bost])
                                ptr, bptr = ps_next()
                                S.op("pe", [MM(ptr[:, b * 128:(b + 1) * 128], ost[:, b * 128:(b + 1) * 128], ident[:]) for b in range(4)],
                                     [bost, b_id], [bptr])
                                S.op("act", ACTF(oT[:, 4 + h, G * 512:(G + 1) * 512], ptr[:], AF.Copy), [bptr], [boT[4 + h][G]])
                    S.mark_barrier()
                dump("oT", oT[:], [b for r in boT for b in r])
                chk("B")
                with contextlib.ExitStack() as st_m:
                    S.mark_barrier()
                    mixT = sb(st_m, "mixT", [128, 8, SEQ], BF16); b_mix = [[S.newbuf() for _ in range(4)] for _ in range(8)]
                    gts = [sb(st_m, f"gt{i}", [128, 512], F32) for i in range(4)]; b_gts = [S.newbuf() for _ in range(4)]
                    m1s = [sb(st_m, f"m1{i}", [128, 512], F32) for i in range(4)]; b_m1s = [S.newbuf() for _ in range(4)]
                    rr = 0
                    for ft in range(8):
                        wsg, wbg = w_get(("gate", li, ft))
                        wvg = wview(wsg, 8, 256)
                        wsp, wbp = w_get(("proj", li, ft))
                        wvp = wsp[:, 0:1024].rearrange("p (m c n) -> p m c n", m=2, c=4)
                        for G in range(4):
                            ts = slice(G * 512, (G + 1) * 512)
                            rf, rb = hT_rhs(G)
                            res = []
                            for m in range(2):
                                pg, bpg = proj_fm(lambda c: wvg[:, c, m * 128:(m + 1) * 128], G, 8, rf, rb, wbg)
                                gt, bgt = gts[rr % 4], b_gts[rr % 4]
                                S.op("act", ACTF(gt[:], pg[:], AF.Sigmoid, bias=pc(li, C_BG + m * 8 + ft)), [bpg, b_pc], [bgt])
                                dump("gt0", gt[:], [bgt])
                                pp, bpp = proj_fm(lambda c: wvp[:, m, c, :], G, 4, lambda c: oT[:, m * 4 + c, ts],
                                                  [boT[m * 4 + c][G] for c in range(4)], wbp)
                                m1, bm1 = m1s[rr % 4], b_m1s[rr % 4]
                                rr += 1
                                S.op("dve", TT(m1[:], pp[:], gt[:], ALU.mult), [bpp, bgt], [bm1])
                                res.append((m1, bm1))
                            S.op("dve", TT(mixT[:, ft, ts], res[0][0][:], res[1][0][:], ALU.add), [res[0][1], res[1][1]], [b_mix[ft][G]])
                    dump("mixT", mixT[:], [b for r in b_mix for b in r])
                    for f2 in range(4):
                        ws, wb = w_get(("wo", li, f2))
                        wv = wview(ws, 8, 256)
                        for t2 in range(2):
                            fo = f2 * 2 + t2
                            for G in range(4):
                                ts = slice(G * 512, (G + 1) * 512)
                                po, bpo = proj_fm(lambda c: wv[:, c, t2 * 128:(t2 + 1) * 128], G, 8, lambda c: mixT[:, c, ts],
                                                  [b_mix[c][G] for c in range(8)], wb)
                                S.op("dve", TT(xT[:, fo, ts], po[:], xT[:, fo, ts], ALU.add), [bpo, bx[fo][G]], [bx[fo][G]])
                S.mark_barrier()
            S.mark_barrier()

            dump("xmid", xT[:], [b for r in bx for b in r])
            chk("M")
            rmsnorm_to_hT(li, C_LN2)
            with contextlib.ExitStack() as st_f:
                S.mark_barrier()
                actT = sb(st_f, "actT", [128, 11, SEQ], BF16); b_act = [[S.newbuf() for _ in range(4)] for _ in range(11)]
                raws = [sb(st_f, f"raw{i}", [128, SEQ + 2], F32) for i in range(2)]; b_raws = [S.newbuf() for _ in range(2)]
                us = [sb(st_f, f"u{i}", [128, SEQ], F32) for i in range(2)]; b_us = [S.newbuf() for _ in range(2)]
                sg = sb(st_f, "sg", [128, SEQ], BF16); b_sg = S.newbuf()
                for i2 in range(2):
                    S.op("pool", MEMSET(raws[i2][:, 0:1], 0.0), [], [b_raws[i2]])
                    S.op("pool", MEMSET(raws[i2][:, SEQ + 1:SEQ + 2], 0.0), [b_raws[i2]], [b_raws[i2]])
                tcount = 0
                for half in range(2):
                    for i in range(11):
                        p = half * 11 + i
                        ws, wb = w_get(("up", li, p))
                        wv = wview(ws, 8, 256)
                        for which in (1, 0):
                            ct = p + 22 * which
                            raw, braw = raws[tcount % 2], b_raws[tcount % 2]
                            u, bu = us[tcount % 2], b_us[tcount % 2]
                            tcount += 1
                            for G in range(4):
                                rf, rb = hT_rhs(G)
                                pu, bpu = proj_fm(lambda c: wv[:, c, which * 128:(which + 1) * 128], G, 8, rf, rb, wb)
                                S.op("act", ACTF(raw[:, 1 + G * 512:1 + (G + 1) * 512], pu[:], AF.Copy), [bpu, braw], [braw])
                                S.op("act", ACTF(u[:, G * 512:(G + 1) * 512], pu[:], AF.Identity, scale=pc(li, C_CW + 44 + ct), bias=pc(li, C_CB + ct)),
                                     [bpu, b_pc, bu], [bu])
                            S.op("dve", STT(u[:], raw[:, 0:SEQ], pc(li, C_CW + ct), u[:], ALU.mult, ALU.add), [braw, bu, b_pc], [bu])
                            S.op("dve", STT(u[:], raw[:, 2:SEQ + 2], pc(li, C_CW + 88 + ct), u[:], ALU.mult, ALU.add), [braw, bu, b_pc], [bu])
                            dump("u_g0", u[:], [bu])
                            if which == 1:
                                S.op("act", ACTF(sg[:], u[:], AF.Silu), [bu, b_sg], [b_sg])
                                dump("sg0", sg[:], [b_sg])
                            else:
                                S.op("pool", TT(actT[:, i, :], sg[:], u[:], ALU.mult), [b_sg, bu], b_act[i])
                    for fo in range(8):
                        ws, wb = w_get(("down", li, half, fo))
                        wv = wview(ws, 11, 128)
                        for G in range(4):
                            ts = slice(G * 512, (G + 1) * 512)
                            po, bpo = proj_fm(lambda k: wv[:, k, :], G, 11, lambda k: actT[:, k, ts], [b_act[k][G] for k in range(11)], wb)
                            S.op("dve", TT(xT[:, fo, ts], po[:], xT[:, fo, ts], ALU.add), [bpo, bx[fo][G]], [bx[fo][G]])
            S.mark_barrier()

        S.enabled = True
        allx = [b for row in bx for b in row]
        for c in range(8):
            S.dma("sp", ("out", c), [(out_d[c * 128:(c + 1) * 128, :], xT[:, c, :])], reads=bx[c])
        S.wait_all("sp", allx)
        assert stop is not None or wst["idx"] == len(plan)
        S.emit()
    return nc


def _t5_bucket_np(rel):
    rel = np.asarray(rel, np.int64)
    half, max_exact = 16, 8
    ret = np.where(rel > 0, half, 0)
    n = np.abs(rel)
    nf = np.maximum(n, 1).astype(np.float32)
    large = max_exact + (np.log(nf / np.float32(max_exact)) / np.float32(math.log(128 / max_exact))
                         * np.float32(half - max_exact)).astype(np.int32)
    large = np.minimum(large, half - 1)
    return ret + np.where(n < max_exact, n, large)


def _host_layout(inp, layers):
    f32 = np.float32
    L = list(layers)
    g = lambda k: np.asarray(inp[k], f32)
    pcols = np.zeros((128, len(L) * NCOL), f32)
    brow = np.zeros((len(L), NBROW), f32)
    for li, l in enumerate(L):
        pcl = np.zeros((128, NCOL), f32)
        pcl[:, C_LN1:C_LN1 + 8] = g("ln1_g")[l].reshape(8, 128).T
        pcl[:, C_LN2:C_LN2 + 8] = g("ln2_g")[l].reshape(8, 128).T
        pcl[:, C_GQA] = np.tile(g("qn_a")[l], 2)
        pcl[:, C_GKA] = np.tile(g("kn_a")[l], 2)
        pcl[:, C_GQB] = np.tile(g("qn_b")[l], 2)
        pcl[:, C_GKB] = np.tile(g("kn_b")[l], 2)
        pcl[:, C_BG:C_BG + 16] = g("b_gate")[l].reshape(16, 128).T
        pcl[:, C_CW:C_CW + 132] = g("conv_w")[l].reshape(3, 44, 128).transpose(2, 0, 1).reshape(128, 132)
        pcl[:, C_CB:C_CB + 44] = g("conv_b")[l].reshape(44, 128).T
        pcols[:, li * NCOL:(li + 1) * NCOL] = pcl
        brow[li] = np.concatenate([g("subln_g")[l], g("sink")[l][SINK_PERM], g("lam_q1")[l], g("lam_k1")[l],
                                   g("lam_q2")[l], g("lam_k2")[l]])
    tab = g("rel_bias")
    cfar = np.concatenate([tab[15, 8:12], tab[31, 8:12]]).astype(f32)
    k = np.arange(128)[:, None]
    c = np.arange(1152)[None, :]
    bk = _t5_bucket_np(k - c + 512)
    stripB = np.stack([tab[bk, 8 + h] for h in range(4)]).astype(f32)
    q = np.arange(128)[None, :]
    biasA = np.zeros((128, 2, 3, 4, 128), f32)
    for di in range(3):
        rel = k + 128 * (di - 1) - q
        bkt = _t5_bucket_np(rel)
        ok = np.abs(rel) <= 128
        for kap in range(2):
            for cb in range(4):
                h = 4 * kap + 2 * (cb % 2) + cb // 2
                biasA[:, kap, di, cb, :] = np.where(ok, tab[bkt, h], f32(MASK_NEG))
    bones = np.zeros((128, 128), f32)
    bones[:64, :64] = 1.0 / 64
    bones[64:, 64:] = 1.0 / 64
    common = {
        "w_in": np.ascontiguousarray(g("w_in")[L]), "w_gate": np.ascontiguousarray(g("w_gate")[L]),
        "w_a_proj": np.ascontiguousarray(g("w_a_proj")[L]), "w_b_proj": np.ascontiguousarray(g("w_b_proj")[L]),
        "w_o": np.ascontiguousarray(g("w_o")[L]), "w_up": np.ascontiguousarray(g("w_up")[L]),
        "w_down": np.ascontiguousarray(g("w_down")[L]),
        "pcols": pcols, "brow": brow, "cfar": cfar, "stripB": stripB,
        "biasA": np.ascontiguousarray(biasA.reshape(128, 2 * 3 * 512)),
        "ident": np.eye(128, dtype=f32), "bones": bones,
    }
    return common


_PROGS = {}


def _run(layers, xT_list, inp):
    key = tuple(layers)
    if key not in _PROGS:
        _PROGS[key] = build_program(list(layers))
    nc = _PROGS[key]
    common = _host_layout(inp, layers)
    in_maps = [dict(common, xT=xT_list[b]) for b in range(NCORES)]
    res = run_bass_kernel_spmd(nc, in_maps, core_ids=list(range(NCORES)))
    return [np.asarray(r["outT"]) for r in res.results]


FUSED = True


def kernel(**inputs):
    x = np.asarray(inputs["x"], np.float32)
    xT = [np.ascontiguousarray(x[b].T) for b in range(NCORES)]
    if FUSED:
        xT = _run(range(DEPTH), xT, inputs)
    else:
        for l in range(DEPTH):
            xT = _run([l], xT, inputs)
    return np.stack([t.T for t in xT]).astype(np.float32)
```

```python
import contextlib
import math
import numpy as np
import concourse.bass as bass
import concourse.mybir as mybir
from concourse.bass_utils import run_bass_kernel_spmd

F32 = mybir.dt.float32
BF16 = mybir.dt.bfloat16
AF = mybir.ActivationFunctionType
ALU = mybir.AluOpType
AX = mybir.AxisListType

D = 1024
SEQ = 2048
DEPTH = 4
NCORES = 8
D_FF = 2816
IN_W = 2304
EPS = 1e-6
NCOL = 212
C_LN1, C_LN2, C_GQA, C_GKA, C_GQB, C_GKB, C_BG, C_CW, C_CB = 0, 8, 16, 17, 18, 19, 20, 36, 168
NBROW = 392
SINK_PERM = [0, 2, 1, 3, 4, 6, 5, 7]
MASK_NEG = -30000.0

EPOCH = 2000
ENGS = ("pe", "act", "dve", "pool", "sp")


class Buf:
    __slots__ = ("w", "r", "excl")

    def __init__(self, r=None):
        self.excl = False
        self.w = None
        self.r = dict(r) if r else {}


class Sched:
    def __init__(self, nc, stack):
        self.nc = nc
        self.stack = stack
        self.ops = {e: [] for e in ENGS}
        self.seq = {e: 0 for e in ENGS}
        self.sems = {}
        self.waited = {e: {} for e in ENGS}
        self.dma_cnt = {}
        self.barrier = {}
        self.enabled = True

    def sem(self, key):
        s = self.sems.get(key)
        if s is None:
            s = self.stack.enter_context(self.nc.semaphore("s_" + "_".join(str(k) for k in key)))
            self.sems[key] = s
        return s

    def newbuf(self):
        return Buf(self.barrier)

    def mark_barrier(self):
        b = {}
        for e in ENGS:
            n = self.seq[e]
            if n > 0:
                b[(e, (n - 1) // EPOCH)] = (n - 1) % EPOCH + 1
        for k, c in self.dma_cnt.items():
            b[k] = 16 * c
        self.barrier = b

    def _deps(self, eng, reads, writes):
        deps = {}
        for b in reads:
            if b.w is not None:
                k, v = b.w
                if deps.get(k, 0) < v:
                    deps[k] = v
        for b in writes:
            if b.w is not None:
                k, v = b.w
                if deps.get(k, 0) < v:
                    deps[k] = v
            for k, v in b.r.items():
                if deps.get(k, 0) < v:
                    deps[k] = v
        out = []
        wd = self.waited[eng]
        for k, v in deps.items():
            if eng == "pe" and k[0] == "pe":
                continue
            if wd.get(k, 0) >= v:
                continue
            wd[k] = v
            out.append((self.sem(k), v))
        return out

    def _commit(self, tok, reads, writes):
        k, v = tok
        for b in writes:
            b.w = tok
            b.r = {}
        for b in reads:
            if b.r.get(k, 0) < v:
                b.r[k] = v

    def op(self, eng, fns, reads=(), writes=()):
        if not self.enabled:
            return
        if not isinstance(fns, (list, tuple)):
            fns = [fns]
        if any(b.excl for b in reads):
            writes = list(writes) + [b for b in reads if b.excl]
            reads = [b for b in reads if not b.excl]
        waits = self._deps(eng, reads, writes)
        n = self.seq[eng]
        self.seq[eng] = n + 1
        key = (eng, n // EPOCH)
        tok = (key, n % EPOCH + 1)
        self.ops[eng].append((waits, fns, self.sem(key), 1))
        self._commit(tok, reads, writes)

    def dma(self, eng, slot, pairs, reads=(), writes=(), **kw):
        if not self.enabled:
            return
        waits = self._deps(eng, reads, writes)
        key = ("dma", slot)
        sem = self.sem(key)
        c = self.dma_cnt.get(key, 0)
        first = True
        for (o, i) in pairs:
            c += 1
            fn = (lambda e, o=o, i=i: e.dma_start(out=o, in_=i, **kw))
            self.ops[eng].append((waits if first else [], [fn], sem, 16))
            first = False
        self.dma_cnt[key] = c
        self._commit((key, 16 * c), reads, writes)

    def wait_all(self, eng, bufs):
        waits = self._deps(eng, bufs, bufs)
        self.ops[eng].append((waits, [], None, 0))

    def emit(self):
        objs = {"pe": "tensor", "act": "scalar", "dve": "vector", "pool": "gpsimd", "sp": "sync"}
        with self.nc.Block() as block:
            for e in ENGS:
                lst = self.ops[e]
                if not lst:
                    continue

                def body(engobj, lst=lst):
                    for waits, fns, sem, inc in lst:
                        for s, v in waits:
                            engobj.wait_ge(s, v)
                        for fn in fns[:-1]:
                            fn(engobj)
                        if fns:
                            fns[-1](engobj).then_inc(sem, inc)

                getattr(block, objs[e])(body)


def MM(out, lhsT, rhs, start=True, stop=True):
    return lambda e: e.matmul(out, lhsT=lhsT, rhs=rhs, start=start, stop=stop, skip_group_check=True)


def ACTF(out, in_, func, **kw):
    return lambda e: e.activation(out=out, in_=in_, func=func, **kw)


def STT(out, in0, scalar, in1, op0, op1):
    return lambda e: e.scalar_tensor_tensor(out=out, in0=in0, scalar=scalar, in1=in1, op0=op0, op1=op1)


def TT(out, in0, in1, op):
    return lambda e: e.tensor_tensor(out=out, in0=in0, in1=in1, op=op)


def TS(out, in0, s1, op0, s2=None, op1=None):
    if op1 is None:
        return lambda e: e.tensor_scalar(out=out, in0=in0, scalar1=s1, scalar2=None, op0=op0)
    return lambda e: e.tensor_scalar(out=out, in0=in0, scalar1=s1, scalar2=s2, op0=op0, op1=op1)


def CP(out, in_):
    return lambda e: e.tensor_copy(out=out, in_=in_)


def RECIP(out, in_):
    return lambda e: e.reciprocal(out=out, in_=in_)


def MEMSET(ap, v):
    return lambda e: e.memset(ap, v)


class _Stop(Exception):
    pass


def build_program(layers, stop=None, dumps=()):
    NL = len(layers)
    dbg_outs = {}
    nc = bass.Bass("TRN2", target_bir_lowering=False)

    def din(name, shape):
        return nc.dram_tensor(name, list(shape), F32, kind="ExternalInput").ap()

    xT_d = din("xT", [D, SEQ])
    w_in_d = din("w_in", [NL, D, IN_W])
    w_gate_d = din("w_gate", [NL, D, 2 * D])
    w_a_d = din("w_a_proj", [NL, 512, D])
    w_b_d = din("w_b_proj", [NL, 512, D])
    w_o_d = din("w_o", [NL, D, D])
    w_up_d = din("w_up", [NL, D, 2 * D_FF])
    w_dn_d = din("w_down", [NL, D_FF, D])
    pcols_d = din("pcols", [128, NL * NCOL])
    brow_d = din("brow", [NL, NBROW])
    cfar_d = din("cfar", [8])
    strip_d = din("stripB", [4, 128, 1152])
    biasA_d = din("biasA", [128, 2 * 3 * 512])
    ident_d = din("ident", [128, 128])
    bones_d = din("bones", [128, 128])
    out_d = nc.dram_tensor("outT", [D, SEQ], F32, kind="ExternalOutput").ap()

    with contextlib.ExitStack() as st:
        S = Sched(nc, st)

        uid = [0]

        def sb(stack, name, shape, dt):
            uid[0] += 1
            return stack.enter_context(nc.sbuf_tensor(f"{name}_{uid[0]}", list(shape), dt))

        chk_cnt = {}

        def chk(name):
            chk_cnt[name] = chk_cnt.get(name, 0) + 1
            if stop is None:
                return
            nm, _, n = stop.partition("#")
            if nm == name and chk_cnt[name] == int(n or 1):
                S.enabled = False

        def dump(name, ap, bufs):
            if name not in dumps or name in dbg_outs:
                return
            shp = list(ap.shape)
            dt_ = nc.dram_tensor("dbg_" + name, shp, ap.dtype, kind="ExternalOutput").ap()
            dbg_outs[name] = dt_
            S.dma("sp", ("dbg", name), [(dt_, ap)], reads=bufs)
            S.wait_all("sp", bufs)

        xT = sb(st, "xT_s", [128, 8, SEQ], F32)
        bx = [[S.newbuf() for _ in range(4)] for _ in range(8)]
        hT = sb(st, "hT_s", [128, 8, SEQ], BF16)
        bh = [[S.newbuf() for _ in range(4)] for _ in range(8)]
        pcols = sb(st, "pcols_s", [128, NL * NCOL], F32); b_pc = S.newbuf()
        brow = sb(st, "brow_s", [128, NBROW], F32); b_brow = S.newbuf()
        cfar = sb(st, "cfar_s", [128, 8], F32); b_cfar = S.newbuf()
        identf = sb(st, "identf", [128, 128], F32); b_idf = S.newbuf()
        bonesf = sb(st, "bonesf", [128, 128], F32); b_bof = S.newbuf()
        ident = sb(st, "ident_b", [128, 128], BF16); b_id = S.newbuf()
        bones = sb(st, "bones_b", [128, 128], BF16); b_bo = S.newbuf()
        ones = sb(st, "ones_b", [128, 128], BF16); b_ones = S.newbuf()
        epsc = sb(st, "epsc", [128, 1], F32); b_eps = S.newbuf()
        small = sb(st, "small", [128, 32], F32); b_small = S.newbuf()
        esink = sb(st, "esink", [128, 8], F32); b_esink = S.newbuf()
        G2 = sb(st, "G2", [128, 128], F32); b_G2 = S.newbuf()
        lamt = sb(st, "lamt", [128, 64], F32); b_lamt = S.newbuf()
        NWS = 4
        wslots = [sb(st, f"wslot{i}", [128, 2048], BF16) for i in range(NWS)]
        wbufs = [S.newbuf() for _ in range(NWS)]
        sqs = [sb(st, f"sq{i}", [128, 512], BF16) for i in range(2)]; b_sqs = [S.newbuf() for _ in range(2)]
        rts = [sb(st, f"rt{i}", [128, 512], F32) for i in range(2)]; b_rts = [S.newbuf() for _ in range(2)]

        pbank = [st.enter_context(nc.psum_tensor(f"pb{i}", [128, 512], F32)) for i in range(8)]
        bbank = [S.newbuf() for _ in range(8)]
        for b_ in bbank:
            b_.excl = True
        rot = {"ps": 0, "sq": 0, "rt": 0, "accA": 0, "rndB": 0}

        def ps_next():
            i = 3 + rot["ps"] % 5
            rot["ps"] += 1
            return pbank[i], bbank[i]

        def sq_next():
            i = rot["sq"] % 2
            rot["sq"] += 1
            return sqs[i], b_sqs[i]

        def rt_next():
            i = rot["rt"] % 2
            rot["rt"] += 1
            return rts[i], b_rts[i]

        def wview(slot, a, b):
            return slot[:, 0:a * b].rearrange("p (a b) -> p a b", a=a)

        def plan_layer(li):
            P = []

            def incols(c0, n):
                return w_in_d[li, :, c0:c0 + n].rearrange("(c p) n -> p c n", p=128)

            P.append((("qa", li, 0), lambda s: [(wview(s, 8, 256), incols(0, 256))]))
            P.append((("qa", li, 1), lambda s: [(wview(s, 8, 256), incols(256, 256))]))

            def kdup(s):
                v = s[:, 0:2048].rearrange("p (c k d e) -> p c k d e", c=8, k=2, d=2)
                return [(v[:, :, k, dd, :], incols(512 + k * 64, 64)) for k in range(2) for dd in range(2)]

            P.append((("ka", li), kdup))
            P.append((("va", li), lambda s: [(wview(s, 8, 128), incols(640, 128))]))
            for h in range(4):
                def qk(s, h=h):
                    v = wview(s, 8, 256)
                    return [(v[:, :, 0:128], incols(768 + h * 128, 128)),
                            (v[:, :, 128:256], incols(1280 + h * 128, 128))]
                P.append((("qkb", li, h), qk))
                P.append((("vb", li, h), lambda s, h=h: [(wview(s, 8, 128), incols(1792 + h * 128, 128))]))
            for ft in range(8):
                def gate(s, ft=ft):
                    v = wview(s, 8, 256)
                    g = lambda c0: w_gate_d[li, :, c0:c0 + 128].rearrange("(c p) n -> p c n", p=128)
                    return [(v[:, :, 0:128], g(ft * 128)), (v[:, :, 128:256], g(D + ft * 128))]
                P.append((("gate", li, ft), gate))

                def proj(s, ft=ft):
                    v = s[:, 0:1024].rearrange("p (m c n) -> p m c n", m=2, c=4)
                    a = w_a_d[li, :, ft * 128:(ft + 1) * 128].rearrange("(c p) n -> p c n", p=128)
                    b = w_b_d[li, :, ft * 128:(ft + 1) * 128].rearrange("(c p) n -> p c n", p=128)
                    return [(v[:, 0], a), (v[:, 1], b)]
                P.append((("proj", li, ft), proj))
            for f2 in range(4):
                P.append((("wo", li, f2), lambda s, f2=f2: [(wview(s, 8, 256), w_o_d[li, :, f2 * 256:(f2 + 1) * 256].rearrange("(c p) n -> p c n", p=128))]))
            for half in range(2):
                for i in range(11):
                    p = half * 11 + i

                    def up(s, p=p):
                        v = wview(s, 8, 256)
                        u = lambda c0: w_up_d[li, :, c0:c0 + 128].rearrange("(c p) n -> p c n", p=128)
                        return [(v[:, :, 0:128], u(p * 128)), (v[:, :, 128:256], u(D_FF + p * 128))]
                    P.append((("up", li, p), up))
                for fo in range(8):
                    def down(s, half=half, fo=fo):
                        v = wview(s, 11, 128)
                        src = w_dn_d[li, half * 1408:(half + 1) * 1408, fo * 128:(fo + 1) * 128].rearrange("(k p) n -> p k n", p=128)
                        return [(v, src)]
                    P.append((("down", li, half, fo), down))
            return P

        plan = []
        for li in range(NL):
            plan += plan_layer(li)
        wst = {"idx": 0, "issued": 0}
        AHEAD = NWS - 2

        def w_issue_upto(k):
            while wst["issued"] <= min(k, len(plan) - 1):
                t = wst["issued"]
                slot = t % NWS
                S.dma("pool", ("w", slot, plan[t][0][1]), plan[t][1](wslots[slot]), writes=[wbufs[slot]])
                wst["issued"] += 1

        def w_get(key):
            t = wst["idx"]
            assert plan[t][0] == key, (plan[t][0], key)
            w_issue_upto(t + AHEAD)
            wst["idx"] += 1
            return wslots[t % NWS], wbufs[t % NWS]

        for c in range(8):
            S.dma("sp", ("x", c), [(xT[:, c, :], xT_d[c * 128:(c + 1) * 128, :])], writes=bx[c])
        S.dma("sp", "c0", [(pcols[:], pcols_d)], writes=[b_pc])
        S.dma("sp", "c1", [(cfar[:], cfar_d.partition_broadcast(128))], writes=[b_cfar])
        S.dma("sp", "c2", [(identf[:], ident_d)], writes=[b_idf])
        S.dma("sp", "c3", [(bonesf[:], bones_d)], writes=[b_bof])
        S.op("dve", CP(ident[:], identf[:]), [b_idf], [b_id])
        S.op("dve", CP(bones[:], bonesf[:]), [b_bof], [b_bo])
        S.op("pool", MEMSET(ones[:], 1.0 / D), [], [b_ones])
        S.op("pool", MEMSET(epsc[:], EPS), [], [b_eps])
        w_issue_upto(AHEAD - 1)

        def pc(li, col, n=1):
            return pcols[:, li * NCOL + col: li * NCOL + col + n]

        def rstd_from_ms(pm, bpm, scale=1.0):
            rt, brt = rt_next()
            S.op("act", ACTF(rt[:], pm, AF.Ln, bias=epsc[:, 0:1], scale=scale), [bpm, b_eps], [brt])
            S.op("act", ACTF(rt[:], rt[:], AF.Exp, scale=-0.5), [brt], [brt])
            return rt, brt

        def rmsnorm_to_hT(li, gcol0):
            for G in range(4):
                ts = slice(G * 512, (G + 1) * 512)
                pm, bpm = ps_next()
                for c in range(8):
                    sq, bsq = sq_next()
                    S.op("act", ACTF(sq[:], xT[:, c, ts], AF.Square), [bx[c][G]], [bsq])
                    S.op("pe", MM(pm[:], ones[:], sq[:], start=(c == 0), stop=(c == 7)), [b_ones, bsq], [bpm])
                rt, brt = rstd_from_ms(pm[:], bpm)
                for c in range(8):
                    S.op("dve", STT(hT[:, c, ts], xT[:, c, ts], pc(li, gcol0 + c), rt[:], ALU.mult, ALU.mult),
                         [bx[c][G], b_pc, brt], [bh[c][G]])

        def proj_fm(wap_fn, G, kc, rhs_fn, rhs_bufs, wb):
            pz, bpz = ps_next()
            S.op("pe", [MM(pz[:], wap_fn(c), rhs_fn(c), start=(c == 0), stop=(c == kc - 1)) for c in range(kc)],
                 [wb] + rhs_bufs, [bpz])
            return pz, bpz

        def norm64_store(pz, bpz, gcol, dst, bdst):
            sq, bsq = sq_next()
            S.op("act", ACTF(sq[:], pz[:], AF.Square), [bpz], [bsq])
            pm, bpm = ps_next()
            S.op("pe", MM(pm[:], bones[:], sq[:]), [b_bo, bsq], [bpm])
            rt, brt = rstd_from_ms(pm[:], bpm)
            S.op("dve", STT(dst, pz[:], gcol, rt[:], ALU.mult, ALU.mult), [bpz, b_pc, brt], [bdst])

        def hT_rhs(G):
            return (lambda c: hT[:, c, G * 512:(G + 1) * 512]), [bh[c][G] for c in range(8)]

        for li, l in enumerate(layers):
            lam_init = 0.8 - 0.6 * math.exp(-0.3 * l)
            S.dma("sp", "brow", [(brow[:], brow_d[li].partition_broadcast(128))], writes=[b_brow])
            S.op("dve", TT(lamt[:], brow[:, 136:200], brow[:, 200:264], ALU.mult), [b_brow], [b_lamt])
            S.op("dve", lambda e: e.reduce_sum(out=small[:, 0:1], in_=lamt[:], axis=AX.X), [b_lamt], [b_small])
            S.op("dve", TT(lamt[:], brow[:, 264:328], brow[:, 328:392], ALU.mult), [b_brow, b_lamt], [b_lamt])
            S.op("dve", lambda e: e.reduce_sum(out=small[:, 1:2], in_=lamt[:], axis=AX.X), [b_lamt, b_small], [b_small])
            S.op("act", ACTF(small[:, 2:4], small[:, 0:2], AF.Exp), [b_small], [b_small])
            S.op("dve", TT(small[:, 4:5], small[:, 3:4], small[:, 2:3], ALU.subtract), [b_small], [b_small])
            S.op("dve", TS(small[:, 4:5], small[:, 4:5], -lam_init, ALU.add), [b_small], [b_small])
            S.op("act", ACTF(esink[:], brow[:, 128:136], AF.Exp), [b_brow], [b_esink])
            S.op("dve", TS(G2[:], brow[:, 0:128], 1.0 - lam_init, ALU.mult), [b_brow], [b_G2])

            rmsnorm_to_hT(li, C_LN1)
            dump("hT", hT[:], [b for r in bh for b in r])
            chk("N1")

            with contextlib.ExitStack() as st_att:
                S.mark_barrier()
                oT = sb(st_att, "oT", [128, 8, SEQ], BF16)
                boT = [[S.newbuf() for _ in range(4)] for _ in range(8)]
                with contextlib.ExitStack() as st_ab:
                    pTs = [sb(st_ab, f"pT{i}", [128, 512], BF16) for i in range(3)]; b_pTs = [S.newbuf() for _ in range(3)]
                    sbts = [sb(st_ab, f"sbt{i}", [128, 512], F32) for i in range(2)]; b_sbts = [S.newbuf() for _ in range(2)]
                    rot["pT"] = 0; rot["sbt"] = 0

                    def pT_next():
                        i = rot["pT"] % 3
                        rot["pT"] += 1
                        return pTs[i], b_pTs[i]

                    def sbt_next():
                        i = rot["sbt"] % 2
                        rot["sbt"] += 1
                        return sbts[i], b_sbts[i]

                    with contextlib.ExitStack() as st_a:
                        qaT = sb(st_a, "qaT", [128, 4, SEQ], BF16); b_qa = [[S.newbuf() for _ in range(4)] for _ in range(4)]
                        kaT = sb(st_a, "kaT", [128, 2, SEQ], BF16); b_ka = [[S.newbuf() for _ in range(4)] for _ in range(2)]
                        vaA = sb(st_a, "vaA", [128, 16, 2, 66], BF16); b_va = [S.newbuf() for _ in range(16)]
                        biasA = sb(st_a, "biasA_s", [128, 2, 3, 512], F32); b_biasA = S.newbuf()
                        ostA = [sb(st_a, f"ostA{i}", [128, 512], BF16) for i in range(2)]; b_ostA = [[S.newbuf() for _ in range(8)] for _ in range(2)]
                        r4s = sb(st_a, "r4s", [128, 2, 4], F32); b_r4 = [S.newbuf() for _ in range(2)]
                        S.dma("sp", "biasA", [(biasA[:].rearrange("p a b c -> p (a b c)"), biasA_d)], writes=[b_biasA])
                        S.op("pool", MEMSET(vaA[:, :, :, 64:66], 1.0), [], b_va)
                        for half2 in range(2):
                            ws, wb = w_get(("qa", li, half2))
                            wv = wview(ws, 8, 256)
                            for t2 in range(2):
                                ft = half2 * 2 + t2
                                for G in range(4):
                                    rf, rb = hT_rhs(G)
                                    pz, bpz = proj_fm(lambda c: wv[:, c, t2 * 128:(t2 + 1) * 128], G, 8, rf, rb, wb)
                                    norm64_store(pz, bpz, pc(li, C_GQA), qaT[:, ft, G * 512:(G + 1) * 512], b_qa[ft][G])
                        ws, wb = w_get(("ka", li))
                        wv = wview(ws, 8, 256)
                        for kap in range(2):
                            for G in range(4):
                                rf, rb = hT_rhs(G)
                                pz, bpz = proj_fm(lambda c: wv[:, c, kap * 128:(kap + 1) * 128], G, 8, rf, rb, wb)
                                norm64_store(pz, bpz, pc(li, C_GKA), kaT[:, kap, G * 512:(G + 1) * 512], b_ka[kap][G])
                        ws, wb = w_get(("va", li))
                        wv = wview(ws, 8, 128)
                        for t4 in range(4):
                            pv, bpv = ps_next()
                            for tq in range(4):
                                tt = t4 * 4 + tq
                                S.op("pe", [MM(pv[:, tq * 128:(tq + 1) * 128], hT[:, c, tt * 128:(tt + 1) * 128], wv[:, c, :],
                                               start=(c == 0), stop=(c == 7)) for c in range(8)],
                                     [wb] + [bh[c][t4] for c in range(8)], [bpv])
                            S.op("act", ACTF(vaA[:, t4 * 4:(t4 + 1) * 4, :, 0:64].rearrange("p t k e -> p (t k) e"),
                                             pv[:].rearrange("p (a e) -> p a e", e=64), AF.Copy),
                                 [bpv], b_va[t4 * 4:(t4 + 1) * 4])
                        dump("qaT", qaT[:], [b for r in b_qa for b in r])
                        dump("kaT", kaT[:], [b for r in b_ka for b in r])
                        dump("vaA", vaA[:], b_va)
                        chk("Ain")
                        for i in range(16):
                            ost, bost = ostA[i % 2], b_ostA[i % 2]
                            for kap in range(2):
                                js = [j for j in (i - 1, i, i + 1) if 0 <= j < 16]
                                acc, bacc = pbank[rot["accA"] % 3], bbank[rot["accA"] % 3]
                                rot["accA"] += 1
                                for jn, j in enumerate(js):
                                    sbt, bsbt = sbt_next()
                                    for hf in range(2):
                                        pS, bpS = ps_next()
                                        S.op("pe", MM(pS[:, 0:256],
                                                      kaT[hf * 64:(hf + 1) * 64, kap, j * 128:(j + 1) * 128],
                                                      qaT[hf * 64:(hf + 1) * 64, 2 * kap:2 * kap + 2, i * 128:(i + 1) * 128]),
                                             [b_ka[kap][j // 4], b_qa[2 * kap][i // 4], b_qa[2 * kap + 1][i // 4]], [bpS])
                                        S.op("dve", STT(sbt[:, hf * 256:(hf + 1) * 256], pS[:, 0:256], 0.125,
                                                        biasA[:, kap, j - i + 1, hf * 256:(hf + 1) * 256], ALU.mult, ALU.add),
                                             [bpS, b_biasA, bsbt], [bsbt])
                                    chk("A1")
                                    pT, bpT = pT_next()
                                    S.op("act", ACTF(pT[:], sbt[:], AF.Exp), [bsbt], [bpT])
                                    chk("A2")
                                    S.op("pe", [MM(acc[:, cb * 66:(cb + 1) * 66], pT[:, cb * 128:(cb + 1) * 128], vaA[:, j, kap, :],
                                                   start=(jn == 0 and cb == 0), stop=(jn == len(js) - 1))
                                                for cb in range(4)],
                                         [bpT, b_va[j]], [bacc])
                                    chk("A3")
                                accv = acc[:, 0:264].rearrange("p (a e) -> p a e", e=66)
                                r4 = r4s[:, kap, :]
                                S.op("dve", TT(r4, accv[:, :, 64], esink[:, kap * 4:(kap + 1) * 4], ALU.add), [bacc, b_esink], [b_r4[kap]])
                                S.op("dve", RECIP(r4, r4), [b_r4[kap]], [b_r4[kap]])
                                chk("A4")
                                for cb in range(4):
                                    h = 4 * kap + 2 * (cb % 2) + cb // 2
                                    if True:
                                        S.op("act", ACTF(ost[:, h * 64:(h + 1) * 64], accv[:, cb, 0:64], AF.Identity, scale=r4s[:, kap, cb:cb + 1]),
                                             [bacc, b_r4[kap]], [bost[h]])
                                    else:
                                        S.op("dve", TS(ost[:, h * 64:(h + 1) * 64], accv[:, cb, 0:64], r4s[:, kap, cb:cb + 1], ALU.mult),
                                             [bacc, b_r4[kap]], [bost[h]])
                                chk("A4b")
                            chk("A5")
                            ptr, bptr = ps_next()
                            S.op("pe", [MM(ptr[:, ft * 128:(ft + 1) * 128], ost[:, ft * 128:(ft + 1) * 128], ident[:]) for ft in range(4)],
                                 bost + [b_id], [bptr])
                            S.op("act", ACTF(oT[:, 0:4, i * 128:(i + 1) * 128], ptr[:].rearrange("p (a b) -> p a b", a=4), AF.Copy),
                                 [bptr], [boT[ft][i // 4] for ft in range(4)])
                    S.mark_barrier()
                    dump("oTa", oT[:, 0:4, :], [b for r in boT[0:4] for b in r])
                    chk("A")

                    with contextlib.ExitStack() as st_b:
                        qkT = [sb(st_b, f"qkT{i}", [128, 2, SEQ], BF16) for i in range(2)]
                        b_qk = [[[S.newbuf() for _ in range(4)] for _ in range(2)] for _ in range(2)]
                        vB = [sb(st_b, f"vB{i}", [128, 16, 130], BF16) for i in range(2)]
                        b_vB = [[S.newbuf() for _ in range(16)] for _ in range(2)]
                        strips = [sb(st_b, f"strip{i}", [128, 1152], F32) for i in range(2)]; b_strip = [S.newbuf() for _ in range(2)]
                        accS = sb(st_b, "accS", [128, 8, 130], F32); b_accS = S.newbuf()
                        o32 = sb(st_b, "o32", [128, 4, 128], F32); b_o32 = S.newbuf()
                        junk = sb(st_b, "junkB", [128, 128], F32); b_junk = S.newbuf()
                        ostB = [sb(st_b, f"ostB{i}", [128, 512], BF16) for i in range(2)]; b_ostB = [S.newbuf() for _ in range(2)]
                        sm = sb(st_b, "smB", [128, 32], F32); b_sm = S.newbuf()
                        for par in range(2):
                            S.op("pool", MEMSET(vB[par][:, :, 128:130], 1.0), [], b_vB[par])
                        rnd = 0
                        for h in range(4):
                            par = h % 2
                            qk, bqk, vb_, bvb = qkT[par], b_qk[par], vB[par], b_vB[par]
                            S.dma("sp", ("strip", par), [(strips[par][:], strip_d[h])], writes=[b_strip[par]])
                            ws, wb = w_get(("qkb", li, h))
                            wv = wview(ws, 8, 256)
                            for m in range(2):
                                for G in range(4):
                                    rf, rb = hT_rhs(G)
                                    pz, bpz = proj_fm(lambda c: wv[:, c, m * 128:(m + 1) * 128], G, 8, rf, rb, wb)
                                    norm64_store(pz, bpz, pc(li, C_GQB + m), qk[:, m, G * 512:(G + 1) * 512], bqk[m][G])
                            ws, wb = w_get(("vb", li, h))
                            wv = wview(ws, 8, 128)
                            for t4 in range(4):
                                pv, bpv = ps_next()
                                for tq in range(4):
                                    tt = t4 * 4 + tq
                                    S.op("pe", [MM(pv[:, tq * 128:(tq + 1) * 128], hT[:, c, tt * 128:(tt + 1) * 128], wv[:, c, :],
                                                   start=(c == 0), stop=(c == 7)) for c in range(8)],
                                         [wb] + [bh[c][t4] for c in range(8)], [bpv])
                                S.op("act", ACTF(vb_[:, t4 * 4:(t4 + 1) * 4, 0:128], pv[:].rearrange("p (a e) -> p a e", e=128), AF.Copy),
                                     [bpv], bvb[t4 * 4:(t4 + 1) * 4])
                            def evac_round(G, h=h):
                                S.op("act", ACTF(accS[:, 0:3, :], pbank[0][:, 0:390].rearrange("p (a e) -> p a e", e=130), AF.Copy), [bbank[0]], [b_accS])
                                S.op("dve", CP(accS[:, 3:6, :], pbank[1][:, 0:390].rearrange("p (a e) -> p a e", e=130)), [bbank[1], b_accS], [b_accS])
                                S.op("act", ACTF(accS[:, 6:8, :], pbank[2][:, 0:260].rearrange("p (a e) -> p a e", e=130), AF.Copy), [bbank[2], b_accS], [b_accS])
                                S.op("dve", RECIP(sm[:, 0:8], accS[:, :, 128]), [b_accS, b_sm], [b_sm])
                                S.op("dve", TS(sm[:, 8:12], sm[:, 4:8], small[:, 4:5], ALU.mult), [b_sm, b_small], [b_sm])
                                for b in range(4):
                                    S.op("act", ACTF(o32[:, b, :], accS[:, b, 0:128], AF.Identity, scale=sm[:, b:b + 1]), [b_accS, b_sm, b_o32], [b_o32])
                                for b in range(4):
                                    S.op("dve", STT(o32[:, b, :], accS[:, 4 + b, 0:128], sm[:, 8 + b:9 + b], o32[:, b, :], ALU.mult, ALU.add),
                                         [b_accS, b_sm, b_o32], [b_o32])
                                for b in range(4):
                                    S.op("act", ACTF(junk[:], o32[:, b, :], AF.Square, accum_out=sm[:, 12 + b:13 + b]), [b_o32, b_junk, b_sm], [b_junk, b_sm])
                                S.op("act", ACTF(sm[:, 16:20], sm[:, 12:16], AF.Ln, bias=epsc[:, 0:1], scale=1.0 / 128), [b_sm, b_eps], [b_sm])
                                S.op("act", ACTF(sm[:, 16:20], sm[:, 16:20], AF.Exp, scale=-0.5), [b_sm], [b_sm])
                                ost, bost = ostB[rot['rndB'] % 2], b_ostB[rot['rndB'] % 2]
                                rot['rndB'] += 1
                                for b in range(4):
                                    S.op("dve", STT(ost[:, b * 128:(b + 1) * 128], o32[:, b, :], sm[:, 16 + b:17 + b], G2[:], ALU.mult, ALU.mult),
                                         [b_o32, b_sm, b_G2], [bost])
                                ptr, bptr = ps_next()
                                S.op("pe", [MM(ptr[:, b * 128:(b + 1) * 128], ost[:, b * 128:(b + 1) * 128], ident[:]) for b in range(4)],
                                     [bost, b_id], [bptr])
                                S.op("act", ACTF(oT[:, 4 + h, G * 512:(G + 1) * 512], ptr[:], AF.Copy), [bptr], [boT[4 + h][G]])

                            steps = [(G, cm, j) for G in range(4) for cm in range(2) for j in range(16)]
                            LA = 2

                            def emit_qk(s_, qk=qk, bqk=bqk):
                                G, cm, j = steps[s_]
                                pS, bpS = ps_next()
                                S.op("pe", MM(pS[:], qk[cm * 64:(cm + 1) * 64, 1, j * 128:(j + 1) * 128],
                                              qk[cm * 64:(cm + 1) * 64, 0, G * 512:(G + 1) * 512]),
                                     [bqk[1][j // 4], bqk[0][G]], [bpS])
                                return pS, bpS

                            pend = {}
                            for s_ in range(LA):
                                pend[s_] = emit_qk(s_)
                            started = set()
                            for s_, (G, cm, j) in enumerate(steps):
                                if s_ + LA < len(steps):
                                    pend[s_ + LA] = emit_qk(s_ + LA)
                                pS, bpS = pend.pop(s_)
                                if cm == 0 and j == 0:
                                    started = set()
                                dl = j - 4 * G
                                pT, bpT = pT_next()
                                if -1 <= dl <= 4:
                                    sbt, bsbt = sbt_next()
                                    S.op("dve", STT(sbt[:], pS[:], 0.125, strips[par][:, (4 - dl) * 128:(4 - dl) * 128 + 512], ALU.mult, ALU.add),
                                         [bpS, b_strip[par]], [bsbt])
                                    S.op("act", ACTF(pT[:], sbt[:], AF.Exp), [bsbt], [bpT])
                                else:
                                    col = h if dl < 0 else 4 + h
                                    S.op("act", ACTF(pT[:], pS[:], AF.Exp, scale=0.125, bias=cfar[:, col:col + 1]), [bpS, b_cfar], [bpT])
                                fns = []
                                touched = []
                                for b in range(4):
                                    a = cm * 4 + b
                                    bank, off = a // 3, (a % 3) * 130
                                    fns.append(MM(pbank[bank][:, off:off + 130], pT[:, b * 128:(b + 1) * 128], vb_[:, j, :],
                                                  start=(bank not in started), stop=(j == 15)))
                                    started.add(bank)
                                    if bbank[bank] not in touched:
                                        touched.append(bbank[bank])
                                S.op("pe", fns, [bpT, bvb[j]], touched)
                                if cm == 1 and j == 15:
                                    evac_round(G)
                    S.mark_barrier()
                dump("oT", oT[:], [b for r in boT for b in r])
                chk("B")
                with contextlib.ExitStack() as st_m:
                    S.mark_barrier()
                    mixT = sb(st_m, "mixT", [128, 8, SEQ], BF16); b_mix = [[S.newbuf() for _ in range(4)] for _ in range(8)]
                    gts = [sb(st_m, f"gt{i}", [128, 512], F32) for i in range(4)]; b_gts = [S.newbuf() for _ in range(4)]
                    m1s = [sb(st_m, f"m1{i}", [128, 512], F32) for i in range(4)]; b_m1s = [S.newbuf() for _ in range(4)]
                    rr = 0
                    for ft in range(8):
                        wsg, wbg = w_get(("gate", li, ft))
                        wvg = wview(wsg, 8, 256)
                        wsp, wbp = w_get(("proj", li, ft))
                        wvp = wsp[:, 0:1024].rearrange("p (m c n) -> p m c n", m=2, c=4)
                        for G in range(4):
                            ts = slice(G * 512, (G + 1) * 512)
                            rf, rb = hT_rhs(G)
                            res = []
                            for m in range(2):
                                pg, bpg = proj_fm(lambda c: wvg[:, c, m * 128:(m + 1) * 128], G, 8, rf, rb, wbg)
                                gt, bgt = gts[rr % 4], b_gts[rr % 4]
                                S.op("act", ACTF(gt[:], pg[:], AF.Sigmoid, bias=pc(li, C_BG + m * 8 + ft)), [bpg, b_pc], [bgt])
                                dump("gt0", gt[:], [bgt])
                                pp, bpp = proj_fm(lambda c: wvp[:, m, c, :], G, 4, lambda c: oT[:, m * 4 + c, ts],
                                                  [boT[m * 4 + c][G] for c in range(4)], wbp)
                                m1, bm1 = m1s[rr % 4], b_m1s[rr % 4]
                                rr += 1
                                S.op("dve", TT(m1[:], pp[:], gt[:], ALU.mult), [bpp, bgt], [bm1])
                                res.append((m1, bm1))
                            S.op("dve", TT(mixT[:, ft, ts], res[0][0][:], res[1][0][:], ALU.add), [res[0][1], res[1][1]], [b_mix[ft][G]])
                    dump("mixT", mixT[:], [b for r in b_mix for b in r])
                    for f2 in range(4):
                        ws, wb = w_get(("wo", li, f2))
                        wv = wview(ws, 8, 256)
                        for t2 in range(2):
                            fo = f2 * 2 + t2
                            for G in range(4):
                                ts = slice(G * 512, (G + 1) * 512)
                                po, bpo = proj_fm(lambda c: wv[:, c, t2 * 128:(t2 + 1) * 128], G, 8, lambda c: mixT[:, c, ts],
                                                  [b_mix[c][G] for c in range(8)], wb)
                                S.op("dve", TT(xT[:, fo, ts], po[:], xT[:, fo, ts], ALU.add), [bpo, bx[fo][G]], [bx[fo][G]])
                S.mark_barrier()
            S.mark_barrier()

            dump("xmid", xT[:], [b for r in bx for b in r])
            chk("M")
            rmsnorm_to_hT(li, C_LN2)
            with contextlib.ExitStack() as st_f:
                S.mark_barrier()
                actT = sb(st_f, "actT", [128, 11, SEQ], BF16); b_act = [[S.newbuf() for _ in range(4)] for _ in range(11)]
                raws = [sb(st_f, f"raw{i}", [128, SEQ + 2], F32) for i in range(2)]; b_raws = [S.newbuf() for _ in range(2)]
                us = [sb(st_f, f"u{i}", [128, SEQ], F32) for i in range(2)]; b_us = [S.newbuf() for _ in range(2)]
                sg = sb(st_f, "sg", [128, SEQ], BF16); b_sg = S.newbuf()
                for i2 in range(2):
                    S.op("pool", MEMSET(raws[i2][:, 0:1], 0.0), [], [b_raws[i2]])
                    S.op("pool", MEMSET(raws[i2][:, SEQ + 1:SEQ + 2], 0.0), [b_raws[i2]], [b_raws[i2]])
                tcount = 0
                pending = [None]

                def make_conv(raw, braw, u, bu, ct, which, i):
                    def conv():
                        S.op("dve", STT(u[:], raw[:, 0:SEQ], pc(li, C_CW + ct), u[:], ALU.mult, ALU.add), [braw, bu, b_pc], [bu])
                        S.op("dve", STT(u[:], raw[:, 2:SEQ + 2], pc(li, C_CW + 88 + ct), u[:], ALU.mult, ALU.add), [braw, bu, b_pc], [bu])
                        if which == 1:
                            S.op("act", ACTF(sg[:], u[:], AF.Silu), [bu, b_sg], [b_sg])
                        else:
                            S.op("pool", TT(actT[:, i, :], sg[:], u[:], ALU.mult), [b_sg, bu], b_act[i])
                    return conv

                for half in range(2):
                    for i in range(11):
                        p = half * 11 + i
                        ws, wb = w_get(("up", li, p))
                        wv = wview(ws, 8, 256)
                        for which in (1, 0):
                            ct = p + 22 * which
                            raw, braw = raws[tcount % 2], b_raws[tcount % 2]
                            u, bu = us[tcount % 2], b_us[tcount % 2]
                            tcount += 1
                            for G in range(4):
                                rf, rb = hT_rhs(G)
                                pu, bpu = proj_fm(lambda c: wv[:, c, which * 128:(which + 1) * 128], G, 8, rf, rb, wb)
                                S.op("act", ACTF(raw[:, 1 + G * 512:1 + (G + 1) * 512], pu[:], AF.Copy), [bpu, braw], [braw])
                                S.op("act", ACTF(u[:, G * 512:(G + 1) * 512], pu[:], AF.Identity, scale=pc(li, C_CW + 44 + ct), bias=pc(li, C_CB + ct)),
                                     [bpu, b_pc, bu], [bu])
                            if pending[0] is not None:
                                pending[0]()
                            pending[0] = make_conv(raw, braw, u, bu, ct, which, i)
                    if pending[0] is not None:
                        pending[0]()
                        pending[0] = None
                    for fo in range(8):
                        ws, wb = w_get(("down", li, half, fo))
                        wv = wview(ws, 11, 128)
                        for G in range(4):
                            ts = slice(G * 512, (G + 1) * 512)
                            po, bpo = proj_fm(lambda k: wv[:, k, :], G, 11, lambda k: actT[:, k, ts], [b_act[k][G] for k in range(11)], wb)
                            S.op("dve", TT(xT[:, fo, ts], po[:], xT[:, fo, ts], ALU.add), [bpo, bx[fo][G]], [bx[fo][G]])
            S.mark_barrier()

        S.enabled = True
        allx = [b for row in bx for b in row]
        for c in range(8):
            S.dma("sp", ("out", c), [(out_d[c * 128:(c + 1) * 128, :], xT[:, c, :])], reads=bx[c])
        S.wait_all("sp", allx)
        assert stop is not None or wst["idx"] == len(plan)
        S.emit()
    return nc


def _t5_bucket_np(rel):
    rel = np.asarray(rel, np.int64)
    half, max_exact = 16, 8
    ret = np.where(rel > 0, half, 0)
    n = np.abs(rel)
    nf = np.maximum(n, 1).astype(np.float32)
    large = max_exact + (np.log(nf / np.float32(max_exact)) / np.float32(math.log(128 / max_exact))
                         * np.float32(half - max_exact)).astype(np.int32)
    large = np.minimum(large, half - 1)
    return ret + np.where(n < max_exact, n, large)


def _host_layout(inp, layers):
    f32 = np.float32
    L = list(layers)
    g = lambda k: np.asarray(inp[k], f32)
    pcols = np.zeros((128, len(L) * NCOL), f32)
    brow = np.zeros((len(L), NBROW), f32)
    for li, l in enumerate(L):
        pcl = np.zeros((128, NCOL), f32)
        pcl[:, C_LN1:C_LN1 + 8] = g("ln1_g")[l].reshape(8, 128).T
        pcl[:, C_LN2:C_LN2 + 8] = g("ln2_g")[l].reshape(8, 128).T
        pcl[:, C_GQA] = np.tile(g("qn_a")[l], 2)
        pcl[:, C_GKA] = np.tile(g("kn_a")[l], 2)
        pcl[:, C_GQB] = np.tile(g("qn_b")[l], 2)
        pcl[:, C_GKB] = np.tile(g("kn_b")[l], 2)
        pcl[:, C_BG:C_BG + 16] = g("b_gate")[l].reshape(16, 128).T
        pcl[:, C_CW:C_CW + 132] = g("conv_w")[l].reshape(3, 44, 128).transpose(2, 0, 1).reshape(128, 132)
        pcl[:, C_CB:C_CB + 44] = g("conv_b")[l].reshape(44, 128).T
        pcols[:, li * NCOL:(li + 1) * NCOL] = pcl
        brow[li] = np.concatenate([g("subln_g")[l], g("sink")[l][SINK_PERM], g("lam_q1")[l], g("lam_k1")[l],
                                   g("lam_q2")[l], g("lam_k2")[l]])
    tab = g("rel_bias")
    cfar = np.concatenate([tab[15, 8:12], tab[31, 8:12]]).astype(f32)
    k = np.arange(128)[:, None]
    c = np.arange(1152)[None, :]
    bk = _t5_bucket_np(k - c + 512)
    stripB = np.stack([tab[bk, 8 + h] for h in range(4)]).astype(f32)
    q = np.arange(128)[None, :]
    biasA = np.zeros((128, 2, 3, 4, 128), f32)
    for di in range(3):
        rel = k + 128 * (di - 1) - q
        bkt = _t5_bucket_np(rel)
        ok = np.abs(rel) <= 128
        for kap in range(2):
            for cb in range(4):
                h = 4 * kap + 2 * (cb % 2) + cb // 2
                biasA[:, kap, di, cb, :] = np.where(ok, tab[bkt, h], f32(MASK_NEG))
    bones = np.zeros((128, 128), f32)
    bones[:64, :64] = 1.0 / 64
    bones[64:, 64:] = 1.0 / 64
    common = {
        "w_in": np.ascontiguousarray(g("w_in")[L]), "w_gate": np.ascontiguousarray(g("w_gate")[L]),
        "w_a_proj": np.ascontiguousarray(g("w_a_proj")[L]), "w_b_proj": np.ascontiguousarray(g("w_b_proj")[L]),
        "w_o": np.ascontiguousarray(g("w_o")[L]), "w_up": np.ascontiguousarray(g("w_up")[L]),
        "w_down": np.ascontiguousarray(g("w_down")[L]),
        "pcols": pcols, "brow": brow, "cfar": cfar, "stripB": stripB,
        "biasA": np.ascontiguousarray(biasA.reshape(128, 2 * 3 * 512)),
        "ident": np.eye(128, dtype=f32), "bones": bones,
    }
    return common


_PROGS = {}


def _run(layers, xT_list, inp):
    key = tuple(layers)
    if key not in _PROGS:
        _PROGS[key] = build_program(list(layers))
    nc = _PROGS[key]
    common = _host_layout(inp, layers)
    in_maps = [dict(common, xT=xT_list[b]) for b in range(NCORES)]
    res = run_bass_kernel_spmd(nc, in_maps, core_ids=list(range(NCORES)))
    return [np.asarray(r["outT"]) for r in res.results]


FUSED = True


def kernel(**inputs):
    x = np.asarray(inputs["x"], np.float32)
    xT = [np.ascontiguousarray(x[b].T) for b in range(NCORES)]
    if FUSED:
        xT = _run(range(DEPTH), xT, inputs)
    else:
        for l in range(DEPTH):
            xT = _run([l], xT, inputs)
    return np.stack([t.T for t in xT]).astype(np.float32)
```

```python
import contextlib
import math
import numpy as np
import concourse.bass as bass
import concourse.mybir as mybir
from concourse.bass_utils import run_bass_kernel_spmd

F32 = mybir.dt.float32
BF16 = mybir.dt.bfloat16
AF = mybir.ActivationFunctionType
ALU = mybir.AluOpType
AX = mybir.AxisListType

D = 1024
SEQ = 2048
DEPTH = 4
NCORES = 8
D_FF = 2816
IN_W = 2304
EPS = 1e-6
NCOL = 212
C_LN1, C_LN2, C_GQA, C_GKA, C_GQB, C_GKB, C_BG, C_CW, C_CB = 0, 8, 16, 17, 18, 19, 20, 36, 168
NBROW = 392
SINK_PERM = [0, 2, 1, 3, 4, 6, 5, 7]
MASK_NEG = -30000.0

EPOCH = 2000
ENGS = ("pe", "act", "dve", "pool", "sp")


class Buf:
    __slots__ = ("w", "r", "excl")

    def __init__(self, r=None):
        self.excl = False
        self.w = None
        self.r = dict(r) if r else {}


class Sched:
    def __init__(self, nc, stack):
        self.nc = nc
        self.stack = stack
        self.ops = {e: [] for e in ENGS}
        self.seq = {e: 0 for e in ENGS}
        self.sems = {}
        self.waited = {e: {} for e in ENGS}
        self.dma_cnt = {}
        self.barrier = {}
        self.enabled = True

    def sem(self, key):
        s = self.sems.get(key)
        if s is None:
            s = self.stack.enter_context(self.nc.semaphore("s_" + "_".join(str(k) for k in key)))
            self.sems[key] = s
        return s

    def newbuf(self):
        return Buf(self.barrier)

    def mark_barrier(self):
        b = {}
        for e in ENGS:
            n = self.seq[e]
            if n > 0:
                b[(e, (n - 1) // EPOCH)] = (n - 1) % EPOCH + 1
        for k, c in self.dma_cnt.items():
            b[k] = 16 * c
        self.barrier = b

    def _deps(self, eng, reads, writes):
        deps = {}
        for b in reads:
            if b.w is not None:
                k, v = b.w
                if deps.get(k, 0) < v:
                    deps[k] = v
        for b in writes:
            if b.w is not None:
                k, v = b.w
                if deps.get(k, 0) < v:
                    deps[k] = v
            for k, v in b.r.items():
                if deps.get(k, 0) < v:
                    deps[k] = v
        out = []
        wd = self.waited[eng]
        for k, v in deps.items():
            if eng == "pe" and k[0] == "pe":
                continue
            if wd.get(k, 0) >= v:
                continue
            wd[k] = v
            out.append((self.sem(k), v))
        return out

    def _commit(self, tok, reads, writes):
        k, v = tok
        for b in writes:
            b.w = tok
            b.r = {}
        for b in reads:
            if b.r.get(k, 0) < v:
                b.r[k] = v

    def op(self, eng, fns, reads=(), writes=()):
        if not self.enabled:
            return
        if not isinstance(fns, (list, tuple)):
            fns = [fns]
        if any(b.excl for b in reads):
            writes = list(writes) + [b for b in reads if b.excl]
            reads = [b for b in reads if not b.excl]
        waits = self._deps(eng, reads, writes)
        n = self.seq[eng]
        self.seq[eng] = n + 1
        key = (eng, n // EPOCH)
        tok = (key, n % EPOCH + 1)
        self.ops[eng].append((waits, fns, self.sem(key), 1))
        self._commit(tok, reads, writes)

    def dma(self, eng, slot, pairs, reads=(), writes=(), **kw):
        if not self.enabled:
            return
        waits = self._deps(eng, reads, writes)
        key = ("dma", slot)
        sem = self.sem(key)
        c = self.dma_cnt.get(key, 0)
        first = True
        for (o, i) in pairs:
            c += 1
            fn = (lambda e, o=o, i=i: e.dma_start(out=o, in_=i, **kw))
            self.ops[eng].append((waits if first else [], [fn], sem, 16))
            first = False
        self.dma_cnt[key] = c
        self._commit((key, 16 * c), reads, writes)

    def wait_all(self, eng, bufs):
        waits = self._deps(eng, bufs, bufs)
        self.ops[eng].append((waits, [], None, 0))

    def emit(self):
        objs = {"pe": "tensor", "act": "scalar", "dve": "vector", "pool": "gpsimd", "sp": "sync"}
        with self.nc.Block() as block:
            for e in ENGS:
                lst = self.ops[e]
                if not lst:
                    continue

                def body(engobj, lst=lst):
                    for waits, fns, sem, inc in lst:
                        for s, v in waits:
                            engobj.wait_ge(s, v)
                        for fn in fns[:-1]:
                            fn(engobj)
                        if fns:
                            fns[-1](engobj).then_inc(sem, inc)

                getattr(block, objs[e])(body)


def MM(out, lhsT, rhs, start=True, stop=True):
    return lambda e: e.matmul(out, lhsT=lhsT, rhs=rhs, start=start, stop=stop, skip_group_check=True)


def ACTF(out, in_, func, **kw):
    return lambda e: e.activation(out=out, in_=in_, func=func, **kw)


def STT(out, in0, scalar, in1, op0, op1):
    return lambda e: e.scalar_tensor_tensor(out=out, in0=in0, scalar=scalar, in1=in1, op0=op0, op1=op1)


def TT(out, in0, in1, op):
    return lambda e: e.tensor_tensor(out=out, in0=in0, in1=in1, op=op)


def TS(out, in0, s1, op0, s2=None, op1=None):
    if op1 is None:
        return lambda e: e.tensor_scalar(out=out, in0=in0, scalar1=s1, scalar2=None, op0=op0)
    return lambda e: e.tensor_scalar(out=out, in0=in0, scalar1=s1, scalar2=s2, op0=op0, op1=op1)


def CP(out, in_):
    return lambda e: e.tensor_copy(out=out, in_=in_)


def RECIP(out, in_):
    return lambda e: e.reciprocal(out=out, in_=in_)


def MEMSET(ap, v):
    return lambda e: e.memset(ap, v)


class _Stop(Exception):
    pass


def build_program(layers, stop=None, dumps=()):
    NL = len(layers)
    dbg_outs = {}
    nc = bass.Bass("TRN2", target_bir_lowering=False)

    def din(name, shape):
        return nc.dram_tensor(name, list(shape), F32, kind="ExternalInput").ap()

    xT_d = din("xT", [D, SEQ])
    w_in_d = din("w_in", [NL, D, IN_W])
    w_gate_d = din("w_gate", [NL, D, 2 * D])
    w_a_d = din("w_a_proj", [NL, 512, D])
    w_b_d = din("w_b_proj", [NL, 512, D])
    w_o_d = din("w_o", [NL, D, D])
    w_up_d = din("w_up", [NL, D, 2 * D_FF])
    w_dn_d = din("w_down", [NL, D_FF, D])
    pcols_d = din("pcols", [128, NL * NCOL])
    brow_d = din("brow", [NL, NBROW])
    cfar_d = din("cfar", [8])
    strip_d = din("stripB", [4, 128, 1152])
    biasA_d = din("biasA", [128, 2 * 3 * 512])
    ident_d = din("ident", [128, 128])
    bones_d = din("bones", [128, 128])
    out_d = nc.dram_tensor("outT", [D, SEQ], F32, kind="ExternalOutput").ap()

    with contextlib.ExitStack() as st:
        S = Sched(nc, st)

        uid = [0]

        def sb(stack, name, shape, dt):
            uid[0] += 1
            return stack.enter_context(nc.sbuf_tensor(f"{name}_{uid[0]}", list(shape), dt))

        chk_cnt = {}

        def chk(name):
            chk_cnt[name] = chk_cnt.get(name, 0) + 1
            if stop is None:
                return
            nm, _, n = stop.partition("#")
            if nm == name and chk_cnt[name] == int(n or 1):
                S.enabled = False

        def dump(name, ap, bufs):
            if name not in dumps or name in dbg_outs:
                return
            shp = list(ap.shape)
            dt_ = nc.dram_tensor("dbg_" + name, shp, ap.dtype, kind="ExternalOutput").ap()
            dbg_outs[name] = dt_
            S.dma("sp", ("dbg", name), [(dt_, ap)], reads=bufs)
            S.wait_all("sp", bufs)

        xT = sb(st, "xT_s", [128, 8, SEQ], F32)
        bx = [[S.newbuf() for _ in range(4)] for _ in range(8)]
        hT = sb(st, "hT_s", [128, 8, SEQ], BF16)
        bh = [[S.newbuf() for _ in range(4)] for _ in range(8)]
        pcols = sb(st, "pcols_s", [128, NL * NCOL], F32); b_pc = S.newbuf()
        brow = sb(st, "brow_s", [128, NBROW], F32); b_brow = S.newbuf()
        cfar = sb(st, "cfar_s", [128, 8], F32); b_cfar = S.newbuf()
        identf = sb(st, "identf", [128, 128], F32); b_idf = S.newbuf()
        bonesf = sb(st, "bonesf", [128, 128], F32); b_bof = S.newbuf()
        ident = sb(st, "ident_b", [128, 128], BF16); b_id = S.newbuf()
        bones = sb(st, "bones_b", [128, 128], BF16); b_bo = S.newbuf()
        ones = sb(st, "ones_b", [128, 128], BF16); b_ones = S.newbuf()
        epsc = sb(st, "epsc", [128, 1], F32); b_eps = S.newbuf()
        small = sb(st, "small", [128, 32], F32); b_small = S.newbuf()
        esink = sb(st, "esink", [128, 8], F32); b_esink = S.newbuf()
        G2 = sb(st, "G2", [128, 128], F32); b_G2 = S.newbuf()
        lamt = sb(st, "lamt", [128, 64], F32); b_lamt = S.newbuf()
        NWS = 4
        wslots = [sb(st, f"wslot{i}", [128, 2048], BF16) for i in range(NWS)]
        wbufs = [S.newbuf() for _ in range(NWS)]
        sqs = [sb(st, f"sq{i}", [128, 512], BF16) for i in range(2)]; b_sqs = [S.newbuf() for _ in range(2)]
        rts = [sb(st, f"rt{i}", [128, 512], F32) for i in range(2)]; b_rts = [S.newbuf() for _ in range(2)]

        pbank = [st.enter_context(nc.psum_tensor(f"pb{i}", [128, 512], F32)) for i in range(8)]
        bbank = [S.newbuf() for _ in range(8)]
        for b_ in bbank:
            b_.excl = True
        rot = {"ps": 0, "sq": 0, "rt": 0, "accA": 0, "rndB": 0}

        def ps_next():
            i = 3 + rot["ps"] % 5
            rot["ps"] += 1
            return pbank[i], bbank[i]

        def sq_next():
            i = rot["sq"] % 2
            rot["sq"] += 1
            return sqs[i], b_sqs[i]

        def rt_next():
            i = rot["rt"] % 2
            rot["rt"] += 1
            return rts[i], b_rts[i]

        def wview(slot, a, b):
            return slot[:, 0:a * b].rearrange("p (a b) -> p a b", a=a)

        def plan_layer(li):
            P = []

            def incols(c0, n):
                return w_in_d[li, :, c0:c0 + n].rearrange("(c p) n -> p c n", p=128)

            P.append((("qa", li, 0), lambda s: [(wview(s, 8, 256), incols(0, 256))]))
            P.append((("qa", li, 1), lambda s: [(wview(s, 8, 256), incols(256, 256))]))

            def kdup(s):
                v = s[:, 0:2048].rearrange("p (c k d e) -> p c k d e", c=8, k=2, d=2)
                return [(v[:, :, k, dd, :], incols(512 + k * 64, 64)) for k in range(2) for dd in range(2)]

            P.append((("ka", li), kdup))
            P.append((("va", li), lambda s: [(wview(s, 8, 128), incols(640, 128))]))
            for h in range(4):
                def qk(s, h=h):
                    v = wview(s, 8, 256)
                    return [(v[:, :, 0:128], incols(768 + h * 128, 128)),
                            (v[:, :, 128:256], incols(1280 + h * 128, 128))]
                P.append((("qkb", li, h), qk))
                P.append((("vb", li, h), lambda s, h=h: [(wview(s, 8, 128), incols(1792 + h * 128, 128))]))
            for ft in range(8):
                def gate(s, ft=ft):
                    v = wview(s, 8, 256)
                    g = lambda c0: w_gate_d[li, :, c0:c0 + 128].rearrange("(c p) n -> p c n", p=128)
                    return [(v[:, :, 0:128], g(ft * 128)), (v[:, :, 128:256], g(D + ft * 128))]
                P.append((("gate", li, ft), gate))

                def proj(s, ft=ft):
                    v = s[:, 0:1024].rearrange("p (m c n) -> p m c n", m=2, c=4)
                    a = w_a_d[li, :, ft * 128:(ft + 1) * 128].rearrange("(c p) n -> p c n", p=128)
                    b = w_b_d[li, :, ft * 128:(ft + 1) * 128].rearrange("(c p) n -> p c n", p=128)
                    return [(v[:, 0], a), (v[:, 1], b)]
                P.append((("proj", li, ft), proj))
            for f2 in range(4):
                P.append((("wo", li, f2), lambda s, f2=f2: [(wview(s, 8, 256), w_o_d[li, :, f2 * 256:(f2 + 1) * 256].rearrange("(c p) n -> p c n", p=128))]))
            for half in range(2):
                for i in range(11):
                    p = half * 11 + i

                    def up(s, p=p):
                        v = wview(s, 8, 256)
                        u = lambda c0: w_up_d[li, :, c0:c0 + 128].rearrange("(c p) n -> p c n", p=128)
                        return [(v[:, :, 0:128], u(p * 128)), (v[:, :, 128:256], u(D_FF + p * 128))]
                    P.append((("up", li, p), up))
                for fo in range(8):
                    def down(s, half=half, fo=fo):
                        v = wview(s, 11, 128)
                        src = w_dn_d[li, half * 1408:(half + 1) * 1408, fo * 128:(fo + 1) * 128].rearrange("(k p) n -> p k n", p=128)
                        return [(v, src)]
                    P.append((("down", li, half, fo), down))
            return P

        plan = []
        for li in range(NL):
            plan += plan_layer(li)
        wst = {"idx": 0, "issued": 0}
        AHEAD = NWS - 2

        def w_issue_upto(k):
            while wst["issued"] <= min(k, len(plan) - 1):
                t = wst["issued"]
                slot = t % NWS
                S.dma("pool", ("w", slot, plan[t][0][1]), plan[t][1](wslots[slot]), writes=[wbufs[slot]])
                wst["issued"] += 1

        def w_get(key):
            t = wst["idx"]
            assert plan[t][0] == key, (plan[t][0], key)
            w_issue_upto(t + AHEAD)
            wst["idx"] += 1
            return wslots[t % NWS], wbufs[t % NWS]

        for c in range(8):
            S.dma("sp", ("x", c), [(xT[:, c, :], xT_d[c * 128:(c + 1) * 128, :])], writes=bx[c])
        S.dma("sp", "c0", [(pcols[:], pcols_d)], writes=[b_pc])
        S.dma("sp", "c1", [(cfar[:], cfar_d.partition_broadcast(128))], writes=[b_cfar])
        S.dma("sp", "c2", [(identf[:], ident_d)], writes=[b_idf])
        S.dma("sp", "c3", [(bonesf[:], bones_d)], writes=[b_bof])
        S.op("dve", CP(ident[:], identf[:]), [b_idf], [b_id])
        S.op("dve", CP(bones[:], bonesf[:]), [b_bof], [b_bo])
        S.op("pool", MEMSET(ones[:], 1.0 / D), [], [b_ones])
        S.op("pool", MEMSET(epsc[:], EPS), [], [b_eps])
        w_issue_upto(AHEAD - 1)

        def pc(li, col, n=1):
            return pcols[:, li * NCOL + col: li * NCOL + col + n]

        def rstd_from_ms(pm, bpm, scale=1.0):
            rt, brt = rt_next()
            S.op("act", ACTF(rt[:], pm, AF.Ln, bias=epsc[:, 0:1], scale=scale), [bpm, b_eps], [brt])
            S.op("act", ACTF(rt[:], rt[:], AF.Exp, scale=-0.5), [brt], [brt])
            return rt, brt

        def rmsnorm_to_hT(li, gcol0):
            for G in range(4):
                ts = slice(G * 512, (G + 1) * 512)
                pm, bpm = ps_next()
                for c in range(8):
                    sq, bsq = sq_next()
                    S.op("act", ACTF(sq[:], xT[:, c, ts], AF.Square), [bx[c][G]], [bsq])
                    S.op("pe", MM(pm[:], ones[:], sq[:], start=(c == 0), stop=(c == 7)), [b_ones, bsq], [bpm])
                rt, brt = rstd_from_ms(pm[:], bpm)
                for c in range(8):
                    S.op("dve", STT(hT[:, c, ts], xT[:, c, ts], pc(li, gcol0 + c), rt[:], ALU.mult, ALU.mult),
                         [bx[c][G], b_pc, brt], [bh[c][G]])

        def proj_fm(wap_fn, G, kc, rhs_fn, rhs_bufs, wb):
            pz, bpz = ps_next()
            S.op("pe", [MM(pz[:], wap_fn(c), rhs_fn(c), start=(c == 0), stop=(c == kc - 1)) for c in range(kc)],
                 [wb] + rhs_bufs, [bpz])
            return pz, bpz

        def norm64_store(pz, bpz, gcol, dst, bdst):
            sq, bsq = sq_next()
            S.op("act", ACTF(sq[:], pz[:], AF.Square), [bpz], [bsq])
            pm, bpm = ps_next()
            S.op("pe", MM(pm[:], bones[:], sq[:]), [b_bo, bsq], [bpm])
            rt, brt = rstd_from_ms(pm[:], bpm)
            S.op("dve", STT(dst, pz[:], gcol, rt[:], ALU.mult, ALU.mult), [bpz, b_pc, brt], [bdst])

        def hT_rhs(G):
            return (lambda c: hT[:, c, G * 512:(G + 1) * 512]), [bh[c][G] for c in range(8)]

        dq = []

        def drain(n=1):
            for _ in range(n):
                if dq:
                    dq.pop(0)()

        def run_inproj(jobs):
            prev = None
            for (wfn, G, wb_, gcol, dst, bdst) in jobs:
                rf, rb = hT_rhs(G)
                pz, bpz = proj_fm(wfn, G, 8, rf, rb, wb_)
                if prev is not None:
                    norm64_store(*prev)
                prev = (pz, bpz, gcol, dst, bdst)
            if prev is not None:
                norm64_store(*prev)

        for li, l in enumerate(layers):
            lam_init = 0.8 - 0.6 * math.exp(-0.3 * l)
            S.dma("sp", "brow", [(brow[:], brow_d[li].partition_broadcast(128))], writes=[b_brow])
            S.op("dve", TT(lamt[:], brow[:, 136:200], brow[:, 200:264], ALU.mult), [b_brow], [b_lamt])
            S.op("dve", lambda e: e.reduce_sum(out=small[:, 0:1], in_=lamt[:], axis=AX.X), [b_lamt], [b_small])
            S.op("dve", TT(lamt[:], brow[:, 264:328], brow[:, 328:392], ALU.mult), [b_brow, b_lamt], [b_lamt])
            S.op("dve", lambda e: e.reduce_sum(out=small[:, 1:2], in_=lamt[:], axis=AX.X), [b_lamt, b_small], [b_small])
            S.op("act", ACTF(small[:, 2:4], small[:, 0:2], AF.Exp), [b_small], [b_small])
            S.op("dve", TT(small[:, 4:5], small[:, 3:4], small[:, 2:3], ALU.subtract), [b_small], [b_small])
            S.op("dve", TS(small[:, 4:5], small[:, 4:5], -lam_init, ALU.add), [b_small], [b_small])
            S.op("act", ACTF(esink[:], brow[:, 128:136], AF.Exp), [b_brow], [b_esink])
            S.op("dve", TS(G2[:], brow[:, 0:128], 1.0 - lam_init, ALU.mult), [b_brow], [b_G2])

            rmsnorm_to_hT(li, C_LN1)
            dump("hT", hT[:], [b for r in bh for b in r])
            chk("N1")

            with contextlib.ExitStack() as st_att:
                S.mark_barrier()
                oT = sb(st_att, "oT", [128, 8, SEQ], BF16)
                boT = [[S.newbuf() for _ in range(4)] for _ in range(8)]
                with contextlib.ExitStack() as st_ab:
                    pTs = [sb(st_ab, f"pT{i}", [128, 512], BF16) for i in range(3)]; b_pTs = [S.newbuf() for _ in range(3)]
                    sbts = [sb(st_ab, f"sbt{i}", [128, 512], F32) for i in range(2)]; b_sbts = [S.newbuf() for _ in range(2)]
                    rot["pT"] = 0; rot["sbt"] = 0

                    def pT_next():
                        i = rot["pT"] % 3
                        rot["pT"] += 1
                        return pTs[i], b_pTs[i]

                    def sbt_next():
                        i = rot["sbt"] % 2
                        rot["sbt"] += 1
                        return sbts[i], b_sbts[i]

                    with contextlib.ExitStack() as st_a:
                        qaT = sb(st_a, "qaT", [128, 4, SEQ], BF16); b_qa = [[S.newbuf() for _ in range(4)] for _ in range(4)]
                        kaT = sb(st_a, "kaT", [128, 2, SEQ], BF16); b_ka = [[S.newbuf() for _ in range(4)] for _ in range(2)]
                        vaA = sb(st_a, "vaA", [128, 16, 2, 66], BF16); b_va = [S.newbuf() for _ in range(16)]
                        biasA = sb(st_a, "biasA_s", [128, 2, 3, 512], F32); b_biasA = S.newbuf()
                        ostA = [sb(st_a, f"ostA{i}", [128, 512], BF16) for i in range(2)]; b_ostA = [[S.newbuf() for _ in range(8)] for _ in range(2)]
                        r4s = sb(st_a, "r4s", [128, 2, 4], F32); b_r4 = [S.newbuf() for _ in range(2)]
                        S.dma("sp", "biasA", [(biasA[:].rearrange("p a b c -> p (a b c)"), biasA_d)], writes=[b_biasA])
                        S.op("pool", MEMSET(vaA[:, :, :, 64:66], 1.0), [], b_va)
                        for half2 in range(2):
                            ws, wb = w_get(("qa", li, half2))
                            wv = wview(ws, 8, 256)
                            run_inproj([((lambda c, wv=wv, t2=t2: wv[:, c, t2 * 128:(t2 + 1) * 128]), G, wb, pc(li, C_GQA),
                                         qaT[:, half2 * 2 + t2, G * 512:(G + 1) * 512], b_qa[half2 * 2 + t2][G])
                                        for t2 in range(2) for G in range(4)])
                        ws, wb = w_get(("ka", li))
                        wv = wview(ws, 8, 256)
                        run_inproj([((lambda c, wv=wv, kap=kap: wv[:, c, kap * 128:(kap + 1) * 128]), G, wb, pc(li, C_GKA),
                                     kaT[:, kap, G * 512:(G + 1) * 512], b_ka[kap][G])
                                    for kap in range(2) for G in range(4)])
                        ws, wb = w_get(("va", li))
                        wv = wview(ws, 8, 128)
                        for t4 in range(4):
                            pv, bpv = ps_next()
                            for tq in range(4):
                                tt = t4 * 4 + tq
                                S.op("pe", [MM(pv[:, tq * 128:(tq + 1) * 128], hT[:, c, tt * 128:(tt + 1) * 128], wv[:, c, :],
                                               start=(c == 0), stop=(c == 7)) for c in range(8)],
                                     [wb] + [bh[c][t4] for c in range(8)], [bpv])
                            S.op("act", ACTF(vaA[:, t4 * 4:(t4 + 1) * 4, :, 0:64].rearrange("p t k e -> p (t k) e"),
                                             pv[:].rearrange("p (a e) -> p a e", e=64), AF.Copy),
                                 [bpv], b_va[t4 * 4:(t4 + 1) * 4])
                        dump("qaT", qaT[:], [b for r in b_qa for b in r])
                        dump("kaT", kaT[:], [b for r in b_ka for b in r])
                        dump("vaA", vaA[:], b_va)
                        chk("Ain")
                        stepsA = []
                        for i in range(16):
                            for kap in range(2):
                                js = [j for j in (i - 1, i, i + 1) if 0 <= j < 16]
                                for jn, j in enumerate(js):
                                    stepsA.append((i, kap, jn, j, len(js)))

                        def emit_qk_a(s_):
                            i, kap, jn, j, nj = stepsA[s_]
                            out = []
                            for hf in range(2):
                                pS, bpS = ps_next()
                                S.op("pe", MM(pS[:, 0:256],
                                              kaT[hf * 64:(hf + 1) * 64, kap, j * 128:(j + 1) * 128],
                                              qaT[hf * 64:(hf + 1) * 64, 2 * kap:2 * kap + 2, i * 128:(i + 1) * 128]),
                                     [b_ka[kap][j // 4], b_qa[2 * kap][i // 4], b_qa[2 * kap + 1][i // 4]], [bpS])
                                out.append((pS, bpS))
                            return out

                        def evacA_1(acc, bacc, kap):
                            def f():
                                accv = acc[:, 0:264].rearrange("p (a e) -> p a e", e=66)
                                r4 = r4s[:, kap, :]
                                S.op("dve", TT(r4, accv[:, :, 64], esink[:, kap * 4:(kap + 1) * 4], ALU.add), [bacc, b_esink], [b_r4[kap]])
                                S.op("dve", RECIP(r4, r4), [b_r4[kap]], [b_r4[kap]])
                            return f

                        def evacA_2(acc, bacc, kap, ost, bost):
                            def f():
                                accv = acc[:, 0:264].rearrange("p (a e) -> p a e", e=66)
                                for cb in range(4):
                                    h = 4 * kap + 2 * (cb % 2) + cb // 2
                                    S.op("act", ACTF(ost[:, h * 64:(h + 1) * 64], accv[:, cb, 0:64], AF.Identity, scale=r4s[:, kap, cb:cb + 1]),
                                         [bacc, b_r4[kap]], [bost[h]])
                            return f

                        def evacA_3(i, ost, bost):
                            def f():
                                ptr, bptr = ps_next()
                                S.op("pe", [MM(ptr[:, ft * 128:(ft + 1) * 128], ost[:, ft * 128:(ft + 1) * 128], ident[:]) for ft in range(4)],
                                     bost + [b_id], [bptr])
                                S.op("act", ACTF(oT[:, 0:4, i * 128:(i + 1) * 128], ptr[:].rearrange("p (a b) -> p a b", a=4), AF.Copy),
                                     [bptr], [boT[ft][i // 4] for ft in range(4)])
                            return f

                        pendA = {0: emit_qk_a(0)}
                        acc = bacc = None
                        for s_, (i, kap, jn, j, nj) in enumerate(stepsA):
                            if s_ + 1 < len(stepsA):
                                pendA[s_ + 1] = emit_qk_a(s_ + 1)
                            ost, bost = ostA[i % 2], b_ostA[i % 2]
                            if jn == 0:
                                acc, bacc = pbank[rot["accA"] % 3], bbank[rot["accA"] % 3]
                                rot["accA"] += 1
                            sbt, bsbt = sbt_next()
                            for hf, (pS, bpS) in enumerate(pendA.pop(s_)):
                                S.op("dve", STT(sbt[:, hf * 256:(hf + 1) * 256], pS[:, 0:256], 0.125,
                                                biasA[:, kap, j - i + 1, hf * 256:(hf + 1) * 256], ALU.mult, ALU.add),
                                     [bpS, b_biasA, bsbt], [bsbt])
                            pT, bpT = pT_next()
                            S.op("act", ACTF(pT[:], sbt[:], AF.Exp), [bsbt], [bpT])
                            S.op("pe", [MM(acc[:, cb * 66:(cb + 1) * 66], pT[:, cb * 128:(cb + 1) * 128], vaA[:, j, kap, :],
                                           start=(jn == 0 and cb == 0), stop=(jn == nj - 1))
                                        for cb in range(4)],
                                 [bpT, b_va[j]], [bacc])
                            drain(1)
                            if jn == nj - 1:
                                dq.append(evacA_1(acc, bacc, kap))
                                dq.append(evacA_2(acc, bacc, kap, ost, bost))
                                if kap == 1:
                                    dq.append(evacA_3(i, ost, bost))
                        drain(len(dq))
                    S.mark_barrier()
                    dump("oTa", oT[:, 0:4, :], [b for r in boT[0:4] for b in r])
                    chk("A")

                    with contextlib.ExitStack() as st_b:
                        qkT = [sb(st_b, f"qkT{i}", [128, 2, SEQ], BF16) for i in range(2)]
                        b_qk = [[[S.newbuf() for _ in range(4)] for _ in range(2)] for _ in range(2)]
                        vB = [sb(st_b, f"vB{i}", [128, 16, 130], BF16) for i in range(2)]
                        b_vB = [[S.newbuf() for _ in range(16)] for _ in range(2)]
                        strips = [sb(st_b, f"strip{i}", [128, 1152], F32) for i in range(2)]; b_strip = [S.newbuf() for _ in range(2)]
                        accS = sb(st_b, "accS", [128, 8, 130], F32); b_accS = S.newbuf()
                        o32 = sb(st_b, "o32", [128, 4, 128], F32); b_o32 = S.newbuf()
                        junk = sb(st_b, "junkB", [128, 128], F32); b_junk = S.newbuf()
                        ostB = [sb(st_b, f"ostB{i}", [128, 512], BF16) for i in range(2)]; b_ostB = [S.newbuf() for _ in range(2)]
                        sm = sb(st_b, "smB", [128, 32], F32); b_sm = S.newbuf()
                        for par in range(2):
                            S.op("pool", MEMSET(vB[par][:, :, 128:130], 1.0), [], b_vB[par])
                        rnd = 0
                        for h in range(4):
                            par = h % 2
                            qk, bqk, vb_, bvb = qkT[par], b_qk[par], vB[par], b_vB[par]
                            S.dma("sp", ("strip", par), [(strips[par][:], strip_d[h])], writes=[b_strip[par]])
                            ws, wb = w_get(("qkb", li, h))
                            wv = wview(ws, 8, 256)
                            run_inproj([((lambda c, wv=wv, m=m: wv[:, c, m * 128:(m + 1) * 128]), G, wb, pc(li, C_GQB + m),
                                         qk[:, m, G * 512:(G + 1) * 512], bqk[m][G])
                                        for m in range(2) for G in range(4)])
                            ws, wb = w_get(("vb", li, h))
                            wv = wview(ws, 8, 128)
                            for t4 in range(4):
                                pv, bpv = ps_next()
                                for tq in range(4):
                                    tt = t4 * 4 + tq
                                    S.op("pe", [MM(pv[:, tq * 128:(tq + 1) * 128], hT[:, c, tt * 128:(tt + 1) * 128], wv[:, c, :],
                                                   start=(c == 0), stop=(c == 7)) for c in range(8)],
                                         [wb] + [bh[c][t4] for c in range(8)], [bpv])
                                S.op("act", ACTF(vb_[:, t4 * 4:(t4 + 1) * 4, 0:128], pv[:].rearrange("p (a e) -> p a e", e=128), AF.Copy),
                                     [bpv], bvb[t4 * 4:(t4 + 1) * 4])
                            def evac_round(G, h=h):
                                S.op("act", ACTF(accS[:, 0:3, :], pbank[0][:, 0:390].rearrange("p (a e) -> p a e", e=130), AF.Copy), [bbank[0]], [b_accS])
                                S.op("dve", CP(accS[:, 3:6, :], pbank[1][:, 0:390].rearrange("p (a e) -> p a e", e=130)), [bbank[1], b_accS], [b_accS])
                                S.op("act", ACTF(accS[:, 6:8, :], pbank[2][:, 0:260].rearrange("p (a e) -> p a e", e=130), AF.Copy), [bbank[2], b_accS], [b_accS])
                                ost, bost = ostB[rot['rndB'] % 2], b_ostB[rot['rndB'] % 2]
                                rot['rndB'] += 1

                                def p1():
                                    S.op("dve", RECIP(sm[:, 0:8], accS[:, :, 128]), [b_accS, b_sm], [b_sm])
                                    S.op("dve", TS(sm[:, 8:12], sm[:, 4:8], small[:, 4:5], ALU.mult), [b_sm, b_small], [b_sm])

                                def p2():
                                    for b in range(4):
                                        S.op("act", ACTF(o32[:, b, :], accS[:, b, 0:128], AF.Identity, scale=sm[:, b:b + 1]), [b_accS, b_sm, b_o32], [b_o32])

                                def p3():
                                    for b in range(4):
                                        S.op("dve", STT(o32[:, b, :], accS[:, 4 + b, 0:128], sm[:, 8 + b:9 + b], o32[:, b, :], ALU.mult, ALU.add),
                                             [b_accS, b_sm, b_o32], [b_o32])

                                def p4():
                                    for b in range(4):
                                        S.op("act", ACTF(junk[:], o32[:, b, :], AF.Square, accum_out=sm[:, 12 + b:13 + b]), [b_o32, b_junk, b_sm], [b_junk, b_sm])
                                    S.op("act", ACTF(sm[:, 16:20], sm[:, 12:16], AF.Ln, bias=epsc[:, 0:1], scale=1.0 / 128), [b_sm, b_eps], [b_sm])
                                    S.op("act", ACTF(sm[:, 16:20], sm[:, 16:20], AF.Exp, scale=-0.5), [b_sm], [b_sm])

                                def p5():
                                    for b in range(4):
                                        S.op("dve", STT(ost[:, b * 128:(b + 1) * 128], o32[:, b, :], sm[:, 16 + b:17 + b], G2[:], ALU.mult, ALU.mult),
                                             [b_o32, b_sm, b_G2], [bost])

                                def p6():
                                    ptr, bptr = ps_next()
                                    S.op("pe", [MM(ptr[:, b * 128:(b + 1) * 128], ost[:, b * 128:(b + 1) * 128], ident[:]) for b in range(4)],
                                         [bost, b_id], [bptr])
                                    S.op("act", ACTF(oT[:, 4 + h, G * 512:(G + 1) * 512], ptr[:], AF.Copy), [bptr], [boT[4 + h][G]])

                                dq.extend([p1, p2, p3, p4, p5, p6])

                            steps = [(G, cm, j) for G in range(4) for cm in range(2) for j in range(16)]
                            LA = 2

                            def emit_qk(s_, qk=qk, bqk=bqk):
                                G, cm, j = steps[s_]
                                pS, bpS = ps_next()
                                S.op("pe", MM(pS[:], qk[cm * 64:(cm + 1) * 64, 1, j * 128:(j + 1) * 128],
                                              qk[cm * 64:(cm + 1) * 64, 0, G * 512:(G + 1) * 512]),
                                     [bqk[1][j // 4], bqk[0][G]], [bpS])
                                return pS, bpS

                            pend = {}
                            for s_ in range(LA):
                                pend[s_] = emit_qk(s_)
                            started = set()
                            for s_, (G, cm, j) in enumerate(steps):
                                if s_ + LA < len(steps):
                                    pend[s_ + LA] = emit_qk(s_ + LA)
                                pS, bpS = pend.pop(s_)
                                if cm == 0 and j == 0:
                                    started = set()
                                dl = j - 4 * G
                                pT, bpT = pT_next()
                                if -1 <= dl <= 4:
                                    sbt, bsbt = sbt_next()
                                    S.op("dve", STT(sbt[:], pS[:], 0.125, strips[par][:, (4 - dl) * 128:(4 - dl) * 128 + 512], ALU.mult, ALU.add),
                                         [bpS, b_strip[par]], [bsbt])
                                    S.op("act", ACTF(pT[:], sbt[:], AF.Exp), [bsbt], [bpT])
                                else:
                                    col = h if dl < 0 else 4 + h
                                    S.op("act", ACTF(pT[:], pS[:], AF.Exp, scale=0.125, bias=cfar[:, col:col + 1]), [bpS, b_cfar], [bpT])
                                fns = []
                                touched = []
                                for b in range(4):
                                    a = cm * 4 + b
                                    bank, off = a // 3, (a % 3) * 130
                                    fns.append(MM(pbank[bank][:, off:off + 130], pT[:, b * 128:(b + 1) * 128], vb_[:, j, :],
                                                  start=(bank not in started), stop=(j == 15)))
                                    started.add(bank)
                                    if bbank[bank] not in touched:
                                        touched.append(bbank[bank])
                                S.op("pe", fns, [bpT, bvb[j]], touched)
                                if s_ % 3 == 2:
                                    drain(1)
                                if cm == 1 and j == 15:
                                    assert not dq
                                    evac_round(G)
                        drain(len(dq))
                    S.mark_barrier()
                dump("oT", oT[:], [b for r in boT for b in r])
                chk("B")
                with contextlib.ExitStack() as st_m:
                    S.mark_barrier()
                    mixT = sb(st_m, "mixT", [128, 8, SEQ], BF16); b_mix = [[S.newbuf() for _ in range(4)] for _ in range(8)]
                    gts = [sb(st_m, f"gt{i}", [128, 512], F32) for i in range(4)]; b_gts = [S.newbuf() for _ in range(4)]
                    m1s = [sb(st_m, f"m1{i}", [128, 512], F32) for i in range(4)]; b_m1s = [S.newbuf() for _ in range(4)]
                    rr = 0
                    for ft in range(8):
                        wsg, wbg = w_get(("gate", li, ft))
                        wvg = wview(wsg, 8, 256)
                        wsp, wbp = w_get(("proj", li, ft))
                        wvp = wsp[:, 0:1024].rearrange("p (m c n) -> p m c n", m=2, c=4)
                        for G in range(4):
                            ts = slice(G * 512, (G + 1) * 512)
                            rf, rb = hT_rhs(G)
                            res = []
                            for m in range(2):
                                pg, bpg = proj_fm(lambda c: wvg[:, c, m * 128:(m + 1) * 128], G, 8, rf, rb, wbg)
                                gt, bgt = gts[rr % 4], b_gts[rr % 4]
                                S.op("act", ACTF(gt[:], pg[:], AF.Sigmoid, bias=pc(li, C_BG + m * 8 + ft)), [bpg, b_pc], [bgt])
                                dump("gt0", gt[:], [bgt])
                                pp, bpp = proj_fm(lambda c: wvp[:, m, c, :], G, 4, lambda c: oT[:, m * 4 + c, ts],
                                                  [boT[m * 4 + c][G] for c in range(4)], wbp)
                                m1, bm1 = m1s[rr % 4], b_m1s[rr % 4]
                                rr += 1
                                S.op("dve", TT(m1[:], pp[:], gt[:], ALU.mult), [bpp, bgt], [bm1])
                                res.append((m1, bm1))
                            S.op("dve", TT(mixT[:, ft, ts], res[0][0][:], res[1][0][:], ALU.add), [res[0][1], res[1][1]], [b_mix[ft][G]])
                    dump("mixT", mixT[:], [b for r in b_mix for b in r])
                    for f2 in range(4):
                        ws, wb = w_get(("wo", li, f2))
                        wv = wview(ws, 8, 256)
                        for t2 in range(2):
                            fo = f2 * 2 + t2
                            for G in range(4):
                                ts = slice(G * 512, (G + 1) * 512)
                                po, bpo = proj_fm(lambda c: wv[:, c, t2 * 128:(t2 + 1) * 128], G, 8, lambda c: mixT[:, c, ts],
                                                  [b_mix[c][G] for c in range(8)], wb)
                                S.op("dve", TT(xT[:, fo, ts], po[:], xT[:, fo, ts], ALU.add), [bpo, bx[fo][G]], [bx[fo][G]])
                S.mark_barrier()
            S.mark_barrier()

            dump("xmid", xT[:], [b for r in bx for b in r])
            chk("M")
            rmsnorm_to_hT(li, C_LN2)
            with contextlib.ExitStack() as st_f:
                S.mark_barrier()
                actT = sb(st_f, "actT", [128, 11, SEQ], BF16); b_act = [[S.newbuf() for _ in range(4)] for _ in range(11)]
                raws = [sb(st_f, f"raw{i}", [128, SEQ + 2], F32) for i in range(2)]; b_raws = [S.newbuf() for _ in range(2)]
                us = [sb(st_f, f"u{i}", [128, SEQ], F32) for i in range(2)]; b_us = [S.newbuf() for _ in range(2)]
                sg = sb(st_f, "sg", [128, SEQ], BF16); b_sg = S.newbuf()
                for i2 in range(2):
                    S.op("pool", MEMSET(raws[i2][:, 0:1], 0.0), [], [b_raws[i2]])
                    S.op("pool", MEMSET(raws[i2][:, SEQ + 1:SEQ + 2], 0.0), [b_raws[i2]], [b_raws[i2]])
                tcount = 0
                pending = [None]

                def make_conv(raw, braw, u, bu, ct, which, i):
                    def conv():
                        S.op("dve", STT(u[:], raw[:, 0:SEQ], pc(li, C_CW + ct), u[:], ALU.mult, ALU.add), [braw, bu, b_pc], [bu])
                        S.op("dve", STT(u[:], raw[:, 2:SEQ + 2], pc(li, C_CW + 88 + ct), u[:], ALU.mult, ALU.add), [braw, bu, b_pc], [bu])
                        if which == 1:
                            S.op("act", ACTF(sg[:], u[:], AF.Silu), [bu, b_sg], [b_sg])
                        else:
                            S.op("pool", TT(actT[:, i, :], sg[:], u[:], ALU.mult), [b_sg, bu], b_act[i])
                    return conv

                for half in range(2):
                    for i in range(11):
                        p = half * 11 + i
                        ws, wb = w_get(("up", li, p))
                        wv = wview(ws, 8, 256)
                        for which in (1, 0):
                            ct = p + 22 * which
                            raw, braw = raws[tcount % 2], b_raws[tcount % 2]
                            u, bu = us[tcount % 2], b_us[tcount % 2]
                            tcount += 1
                            for G in range(4):
                                rf, rb = hT_rhs(G)
                                pu, bpu = proj_fm(lambda c: wv[:, c, which * 128:(which + 1) * 128], G, 8, rf, rb, wb)
                                S.op("act", ACTF(raw[:, 1 + G * 512:1 + (G + 1) * 512], pu[:], AF.Copy), [bpu, braw], [braw])
                                S.op("act", ACTF(u[:, G * 512:(G + 1) * 512], pu[:], AF.Identity, scale=pc(li, C_CW + 44 + ct), bias=pc(li, C_CB + ct)),
                                     [bpu, b_pc, bu], [bu])
                            if pending[0] is not None:
                                pending[0]()
                            pending[0] = make_conv(raw, braw, u, bu, ct, which, i)
                    if pending[0] is not None:
                        pending[0]()
                        pending[0] = None
                    for fo in range(8):
                        ws, wb = w_get(("down", li, half, fo))
                        wv = wview(ws, 11, 128)
                        for G in range(4):
                            ts = slice(G * 512, (G + 1) * 512)
                            po, bpo = proj_fm(lambda k: wv[:, k, :], G, 11, lambda k: actT[:, k, ts], [b_act[k][G] for k in range(11)], wb)
                            S.op("dve", TT(xT[:, fo, ts], po[:], xT[:, fo, ts], ALU.add), [bpo, bx[fo][G]], [bx[fo][G]])
            S.mark_barrier()

        S.enabled = True
        allx = [b for row in bx for b in row]
        for c in range(8):
            S.dma("sp", ("out", c), [(out_d[c * 128:(c + 1) * 128, :], xT[:, c, :])], reads=bx[c])
        S.wait_all("sp", allx)
        assert stop is not None or wst["idx"] == len(plan)
        S.emit()
    return nc


def _t5_bucket_np(rel):
    rel = np.asarray(rel, np.int64)
    half, max_exact = 16, 8
    ret = np.where(rel > 0, half, 0)
    n = np.abs(rel)
    nf = np.maximum(n, 1).astype(np.float32)
    large = max_exact + (np.log(nf / np.float32(max_exact)) / np.float32(math.log(128 / max_exact))
                         * np.float32(half - max_exact)).astype(np.int32)
    large = np.minimum(large, half - 1)
    return ret + np.where(n < max_exact, n, large)


def _host_layout(inp, layers):
    f32 = np.float32
    L = list(layers)
    g = lambda k: np.asarray(inp[k], f32)
    pcols = np.zeros((128, len(L) * NCOL), f32)
    brow = np.zeros((len(L), NBROW), f32)
    for li, l in enumerate(L):
        pcl = np.zeros((128, NCOL), f32)
        pcl[:, C_LN1:C_LN1 + 8] = g("ln1_g")[l].reshape(8, 128).T
        pcl[:, C_LN2:C_LN2 + 8] = g("ln2_g")[l].reshape(8, 128).T
        pcl[:, C_GQA] = np.tile(g("qn_a")[l], 2)
        pcl[:, C_GKA] = np.tile(g("kn_a")[l], 2)
        pcl[:, C_GQB] = np.tile(g("qn_b")[l], 2)
        pcl[:, C_GKB] = np.tile(g("kn_b")[l], 2)
        pcl[:, C_BG:C_BG + 16] = g("b_gate")[l].reshape(16, 128).T
        pcl[:, C_CW:C_CW + 132] = g("conv_w")[l].reshape(3, 44, 128).transpose(2, 0, 1).reshape(128, 132)
        pcl[:, C_CB:C_CB + 44] = g("conv_b")[l].reshape(44, 128).T
        pcols[:, li * NCOL:(li + 1) * NCOL] = pcl
        brow[li] = np.concatenate([g("subln_g")[l], g("sink")[l][SINK_PERM], g("lam_q1")[l], g("lam_k1")[l],
                                   g("lam_q2")[l], g("lam_k2")[l]])
    tab = g("rel_bias")
    cfar = np.concatenate([tab[15, 8:12], tab[31, 8:12]]).astype(f32)
    k = np.arange(128)[:, None]
    c = np.arange(1152)[None, :]
    bk = _t5_bucket_np(k - c + 512)
    stripB = np.stack([tab[bk, 8 + h] for h in range(4)]).astype(f32)
    q = np.arange(128)[None, :]
    biasA = np.zeros((128, 2, 3, 4, 128), f32)
    for di in range(3):
        rel = k + 128 * (di - 1) - q
        bkt = _t5_bucket_np(rel)
        ok = np.abs(rel) <= 128
        for kap in range(2):
            for cb in range(4):
                h = 4 * kap + 2 * (cb % 2) + cb // 2
                biasA[:, kap, di, cb, :] = np.where(ok, tab[bkt, h], f32(MASK_NEG))
    bones = np.zeros((128, 128), f32)
    bones[:64, :64] = 1.0 / 64
    bones[64:, 64:] = 1.0 / 64
    common = {
        "w_in": np.ascontiguousarray(g("w_in")[L]), "w_gate": np.ascontiguousarray(g("w_gate")[L]),
        "w_a_proj": np.ascontiguousarray(g("w_a_proj")[L]), "w_b_proj": np.ascontiguousarray(g("w_b_proj")[L]),
        "w_o": np.ascontiguousarray(g("w_o")[L]), "w_up": np.ascontiguousarray(g("w_up")[L]),
        "w_down": np.ascontiguousarray(g("w_down")[L]),
        "pcols": pcols, "brow": brow, "cfar": cfar, "stripB": stripB,
        "biasA": np.ascontiguousarray(biasA.reshape(128, 2 * 3 * 512)),
        "ident": np.eye(128, dtype=f32), "bones": bones,
    }
    return common


_PROGS = {}


def _run(layers, xT_list, inp):
    key = tuple(layers)
    if key not in _PROGS:
        _PROGS[key] = build_program(list(layers))
    nc = _PROGS[key]
    common = _host_layout(inp, layers)
    in_maps = [dict(common, xT=xT_list[b]) for b in range(NCORES)]
    res = run_bass_kernel_spmd(nc, in_maps, core_ids=list(range(NCORES)))
    return [np.asarray(r["outT"]) for r in res.results]


FUSED = True


def kernel(**inputs):
    x = np.asarray(inputs["x"], np.float32)
    xT = [np.ascontiguousarray(x[b].T) for b in range(NCORES)]
    if FUSED:
        xT = _run(range(DEPTH), xT, inputs)
    else:
        for l in range(DEPTH):
            xT = _run([l], xT, inputs)
    return np.stack([t.T for t in xT]).astype(np.float32)
```

```python
import contextlib
import math
import numpy as np
import concourse.bass as bass
import concourse.mybir as mybir
from concourse.bass_utils import run_bass_kernel_spmd

F32 = mybir.dt.float32
BF16 = mybir.dt.bfloat16
AF = mybir.ActivationFunctionType
ALU = mybir.AluOpType
AX = mybir.AxisListType

D = 1024
SEQ = 2048
DEPTH = 4
NCORES = 8
D_FF = 2816
IN_W = 2304
EPS = 1e-6
NCOL = 212
C_LN1, C_LN2, C_GQA, C_GKA, C_GQB, C_GKB, C_BG, C_CW, C_CB = 0, 8, 16, 17, 18, 19, 20, 36, 168
NBROW = 392
SINK_PERM = [0, 2, 1, 3, 4, 6, 5, 7]
MASK_NEG = -30000.0

EPOCH = 2000
ENGS = ("pe", "act", "dve", "pool", "sp")


class Buf:
    __slots__ = ("w", "r", "excl")

    def __init__(self, r=None):
        self.excl = False
        self.w = None
        self.r = dict(r) if r else {}


class Sched:
    def __init__(self, nc, stack):
        self.nc = nc
        self.stack = stack
        self.ops = {e: [] for e in ENGS}
        self.seq = {e: 0 for e in ENGS}
        self.sems = {}
        self.waited = {e: {} for e in ENGS}
        self.dma_cnt = {}
        self.barrier = {}
        self.enabled = True

    def sem(self, key):
        s = self.sems.get(key)
        if s is None:
            s = self.stack.enter_context(self.nc.semaphore("s_" + "_".join(str(k) for k in key)))
            self.sems[key] = s
        return s

    def newbuf(self):
        return Buf(self.barrier)

    def mark_barrier(self):
        b = {}
        for e in ENGS:
            n = self.seq[e]
            if n > 0:
                b[(e, (n - 1) // EPOCH)] = (n - 1) % EPOCH + 1
        for k, c in self.dma_cnt.items():
            b[k] = 16 * c
        self.barrier = b

    def _deps(self, eng, reads, writes):
        deps = {}
        for b in reads:
            if b.w is not None:
                k, v = b.w
                if deps.get(k, 0) < v:
                    deps[k] = v
        for b in writes:
            if b.w is not None:
                k, v = b.w
                if deps.get(k, 0) < v:
                    deps[k] = v
            for k, v in b.r.items():
                if deps.get(k, 0) < v:
                    deps[k] = v
        out = []
        wd = self.waited[eng]
        for k, v in deps.items():
            if eng == "pe" and k[0] == "pe":
                continue
            if wd.get(k, 0) >= v:
                continue
            wd[k] = v
            out.append((self.sem(k), v))
        return out

    def _commit(self, tok, reads, writes):
        k, v = tok
        for b in writes:
            b.w = tok
            b.r = {}
        for b in reads:
            if b.r.get(k, 0) < v:
                b.r[k] = v

    def op(self, eng, fns, reads=(), writes=()):
        if not self.enabled:
            return
        if not isinstance(fns, (list, tuple)):
            fns = [fns]
        if any(b.excl for b in reads):
            writes = list(writes) + [b for b in reads if b.excl]
            reads = [b for b in reads if not b.excl]
        waits = self._deps(eng, reads, writes)
        n = self.seq[eng]
        self.seq[eng] = n + 1
        key = (eng, n // EPOCH)
        tok = (key, n % EPOCH + 1)
        self.ops[eng].append((waits, fns, self.sem(key), 1))
        self._commit(tok, reads, writes)

    def dma(self, eng, slot, pairs, reads=(), writes=(), **kw):
        if not self.enabled:
            return
        waits = self._deps(eng, reads, writes)
        key = ("dma", slot)
        sem = self.sem(key)
        c = self.dma_cnt.get(key, 0)
        first = True
        for (o, i) in pairs:
            c += 1
            fn = (lambda e, o=o, i=i: e.dma_start(out=o, in_=i, **kw))
            self.ops[eng].append((waits if first else [], [fn], sem, 16))
            first = False
        self.dma_cnt[key] = c
        self._commit((key, 16 * c), reads, writes)

    def wait_all(self, eng, bufs):
        waits = self._deps(eng, bufs, bufs)
        self.ops[eng].append((waits, [], None, 0))

    def emit(self):
        objs = {"pe": "tensor", "act": "scalar", "dve": "vector", "pool": "gpsimd", "sp": "sync"}
        with self.nc.Block() as block:
            for e in ENGS:
                lst = self.ops[e]
                if not lst:
                    continue

                def body(engobj, lst=lst):
                    for waits, fns, sem, inc in lst:
                        for s, v in waits:
                            engobj.wait_ge(s, v)
                        for fn in fns[:-1]:
                            fn(engobj)
                        if fns:
                            fns[-1](engobj).then_inc(sem, inc)

                getattr(block, objs[e])(body)


def MM(out, lhsT, rhs, start=True, stop=True):
    return lambda e: e.matmul(out, lhsT=lhsT, rhs=rhs, start=start, stop=stop, skip_group_check=True)


def ACTF(out, in_, func, **kw):
    return lambda e: e.activation(out=out, in_=in_, func=func, **kw)


def STT(out, in0, scalar, in1, op0, op1):
    return lambda e: e.scalar_tensor_tensor(out=out, in0=in0, scalar=scalar, in1=in1, op0=op0, op1=op1)


def TT(out, in0, in1, op):
    return lambda e: e.tensor_tensor(out=out, in0=in0, in1=in1, op=op)


def TS(out, in0, s1, op0, s2=None, op1=None):
    if op1 is None:
        return lambda e: e.tensor_scalar(out=out, in0=in0, scalar1=s1, scalar2=None, op0=op0)
    return lambda e: e.tensor_scalar(out=out, in0=in0, scalar1=s1, scalar2=s2, op0=op0, op1=op1)


def CP(out, in_):
    return lambda e: e.tensor_copy(out=out, in_=in_)


def RECIP(out, in_):
    return lambda e: e.reciprocal(out=out, in_=in_)


def MEMSET(ap, v):
    return lambda e: e.memset(ap, v)


class _Stop(Exception):
    pass


def build_program(layers, stop=None, dumps=()):
    NL = len(layers)
    dbg_outs = {}
    nc = bass.Bass("TRN2", target_bir_lowering=False)

    def din(name, shape):
        return nc.dram_tensor(name, list(shape), F32, kind="ExternalInput").ap()

    xT_d = din("xT", [D, SEQ])
    w_in_d = din("w_in", [NL, D, IN_W])
    w_gate_d = din("w_gate", [NL, D, 2 * D])
    w_a_d = din("w_a_proj", [NL, 512, D])
    w_b_d = din("w_b_proj", [NL, 512, D])
    w_o_d = din("w_o", [NL, D, D])
    w_up_d = din("w_up", [NL, D, 2 * D_FF])
    w_dn_d = din("w_down", [NL, D_FF, D])
    pcols_d = din("pcols", [128, NL * NCOL])
    brow_d = din("brow", [NL, NBROW])
    cfar_d = din("cfar", [8])
    strip_d = din("stripB", [4, 128, 1152])
    biasA_d = din("biasA", [128, 2 * 3 * 512])
    ident_d = din("ident", [128, 128])
    bones_d = din("bones", [128, 128])
    out_d = nc.dram_tensor("outT", [D, SEQ], F32, kind="ExternalOutput").ap()

    with contextlib.ExitStack() as st:
        S = Sched(nc, st)

        uid = [0]

        def sb(stack, name, shape, dt):
            uid[0] += 1
            return stack.enter_context(nc.sbuf_tensor(f"{name}_{uid[0]}", list(shape), dt))

        chk_cnt = {}

        def chk(name):
            chk_cnt[name] = chk_cnt.get(name, 0) + 1
            if stop is None:
                return
            nm, _, n = stop.partition("#")
            if nm == name and chk_cnt[name] == int(n or 1):
                S.enabled = False

        def dump(name, ap, bufs):
            if name not in dumps or name in dbg_outs:
                return
            shp = list(ap.shape)
            dt_ = nc.dram_tensor("dbg_" + name, shp, ap.dtype, kind="ExternalOutput").ap()
            dbg_outs[name] = dt_
            S.dma("sp", ("dbg", name), [(dt_, ap)], reads=bufs)
            S.wait_all("sp", bufs)

        xT = sb(st, "xT_s", [128, 8, SEQ], F32)
        bx = [[S.newbuf() for _ in range(4)] for _ in range(8)]
        hT = sb(st, "hT_s", [128, 8, SEQ], BF16)
        bh = [[S.newbuf() for _ in range(4)] for _ in range(8)]
        pcols = sb(st, "pcols_s", [128, NL * NCOL], F32); b_pc = S.newbuf()
        brow = sb(st, "brow_s", [128, NBROW], F32); b_brow = S.newbuf()
        cfar = sb(st, "cfar_s", [128, 8], F32); b_cfar = S.newbuf()
        identf = sb(st, "identf", [128, 128], F32); b_idf = S.newbuf()
        bonesf = sb(st, "bonesf", [128, 128], F32); b_bof = S.newbuf()
        ident = sb(st, "ident_b", [128, 128], BF16); b_id = S.newbuf()
        bones = sb(st, "bones_b", [128, 128], BF16); b_bo = S.newbuf()
        ones = sb(st, "ones_b", [128, 128], BF16); b_ones = S.newbuf()
        epsc = sb(st, "epsc", [128, 1], F32); b_eps = S.newbuf()
        small = sb(st, "small", [128, 32], F32); b_small = S.newbuf()
        esink = sb(st, "esink", [128, 8], F32); b_esink = S.newbuf()
        G2 = sb(st, "G2", [128, 128], F32); b_G2 = S.newbuf()
        lamt = sb(st, "lamt", [128, 64], F32); b_lamt = S.newbuf()
        NWS = 4
        wslots = [sb(st, f"wslot{i}", [128, 2048], BF16) for i in range(NWS)]
        wbufs = [S.newbuf() for _ in range(NWS)]
        sqs = [sb(st, f"sq{i}", [128, 512], BF16) for i in range(2)]; b_sqs = [S.newbuf() for _ in range(2)]
        rts = [sb(st, f"rt{i}", [128, 512], F32) for i in range(2)]; b_rts = [S.newbuf() for _ in range(2)]

        pbank = [st.enter_context(nc.psum_tensor(f"pb{i}", [128, 512], F32)) for i in range(8)]
        bbank = [S.newbuf() for _ in range(8)]
        for b_ in bbank:
            b_.excl = True
        rot = {"ps": 0, "sq": 0, "rt": 0, "accA": 0, "rndB": 0}

        def ps_next():
            i = 3 + rot["ps"] % 5
            rot["ps"] += 1
            return pbank[i], bbank[i]

        def sq_next():
            i = rot["sq"] % 2
            rot["sq"] += 1
            return sqs[i], b_sqs[i]

        def rt_next():
            i = rot["rt"] % 2
            rot["rt"] += 1
            return rts[i], b_rts[i]

        def wview(slot, a, b):
            return slot[:, 0:a * b].rearrange("p (a b) -> p a b", a=a)

        def plan_layer(li):
            P = []

            def incols(c0, n):
                return w_in_d[li, :, c0:c0 + n].rearrange("(c p) n -> p c n", p=128)

            P.append((("qa", li, 0), lambda s: [(wview(s, 8, 256), incols(0, 256))]))
            P.append((("qa", li, 1), lambda s: [(wview(s, 8, 256), incols(256, 256))]))

            def kdup(s):
                v = s[:, 0:2048].rearrange("p (c k d e) -> p c k d e", c=8, k=2, d=2)
                return [(v[:, :, k, dd, :], incols(512 + k * 64, 64)) for k in range(2) for dd in range(2)]

            P.append((("ka", li), kdup))
            P.append((("va", li), lambda s: [(wview(s, 8, 128), incols(640, 128))]))
            for h in range(4):
                def qk(s, h=h):
                    v = wview(s, 8, 256)
                    return [(v[:, :, 0:128], incols(768 + h * 128, 128)),
                            (v[:, :, 128:256], incols(1280 + h * 128, 128))]
                P.append((("qkb", li, h), qk))
                P.append((("vb", li, h), lambda s, h=h: [(wview(s, 8, 128), incols(1792 + h * 128, 128))]))
            for ft in range(8):
                def gate(s, ft=ft):
                    v = wview(s, 8, 256)
                    g = lambda c0: w_gate_d[li, :, c0:c0 + 128].rearrange("(c p) n -> p c n", p=128)
                    return [(v[:, :, 0:128], g(ft * 128)), (v[:, :, 128:256], g(D + ft * 128))]
                P.append((("gate", li, ft), gate))

                def proj(s, ft=ft):
                    v = s[:, 0:1024].rearrange("p (m c n) -> p m c n", m=2, c=4)
                    a = w_a_d[li, :, ft * 128:(ft + 1) * 128].rearrange("(c p) n -> p c n", p=128)
                    b = w_b_d[li, :, ft * 128:(ft + 1) * 128].rearrange("(c p) n -> p c n", p=128)
                    return [(v[:, 0], a), (v[:, 1], b)]
                P.append((("proj", li, ft), proj))
            for f2 in range(4):
                P.append((("wo", li, f2), lambda s, f2=f2: [(wview(s, 8, 256), w_o_d[li, :, f2 * 256:(f2 + 1) * 256].rearrange("(c p) n -> p c n", p=128))]))
            for half in range(2):
                for i in range(11):
                    p = half * 11 + i

                    def up(s, p=p):
                        v = wview(s, 8, 256)
                        u = lambda c0: w_up_d[li, :, c0:c0 + 128].rearrange("(c p) n -> p c n", p=128)
                        return [(v[:, :, 0:128], u(p * 128)), (v[:, :, 128:256], u(D_FF + p * 128))]
                    P.append((("up", li, p), up))
                for fo in range(8):
                    def down(s, half=half, fo=fo):
                        v = wview(s, 11, 128)
                        src = w_dn_d[li, half * 1408:(half + 1) * 1408, fo * 128:(fo + 1) * 128].rearrange("(k p) n -> p k n", p=128)
                        return [(v, src)]
                    P.append((("down", li, half, fo), down))
            return P

        plan = []
        for li in range(NL):
            plan += plan_layer(li)
        wst = {"idx": 0, "issued": 0}
        AHEAD = NWS - 2

        def w_issue_upto(k):
            while wst["issued"] <= min(k, len(plan) - 1):
                t = wst["issued"]
                slot = t % NWS
                S.dma("pool", ("w", slot, plan[t][0][1]), plan[t][1](wslots[slot]), writes=[wbufs[slot]])
                wst["issued"] += 1

        def w_get(key):
            t = wst["idx"]
            assert plan[t][0] == key, (plan[t][0], key)
            w_issue_upto(t + AHEAD)
            wst["idx"] += 1
            return wslots[t % NWS], wbufs[t % NWS]

        for c in range(8):
            S.dma("sp", ("x", c), [(xT[:, c, :], xT_d[c * 128:(c + 1) * 128, :])], writes=bx[c])
        S.dma("sp", "c0", [(pcols[:], pcols_d)], writes=[b_pc])
        S.dma("sp", "c1", [(cfar[:], cfar_d.partition_broadcast(128))], writes=[b_cfar])
        S.dma("sp", "c2", [(identf[:], ident_d)], writes=[b_idf])
        S.dma("sp", "c3", [(bonesf[:], bones_d)], writes=[b_bof])
        S.op("dve", CP(ident[:], identf[:]), [b_idf], [b_id])
        S.op("dve", CP(bones[:], bonesf[:]), [b_bof], [b_bo])
        S.op("pool", MEMSET(ones[:], 1.0 / D), [], [b_ones])
        S.op("pool", MEMSET(epsc[:], EPS), [], [b_eps])
        w_issue_upto(AHEAD - 1)

        def pc(li, col, n=1):
            return pcols[:, li * NCOL + col: li * NCOL + col + n]

        def rstd_from_ms(pm, bpm, scale=1.0):
            rt, brt = rt_next()
            S.op("act", ACTF(rt[:], pm, AF.Ln, bias=epsc[:, 0:1], scale=scale), [bpm, b_eps], [brt])
            S.op("act", ACTF(rt[:], rt[:], AF.Exp, scale=-0.5), [brt], [brt])
            return rt, brt

        def rmsnorm_to_hT(li, gcol0):
            for G in range(4):
                ts = slice(G * 512, (G + 1) * 512)
                pm, bpm = ps_next()
                for c in range(8):
                    sq, bsq = sq_next()
                    S.op("act", ACTF(sq[:], xT[:, c, ts], AF.Square), [bx[c][G]], [bsq])
                    S.op("pe", MM(pm[:], ones[:], sq[:], start=(c == 0), stop=(c == 7)), [b_ones, bsq], [bpm])
                rt, brt = rstd_from_ms(pm[:], bpm)
                for c in range(8):
                    S.op("dve", STT(hT[:, c, ts], xT[:, c, ts], pc(li, gcol0 + c), rt[:], ALU.mult, ALU.mult),
                         [bx[c][G], b_pc, brt], [bh[c][G]])

        def proj_fm(wap_fn, G, kc, rhs_fn, rhs_bufs, wb):
            pz, bpz = ps_next()
            S.op("pe", [MM(pz[:], wap_fn(c), rhs_fn(c), start=(c == 0), stop=(c == kc - 1)) for c in range(kc)],
                 [wb] + rhs_bufs, [bpz])
            return pz, bpz

        def norm64_store(pz, bpz, gcol, dst, bdst):
            sq, bsq = sq_next()
            S.op("act", ACTF(sq[:], pz[:], AF.Square), [bpz], [bsq])
            pm, bpm = ps_next()
            S.op("pe", MM(pm[:], bones[:], sq[:]), [b_bo, bsq], [bpm])
            rt, brt = rstd_from_ms(pm[:], bpm)
            if isinstance(dst, tuple):
                for hf in range(2):
                    ps_ = slice(hf * 64, (hf + 1) * 64)
                    S.op("dve", STT(dst[hf], pz[ps_, :], gcol[ps_, :], rt[ps_, :], ALU.mult, ALU.mult), [bpz, b_pc, brt], [bdst])
            else:
                S.op("dve", STT(dst, pz[:], gcol, rt[:], ALU.mult, ALU.mult), [bpz, b_pc, brt], [bdst])

        def hT_rhs(G):
            return (lambda c: hT[:, c, G * 512:(G + 1) * 512]), [bh[c][G] for c in range(8)]

        dq = []

        def drain(n=1):
            for _ in range(n):
                if dq:
                    dq.pop(0)()

        def run_inproj(jobs):
            prev = None
            for (wfn, G, wb_, gcol, dst, bdst) in jobs:
                rf, rb = hT_rhs(G)
                pz, bpz = proj_fm(wfn, G, 8, rf, rb, wb_)
                if prev is not None:
                    norm64_store(*prev)
                prev = (pz, bpz, gcol, dst, bdst)
            if prev is not None:
                norm64_store(*prev)

        for li, l in enumerate(layers):
            lam_init = 0.8 - 0.6 * math.exp(-0.3 * l)
            S.dma("sp", "brow", [(brow[:], brow_d[li].partition_broadcast(128))], writes=[b_brow])
            S.op("dve", TT(lamt[:], brow[:, 136:200], brow[:, 200:264], ALU.mult), [b_brow], [b_lamt])
            S.op("dve", lambda e: e.reduce_sum(out=small[:, 0:1], in_=lamt[:], axis=AX.X), [b_lamt], [b_small])
            S.op("dve", TT(lamt[:], brow[:, 264:328], brow[:, 328:392], ALU.mult), [b_brow, b_lamt], [b_lamt])
            S.op("dve", lambda e: e.reduce_sum(out=small[:, 1:2], in_=lamt[:], axis=AX.X), [b_lamt, b_small], [b_small])
            S.op("act", ACTF(small[:, 2:4], small[:, 0:2], AF.Exp), [b_small], [b_small])
            S.op("dve", TT(small[:, 4:5], small[:, 3:4], small[:, 2:3], ALU.subtract), [b_small], [b_small])
            S.op("dve", TS(small[:, 4:5], small[:, 4:5], -lam_init, ALU.add), [b_small], [b_small])
            S.op("act", ACTF(esink[:], brow[:, 128:136], AF.Exp), [b_brow], [b_esink])
            S.op("dve", TS(G2[:], brow[:, 0:128], 1.0 - lam_init, ALU.mult), [b_brow], [b_G2])

            rmsnorm_to_hT(li, C_LN1)
            dump("hT", hT[:], [b for r in bh for b in r])
            chk("N1")

            with contextlib.ExitStack() as st_att:
                S.mark_barrier()
                oT = sb(st_att, "oT", [128, 8, SEQ], BF16)
                boT = [[S.newbuf() for _ in range(4)] for _ in range(8)]
                with contextlib.ExitStack() as st_ab:
                    pTs = [sb(st_ab, f"pT{i}", [128, 512], BF16) for i in range(3)]; b_pTs = [S.newbuf() for _ in range(3)]
                    sbts = [sb(st_ab, f"sbt{i}", [128, 512], F32) for i in range(2)]; b_sbts = [S.newbuf() for _ in range(2)]
                    rot["pT"] = 0; rot["sbt"] = 0

                    def pT_next():
                        i = rot["pT"] % 3
                        rot["pT"] += 1
                        return pTs[i], b_pTs[i]

                    def sbt_next():
                        i = rot["sbt"] % 2
                        rot["sbt"] += 1
                        return sbts[i], b_sbts[i]

                    with contextlib.ExitStack() as st_a:
                        qaT = sb(st_a, "qaT", [128, 4, SEQ], BF16); b_qa = [[S.newbuf() for _ in range(4)] for _ in range(4)]
                        kaT = sb(st_a, "kaT", [128, 2, SEQ], BF16); b_ka = [[S.newbuf() for _ in range(4)] for _ in range(2)]
                        vaA = sb(st_a, "vaA", [128, 16, 2, 66], BF16); b_va = [S.newbuf() for _ in range(16)]
                        biasA = sb(st_a, "biasA_s", [128, 2, 3, 512], F32); b_biasA = S.newbuf()
                        ostA = [sb(st_a, f"ostA{i}", [128, 512], BF16) for i in range(2)]; b_ostA = [[S.newbuf() for _ in range(8)] for _ in range(2)]
                        r4s = sb(st_a, "r4s", [128, 2, 4], F32); b_r4 = [S.newbuf() for _ in range(2)]
                        S.dma("sp", "biasA", [(biasA[:].rearrange("p a b c -> p (a b c)"), biasA_d)], writes=[b_biasA])
                        S.op("pool", MEMSET(vaA[:, :, :, 64:66], 1.0), [], b_va)
                        for half2 in range(2):
                            ws, wb = w_get(("qa", li, half2))
                            wv = wview(ws, 8, 256)
                            run_inproj([((lambda c, wv=wv, t2=t2: wv[:, c, t2 * 128:(t2 + 1) * 128]), G, wb, pc(li, C_GQA),
                                         qaT[:, half2 * 2 + t2, G * 512:(G + 1) * 512], b_qa[half2 * 2 + t2][G])
                                        for t2 in range(2) for G in range(4)])
                        ws, wb = w_get(("ka", li))
                        wv = wview(ws, 8, 256)
                        run_inproj([((lambda c, wv=wv, kap=kap: wv[:, c, kap * 128:(kap + 1) * 128]), G, wb, pc(li, C_GKA),
                                     kaT[:, kap, G * 512:(G + 1) * 512], b_ka[kap][G])
                                    for kap in range(2) for G in range(4)])
                        ws, wb = w_get(("va", li))
                        wv = wview(ws, 8, 128)
                        for t4 in range(4):
                            pv, bpv = ps_next()
                            for tq in range(4):
                                tt = t4 * 4 + tq
                                S.op("pe", [MM(pv[:, tq * 128:(tq + 1) * 128], hT[:, c, tt * 128:(tt + 1) * 128], wv[:, c, :],
                                               start=(c == 0), stop=(c == 7)) for c in range(8)],
                                     [wb] + [bh[c][t4] for c in range(8)], [bpv])
                            S.op("act", ACTF(vaA[:, t4 * 4:(t4 + 1) * 4, :, 0:64].rearrange("p t k e -> p (t k) e"),
                                             pv[:].rearrange("p (a e) -> p a e", e=64), AF.Copy),
                                 [bpv], b_va[t4 * 4:(t4 + 1) * 4])
                        dump("qaT", qaT[:], [b for r in b_qa for b in r])
                        dump("kaT", kaT[:], [b for r in b_ka for b in r])
                        dump("vaA", vaA[:], b_va)
                        chk("Ain")
                        stepsA = []
                        for i in range(16):
                            for kap in range(2):
                                js = [j for j in (i - 1, i, i + 1) if 0 <= j < 16]
                                for jn, j in enumerate(js):
                                    stepsA.append((i, kap, jn, j, len(js)))

                        def emit_qk_a(s_):
                            i, kap, jn, j, nj = stepsA[s_]
                            out = []
                            for hf in range(2):
                                pS, bpS = ps_next()
                                S.op("pe", MM(pS[:, 0:256],
                                              kaT[hf * 64:(hf + 1) * 64, kap, j * 128:(j + 1) * 128],
                                              qaT[hf * 64:(hf + 1) * 64, 2 * kap:2 * kap + 2, i * 128:(i + 1) * 128]),
                                     [b_ka[kap][j // 4], b_qa[2 * kap][i // 4], b_qa[2 * kap + 1][i // 4]], [bpS])
                                out.append((pS, bpS))
                            return out

                        def evacA_1(acc, bacc, kap):
                            def f():
                                accv = acc[:, 0:264].rearrange("p (a e) -> p a e", e=66)
                                r4 = r4s[:, kap, :]
                                S.op("dve", TT(r4, accv[:, :, 64], esink[:, kap * 4:(kap + 1) * 4], ALU.add), [bacc, b_esink], [b_r4[kap]])
                                S.op("dve", RECIP(r4, r4), [b_r4[kap]], [b_r4[kap]])
                            return f

                        def evacA_2(acc, bacc, kap, ost, bost):
                            def f():
                                accv = acc[:, 0:264].rearrange("p (a e) -> p a e", e=66)
                                for cb in range(4):
                                    h = 4 * kap + 2 * (cb % 2) + cb // 2
                                    S.op("act", ACTF(ost[:, h * 64:(h + 1) * 64], accv[:, cb, 0:64], AF.Identity, scale=r4s[:, kap, cb:cb + 1]),
                                         [bacc, b_r4[kap]], [bost[h]])
                            return f

                        def evacA_3(i, ost, bost):
                            def f():
                                ptr, bptr = ps_next()
                                S.op("pe", [MM(ptr[:, ft * 128:(ft + 1) * 128], ost[:, ft * 128:(ft + 1) * 128], ident[:]) for ft in range(4)],
                                     bost + [b_id], [bptr])
                                S.op("act", ACTF(oT[:, 0:4, i * 128:(i + 1) * 128], ptr[:].rearrange("p (a b) -> p a b", a=4), AF.Copy),
                                     [bptr], [boT[ft][i // 4] for ft in range(4)])
                            return f

                        pendA = {0: emit_qk_a(0)}
                        acc = bacc = None
                        for s_, (i, kap, jn, j, nj) in enumerate(stepsA):
                            if s_ + 1 < len(stepsA):
                                pendA[s_ + 1] = emit_qk_a(s_ + 1)
                            ost, bost = ostA[i % 2], b_ostA[i % 2]
                            if jn == 0:
                                acc, bacc = pbank[rot["accA"] % 3], bbank[rot["accA"] % 3]
                                rot["accA"] += 1
                            sbt, bsbt = sbt_next()
                            for hf, (pS, bpS) in enumerate(pendA.pop(s_)):
                                S.op("dve", STT(sbt[:, hf * 256:(hf + 1) * 256], pS[:, 0:256], 0.125,
                                                biasA[:, kap, j - i + 1, hf * 256:(hf + 1) * 256], ALU.mult, ALU.add),
                                     [bpS, b_biasA, bsbt], [bsbt])
                            pT, bpT = pT_next()
                            S.op("act", ACTF(pT[:], sbt[:], AF.Exp), [bsbt], [bpT])
                            S.op("pe", [MM(acc[:, cb * 66:(cb + 1) * 66], pT[:, cb * 128:(cb + 1) * 128], vaA[:, j, kap, :],
                                           start=(jn == 0 and cb == 0), stop=(jn == nj - 1))
                                        for cb in range(4)],
                                 [bpT, b_va[j]], [bacc])
                            drain(1)
                            if jn == nj - 1:
                                dq.append(evacA_1(acc, bacc, kap))
                                dq.append(evacA_2(acc, bacc, kap, ost, bost))
                                if kap == 1:
                                    dq.append(evacA_3(i, ost, bost))
                        drain(len(dq))
                    S.mark_barrier()
                    dump("oTa", oT[:, 0:4, :], [b for r in boT[0:4] for b in r])
                    chk("A")

                    with contextlib.ExitStack() as st_b:
                        qpad = sb(st_b, "qpad", [128, 2, SEQ], BF16)
                        kTb = sb(st_b, "kTb", [128, SEQ], BF16)
                        b_qk = [[S.newbuf() for _ in range(4)] for _ in range(2)]
                        vB = [sb(st_b, "vB0", [128, 16, 130], BF16)]
                        b_vB = [[S.newbuf() for _ in range(16)]]
                        strips = [sb(st_b, "strip0", [128, 1152], F32)]; b_strip = [S.newbuf()]
                        accS = sb(st_b, "accS", [128, 8, 130], F32); b_accS = S.newbuf()
                        o32 = sb(st_b, "o32", [128, 4, 128], F32); b_o32 = S.newbuf()
                        junk = sb(st_b, "junkB", [128, 128], F32); b_junk = S.newbuf()
                        ostB = [sb(st_b, f"ostB{i}", [128, 512], BF16) for i in range(2)]; b_ostB = [S.newbuf() for _ in range(2)]
                        sm = sb(st_b, "smB", [128, 32], F32); b_sm = S.newbuf()
                        S.op("pool", MEMSET(vB[0][:, :, 128:130], 1.0), [], b_vB[0])
                        S.op("pool", MEMSET(qpad[:].rearrange("p a t -> p (a t)"), 0.0), [], b_qk[0])
                        rnd = 0
                        for h in range(4):
                            par = 0
                            bqk, vb_, bvb = b_qk, vB[0], b_vB[0]
                            S.dma("sp", ("strip", 0), [(strips[0][:], strip_d[h])], writes=[b_strip[0]])
                            ws, wb = w_get(("qkb", li, h))
                            wv = wview(ws, 8, 256)
                            run_inproj([((lambda c, wv=wv, m=m: wv[:, c, m * 128:(m + 1) * 128]), G, wb, pc(li, C_GQB + m),
                                         ((qpad[0:64, 0, G * 512:(G + 1) * 512], qpad[64:128, 1, G * 512:(G + 1) * 512]) if m == 0
                                          else kTb[:, G * 512:(G + 1) * 512]), bqk[m][G])
                                        for m in range(2) for G in range(4)])
                            ws, wb = w_get(("vb", li, h))
                            wv = wview(ws, 8, 128)
                            for t4 in range(4):
                                pv, bpv = ps_next()
                                for tq in range(4):
                                    tt = t4 * 4 + tq
                                    S.op("pe", [MM(pv[:, tq * 128:(tq + 1) * 128], hT[:, c, tt * 128:(tt + 1) * 128], wv[:, c, :],
                                                   start=(c == 0), stop=(c == 7)) for c in range(8)],
                                         [wb] + [bh[c][t4] for c in range(8)], [bpv])
                                S.op("act", ACTF(vb_[:, t4 * 4:(t4 + 1) * 4, 0:128], pv[:].rearrange("p (a e) -> p a e", e=128), AF.Copy),
                                     [bpv], bvb[t4 * 4:(t4 + 1) * 4])
                            def evac_round(G, h=h):
                                S.op("act", ACTF(accS[:, 0:3, :], pbank[0][:, 0:390].rearrange("p (a e) -> p a e", e=130), AF.Copy), [bbank[0]], [b_accS])
                                S.op("dve", CP(accS[:, 3:6, :], pbank[1][:, 0:390].rearrange("p (a e) -> p a e", e=130)), [bbank[1], b_accS], [b_accS])
                                S.op("act", ACTF(accS[:, 6:8, :], pbank[2][:, 0:260].rearrange("p (a e) -> p a e", e=130), AF.Copy), [bbank[2], b_accS], [b_accS])
                                ost, bost = ostB[rot['rndB'] % 2], b_ostB[rot['rndB'] % 2]
                                rot['rndB'] += 1

                                def p1():
                                    S.op("dve", RECIP(sm[:, 0:8], accS[:, :, 128]), [b_accS, b_sm], [b_sm])
                                    S.op("dve", TS(sm[:, 8:12], sm[:, 4:8], small[:, 4:5], ALU.mult), [b_sm, b_small], [b_sm])

                                def p2():
                                    for b in range(4):
                                        S.op("act", ACTF(o32[:, b, :], accS[:, b, 0:128], AF.Identity, scale=sm[:, b:b + 1]), [b_accS, b_sm, b_o32], [b_o32])

                                def p3():
                                    for b in range(4):
                                        S.op("dve", STT(o32[:, b, :], accS[:, 4 + b, 0:128], sm[:, 8 + b:9 + b], o32[:, b, :], ALU.mult, ALU.add),
                                             [b_accS, b_sm, b_o32], [b_o32])

                                def p4():
                                    for b in range(4):
                                        S.op("act", ACTF(junk[:], o32[:, b, :], AF.Square, accum_out=sm[:, 12 + b:13 + b]), [b_o32, b_junk, b_sm], [b_junk, b_sm])
                                    S.op("act", ACTF(sm[:, 16:20], sm[:, 12:16], AF.Ln, bias=epsc[:, 0:1], scale=1.0 / 128), [b_sm, b_eps], [b_sm])
                                    S.op("act", ACTF(sm[:, 16:20], sm[:, 16:20], AF.Exp, scale=-0.5), [b_sm], [b_sm])

                                def p5():
                                    for b in range(4):
                                        S.op("dve", STT(ost[:, b * 128:(b + 1) * 128], o32[:, b, :], sm[:, 16 + b:17 + b], G2[:], ALU.mult, ALU.mult),
                                             [b_o32, b_sm, b_G2], [bost])

                                def p6():
                                    ptr, bptr = ps_next()
                                    S.op("pe", [MM(ptr[:, b * 128:(b + 1) * 128], ost[:, b * 128:(b + 1) * 128], ident[:]) for b in range(4)],
                                         [bost, b_id], [bptr])
                                    S.op("act", ACTF(oT[:, 4 + h, G * 512:(G + 1) * 512], ptr[:], AF.Copy), [bptr], [boT[4 + h][G]])

                                dq.extend([p1, p2, p3, p4, p5, p6])

                            steps = [(G, cm, j) for G in range(4) for cm in range(2) for j in range(16)]
                            LA = 2

                            def emit_qk(s_, bqk=bqk):
                                G, cm, j = steps[s_]
                                pS, bpS = ps_next()
                                S.op("pe", MM(pS[:], kTb[:, j * 128:(j + 1) * 128], qpad[:, cm, G * 512:(G + 1) * 512]),
                                     [bqk[1][j // 4], bqk[0][G]], [bpS])
                                return pS, bpS

                            pend = {}
                            for s_ in range(LA):
                                pend[s_] = emit_qk(s_)
                            started = set()
                            for s_, (G, cm, j) in enumerate(steps):
                                if s_ + LA < len(steps):
                                    pend[s_ + LA] = emit_qk(s_ + LA)
                                pS, bpS = pend.pop(s_)
                                if cm == 0 and j == 0:
                                    started = set()
                                dl = j - 4 * G
                                pT, bpT = pT_next()
                                if -1 <= dl <= 4:
                                    sbt, bsbt = sbt_next()
                                    S.op("dve", STT(sbt[:], pS[:], 0.125, strips[par][:, (4 - dl) * 128:(4 - dl) * 128 + 512], ALU.mult, ALU.add),
                                         [bpS, b_strip[par]], [bsbt])
                                    S.op("act", ACTF(pT[:], sbt[:], AF.Exp), [bsbt], [bpT])
                                else:
                                    col = h if dl < 0 else 4 + h
                                    S.op("act", ACTF(pT[:], pS[:], AF.Exp, scale=0.125, bias=cfar[:, col:col + 1]), [bpS, b_cfar], [bpT])
                                fns = []
                                touched = []
                                for b in range(4):
                                    a = cm * 4 + b
                                    bank, off = a // 3, (a % 3) * 130
                                    fns.append(MM(pbank[bank][:, off:off + 130], pT[:, b * 128:(b + 1) * 128], vb_[:, j, :],
                                                  start=(bank not in started), stop=(j == 15)))
                                    started.add(bank)
                                    if bbank[bank] not in touched:
                                        touched.append(bbank[bank])
                                S.op("pe", fns, [bpT, bvb[j]], touched)
                                if s_ % 3 == 2:
                                    drain(1)
                                if cm == 1 and j == 15:
                                    assert not dq
                                    evac_round(G)
                        drain(len(dq))
                    S.mark_barrier()
                dump("oT", oT[:], [b for r in boT for b in r])
                chk("B")
                with contextlib.ExitStack() as st_m:
                    S.mark_barrier()
                    mixT = sb(st_m, "mixT", [128, 8, SEQ], BF16); b_mix = [[S.newbuf() for _ in range(4)] for _ in range(8)]
                    gts = [sb(st_m, f"gt{i}", [128, 512], F32) for i in range(4)]; b_gts = [S.newbuf() for _ in range(4)]
                    m1s = [sb(st_m, f"m1{i}", [128, 512], F32) for i in range(4)]; b_m1s = [S.newbuf() for _ in range(4)]
                    rr = 0
                    for ft in range(8):
                        wsg, wbg = w_get(("gate", li, ft))
                        wvg = wview(wsg, 8, 256)
                        wsp, wbp = w_get(("proj", li, ft))
                        wvp = wsp[:, 0:1024].rearrange("p (m c n) -> p m c n", m=2, c=4)
                        for G in range(4):
                            ts = slice(G * 512, (G + 1) * 512)
                            rf, rb = hT_rhs(G)
                            res = []
                            for m in range(2):
                                pg, bpg = proj_fm(lambda c: wvg[:, c, m * 128:(m + 1) * 128], G, 8, rf, rb, wbg)
                                gt, bgt = gts[rr % 4], b_gts[rr % 4]
                                S.op("act", ACTF(gt[:], pg[:], AF.Sigmoid, bias=pc(li, C_BG + m * 8 + ft)), [bpg, b_pc], [bgt])
                                dump("gt0", gt[:], [bgt])
                                pp, bpp = proj_fm(lambda c: wvp[:, m, c, :], G, 4, lambda c: oT[:, m * 4 + c, ts],
                                                  [boT[m * 4 + c][G] for c in range(4)], wbp)
                                m1, bm1 = m1s[rr % 4], b_m1s[rr % 4]
                                rr += 1
                                S.op("dve", TT(m1[:], pp[:], gt[:], ALU.mult), [bpp, bgt], [bm1])
                                res.append((m1, bm1))
                            S.op("dve", TT(mixT[:, ft, ts], res[0][0][:], res[1][0][:], ALU.add), [res[0][1], res[1][1]], [b_mix[ft][G]])
                    dump("mixT", mixT[:], [b for r in b_mix for b in r])
                    for f2 in range(4):
                        ws, wb = w_get(("wo", li, f2))
                        wv = wview(ws, 8, 256)
                        for t2 in range(2):
                            fo = f2 * 2 + t2
                            for G in range(4):
                                ts = slice(G * 512, (G + 1) * 512)
                                po, bpo = proj_fm(lambda c: wv[:, c, t2 * 128:(t2 + 1) * 128], G, 8, lambda c: mixT[:, c, ts],
                                                  [b_mix[c][G] for c in range(8)], wb)
                                S.op("dve", TT(xT[:, fo, ts], po[:], xT[:, fo, ts], ALU.add), [bpo, bx[fo][G]], [bx[fo][G]])
                S.mark_barrier()
            S.mark_barrier()

            dump("xmid", xT[:], [b for r in bx for b in r])
            chk("M")
            rmsnorm_to_hT(li, C_LN2)
            with contextlib.ExitStack() as st_f:
                S.mark_barrier()
                actT = sb(st_f, "actT", [128, 11, SEQ], BF16); b_act = [[S.newbuf() for _ in range(4)] for _ in range(11)]
                raws = [sb(st_f, f"raw{i}", [128, SEQ + 2], F32) for i in range(2)]; b_raws = [S.newbuf() for _ in range(2)]
                us = [sb(st_f, f"u{i}", [128, SEQ], F32) for i in range(2)]; b_us = [S.newbuf() for _ in range(2)]
                sg = sb(st_f, "sg", [128, SEQ], BF16); b_sg = S.newbuf()
                for i2 in range(2):
                    S.op("pool", MEMSET(raws[i2][:, 0:1], 0.0), [], [b_raws[i2]])
                    S.op("pool", MEMSET(raws[i2][:, SEQ + 1:SEQ + 2], 0.0), [b_raws[i2]], [b_raws[i2]])
                tcount = 0
                pending = [None]

                def make_conv(raw, braw, u, bu, ct, which, i):
                    def conv():
                        S.op("dve", STT(u[:], raw[:, 0:SEQ], pc(li, C_CW + ct), u[:], ALU.mult, ALU.add), [braw, bu, b_pc], [bu])
                        S.op("dve", STT(u[:], raw[:, 2:SEQ + 2], pc(li, C_CW + 88 + ct), u[:], ALU.mult, ALU.add), [braw, bu, b_pc], [bu])
                        if which == 1:
                            S.op("act", ACTF(sg[:], u[:], AF.Silu), [bu, b_sg], [b_sg])
                        else:
                            S.op("pool", TT(actT[:, i, :], sg[:], u[:], ALU.mult), [b_sg, bu], b_act[i])
                    return conv

                for half in range(2):
                    for i in range(11):
                        p = half * 11 + i
                        ws, wb = w_get(("up", li, p))
                        wv = wview(ws, 8, 256)
                        for which in (1, 0):
                            ct = p + 22 * which
                            raw, braw = raws[tcount % 2], b_raws[tcount % 2]
                            u, bu = us[tcount % 2], b_us[tcount % 2]
                            tcount += 1
                            for G in range(4):
                                rf, rb = hT_rhs(G)
                                pu, bpu = proj_fm(lambda c: wv[:, c, which * 128:(which + 1) * 128], G, 8, rf, rb, wb)
                                S.op("act", ACTF(raw[:, 1 + G * 512:1 + (G + 1) * 512], pu[:], AF.Copy), [bpu, braw], [braw])
                                S.op("act", ACTF(u[:, G * 512:(G + 1) * 512], pu[:], AF.Identity, scale=pc(li, C_CW + 44 + ct), bias=pc(li, C_CB + ct)),
                                     [bpu, b_pc, bu], [bu])
                            if pending[0] is not None:
                                pending[0]()
                            pending[0] = make_conv(raw, braw, u, bu, ct, which, i)
                    if pending[0] is not None:
                        pending[0]()
                        pending[0] = None
                    for fo in range(8):
                        ws, wb = w_get(("down", li, half, fo))
                        wv = wview(ws, 11, 128)
                        for G in range(4):
                            ts = slice(G * 512, (G + 1) * 512)
                            po, bpo = proj_fm(lambda k: wv[:, k, :], G, 11, lambda k: actT[:, k, ts], [b_act[k][G] for k in range(11)], wb)
                            S.op("dve", TT(xT[:, fo, ts], po[:], xT[:, fo, ts], ALU.add), [bpo, bx[fo][G]], [bx[fo][G]])
            S.mark_barrier()

        S.enabled = True
        allx = [b for row in bx for b in row]
        for c in range(8):
            S.dma("sp", ("out", c), [(out_d[c * 128:(c + 1) * 128, :], xT[:, c, :])], reads=bx[c])
        S.wait_all("sp", allx)
        assert stop is not None or wst["idx"] == len(plan)
        S.emit()
    return nc


def _t5_bucket_np(rel):
    rel = np.asarray(rel, np.int64)
    half, max_exact = 16, 8
    ret = np.where(rel > 0, half, 0)
    n = np.abs(rel)
    nf = np.maximum(n, 1).astype(np.float32)
    large = max_exact + (np.log(nf / np.float32(max_exact)) / np.float32(math.log(128 / max_exact))
                         * np.float32(half - max_exact)).astype(np.int32)
    large = np.minimum(large, half - 1)
    return ret + np.where(n < max_exact, n, large)


def _host_layout(inp, layers):
    f32 = np.float32
    L = list(layers)
    g = lambda k: np.asarray(inp[k], f32)
    pcols = np.zeros((128, len(L) * NCOL), f32)
    brow = np.zeros((len(L), NBROW), f32)
    for li, l in enumerate(L):
        pcl = np.zeros((128, NCOL), f32)
        pcl[:, C_LN1:C_LN1 + 8] = g("ln1_g")[l].reshape(8, 128).T
        pcl[:, C_LN2:C_LN2 + 8] = g("ln2_g")[l].reshape(8, 128).T
        pcl[:, C_GQA] = np.tile(g("qn_a")[l], 2)
        pcl[:, C_GKA] = np.tile(g("kn_a")[l], 2)
        pcl[:, C_GQB] = np.tile(g("qn_b")[l], 2)
        pcl[:, C_GKB] = np.tile(g("kn_b")[l], 2)
        pcl[:, C_BG:C_BG + 16] = g("b_gate")[l].reshape(16, 128).T
        pcl[:, C_CW:C_CW + 132] = g("conv_w")[l].reshape(3, 44, 128).transpose(2, 0, 1).reshape(128, 132)
        pcl[:, C_CB:C_CB + 44] = g("conv_b")[l].reshape(44, 128).T
        pcols[:, li * NCOL:(li + 1) * NCOL] = pcl
        brow[li] = np.concatenate([g("subln_g")[l], g("sink")[l][SINK_PERM], g("lam_q1")[l], g("lam_k1")[l],
                                   g("lam_q2")[l], g("lam_k2")[l]])
    tab = g("rel_bias")
    cfar = np.concatenate([tab[15, 8:12], tab[31, 8:12]]).astype(f32)
    k = np.arange(128)[:, None]
    c = np.arange(1152)[None, :]
    bk = _t5_bucket_np(k - c + 512)
    stripB = np.stack([tab[bk, 8 + h] for h in range(4)]).astype(f32)
    q = np.arange(128)[None, :]
    biasA = np.zeros((128, 2, 3, 4, 128), f32)
    for di in range(3):
        rel = k + 128 * (di - 1) - q
        bkt = _t5_bucket_np(rel)
        ok = np.abs(rel) <= 128
        for kap in range(2):
            for cb in range(4):
                h = 4 * kap + 2 * (cb % 2) + cb // 2
                biasA[:, kap, di, cb, :] = np.where(ok, tab[bkt, h], f32(MASK_NEG))
    bones = np.zeros((128, 128), f32)
    bones[:64, :64] = 1.0 / 64
    bones[64:, 64:] = 1.0 / 64
    common = {
        "w_in": np.ascontiguousarray(g("w_in")[L]), "w_gate": np.ascontiguousarray(g("w_gate")[L]),
        "w_a_proj": np.ascontiguousarray(g("w_a_proj")[L]), "w_b_proj": np.ascontiguousarray(g("w_b_proj")[L]),
        "w_o": np.ascontiguousarray(g("w_o")[L]), "w_up": np.ascontiguousarray(g("w_up")[L]),
        "w_down": np.ascontiguousarray(g("w_down")[L]),
        "pcols": pcols, "brow": brow, "cfar": cfar, "stripB": stripB,
        "biasA": np.ascontiguousarray(biasA.reshape(128, 2 * 3 * 512)),
        "ident": np.eye(128, dtype=f32), "bones": bones,
    }
    return common


_PROGS = {}


def _run(layers, xT_list, inp):
    key = tuple(layers)
    if key not in _PROGS:
        _PROGS[key] = build_program(list(layers))
    nc = _PROGS[key]
    common = _host_layout(inp, layers)
    in_maps = [dict(common, xT=xT_list[b]) for b in range(NCORES)]
    res = run_bass_kernel_spmd(nc, in_maps, core_ids=list(range(NCORES)))
    return [np.asarray(r["outT"]) for r in res.results]


FUSED = True


def kernel(**inputs):
    x = np.asarray(inputs["x"], np.float32)
    xT = [np.ascontiguousarray(x[b].T) for b in range(NCORES)]
    if FUSED:
        xT = _run(range(DEPTH), xT, inputs)
    else:
        for l in range(DEPTH):
            xT = _run([l], xT, inputs)
    return np.stack([t.T for t in xT]).astype(np.float32)
```

```python
import contextlib
import math
import numpy as np
import concourse.bass as bass
import concourse.mybir as mybir
from concourse.bass_utils import run_bass_kernel_spmd

F32 = mybir.dt.float32
BF16 = mybir.dt.bfloat16
AF = mybir.ActivationFunctionType
ALU = mybir.AluOpType
AX = mybir.AxisListType

D = 1024
SEQ = 2048
DEPTH = 4
NCORES = 8
D_FF = 2816
IN_W = 2304
EPS = 1e-6
NCOL = 212
C_LN1, C_LN2, C_GQA, C_GKA, C_GQB, C_GKB, C_BG, C_CW, C_CB = 0, 8, 16, 17, 18, 19, 20, 36, 168
NBROW = 392
SINK_PERM = [0, 2, 1, 3, 4, 6, 5, 7]
MASK_NEG = -30000.0

EPOCH = 2000
ENGS = ("pe", "act", "dve", "pool", "sp")


class Buf:
    __slots__ = ("w", "r", "excl")

    def __init__(self, r=None):
        self.excl = False
        self.w = None
        self.r = dict(r) if r else {}


class Sched:
    def __init__(self, nc, stack):
        self.nc = nc
        self.stack = stack
        self.ops = {e: [] for e in ENGS}
        self.seq = {e: 0 for e in ENGS}
        self.sems = {}
        self.waited = {e: {} for e in ENGS}
        self.dma_cnt = {}
        self.barrier = {}
        self.enabled = True

    def sem(self, key):
        s = self.sems.get(key)
        if s is None:
            s = self.stack.enter_context(self.nc.semaphore("s_" + "_".join(str(k) for k in key)))
            self.sems[key] = s
        return s

    def newbuf(self):
        return Buf(self.barrier)

    def mark_barrier(self):
        b = {}
        for e in ENGS:
            n = self.seq[e]
            if n > 0:
                b[(e, (n - 1) // EPOCH)] = (n - 1) % EPOCH + 1
        for k, c in self.dma_cnt.items():
            b[k] = 16 * c
        self.barrier = b

    def _deps(self, eng, reads, writes):
        deps = {}
        for b in reads:
            if b.w is not None:
                k, v = b.w
                if deps.get(k, 0) < v:
                    deps[k] = v
        for b in writes:
            if b.w is not None:
                k, v = b.w
                if deps.get(k, 0) < v:
                    deps[k] = v
            for k, v in b.r.items():
                if deps.get(k, 0) < v:
                    deps[k] = v
        out = []
        wd = self.waited[eng]
        for k, v in deps.items():
            if eng == "pe" and k[0] == "pe":
                continue
            if wd.get(k, 0) >= v:
                continue
            wd[k] = v
            out.append((self.sem(k), v))
        return out

    def _commit(self, tok, reads, writes):
        k, v = tok
        for b in writes:
            b.w = tok
            b.r = {}
        for b in reads:
            if b.r.get(k, 0) < v:
                b.r[k] = v

    def op(self, eng, fns, reads=(), writes=()):
        if not self.enabled:
            return
        if not isinstance(fns, (list, tuple)):
            fns = [fns]
        if any(b.excl for b in reads):
            writes = list(writes) + [b for b in reads if b.excl]
            reads = [b for b in reads if not b.excl]
        waits = self._deps(eng, reads, writes)
        n = self.seq[eng]
        self.seq[eng] = n + 1
        key = (eng, n // EPOCH)
        tok = (key, n % EPOCH + 1)
        self.ops[eng].append((waits, fns, self.sem(key), 1))
        self._commit(tok, reads, writes)

    def dma(self, eng, slot, pairs, reads=(), writes=(), **kw):
        if not self.enabled:
            return
        waits = self._deps(eng, reads, writes)
        key = ("dma", slot)
        sem = self.sem(key)
        c = self.dma_cnt.get(key, 0)
        first = True
        for (o, i) in pairs:
            c += 1
            fn = (lambda e, o=o, i=i: e.dma_start(out=o, in_=i, **kw))
            self.ops[eng].append((waits if first else [], [fn], sem, 16))
            first = False
        self.dma_cnt[key] = c
        self._commit((key, 16 * c), reads, writes)

    def wait_all(self, eng, bufs):
        waits = self._deps(eng, bufs, bufs)
        self.ops[eng].append((waits, [], None, 0))

    def emit(self):
        objs = {"pe": "tensor", "act": "scalar", "dve": "vector", "pool": "gpsimd", "sp": "sync"}
        with self.nc.Block() as block:
            for e in ENGS:
                lst = self.ops[e]
                if not lst:
                    continue

                def body(engobj, lst=lst):
                    for waits, fns, sem, inc in lst:
                        for s, v in waits:
                            engobj.wait_ge(s, v)
                        for fn in fns[:-1]:
                            fn(engobj)
                        if fns:
                            fns[-1](engobj).then_inc(sem, inc)

                getattr(block, objs[e])(body)


def MM(out, lhsT, rhs, start=True, stop=True):
    return lambda e: e.matmul(out, lhsT=lhsT, rhs=rhs, start=start, stop=stop, skip_group_check=True)


def ACTF(out, in_, func, **kw):
    return lambda e: e.activation(out=out, in_=in_, func=func, **kw)


def STT(out, in0, scalar, in1, op0, op1):
    return lambda e: e.scalar_tensor_tensor(out=out, in0=in0, scalar=scalar, in1=in1, op0=op0, op1=op1)


def TT(out, in0, in1, op):
    return lambda e: e.tensor_tensor(out=out, in0=in0, in1=in1, op=op)


def TS(out, in0, s1, op0, s2=None, op1=None):
    if op1 is None:
        return lambda e: e.tensor_scalar(out=out, in0=in0, scalar1=s1, scalar2=None, op0=op0)
    return lambda e: e.tensor_scalar(out=out, in0=in0, scalar1=s1, scalar2=s2, op0=op0, op1=op1)


def CP(out, in_):
    return lambda e: e.tensor_copy(out=out, in_=in_)


def RECIP(out, in_):
    return lambda e: e.reciprocal(out=out, in_=in_)


def MEMSET(ap, v):
    return lambda e: e.memset(ap, v)


class _Stop(Exception):
    pass


def build_program(layers, stop=None, dumps=()):
    NL = len(layers)
    dbg_outs = {}
    nc = bass.Bass("TRN2", target_bir_lowering=False)

    def din(name, shape):
        return nc.dram_tensor(name, list(shape), F32, kind="ExternalInput").ap()

    xT_d = din("xT", [D, SEQ])
    w_in_d = din("w_in", [NL, D, IN_W])
    w_gate_d = din("w_gate", [NL, D, 2 * D])
    w_a_d = din("w_a_proj", [NL, 512, D])
    w_b_d = din("w_b_proj", [NL, 512, D])
    w_o_d = din("w_o", [NL, D, D])
    w_up_d = din("w_up", [NL, D, 2 * D_FF])
    w_dn_d = din("w_down", [NL, D_FF, D])
    pcols_d = din("pcols", [128, NL * NCOL])
    brow_d = din("brow", [NL, NBROW])
    cfar_d = din("cfar", [8])
    strip_d = din("stripB", [4, 128, 1408])
    biasA_d = din("biasA", [128, 2 * 3 * 512])
    ident_d = din("ident", [128, 128])
    bones_d = din("bones", [128, 128])
    out_d = nc.dram_tensor("outT", [D, SEQ], F32, kind="ExternalOutput").ap()

    with contextlib.ExitStack() as st:
        S = Sched(nc, st)

        uid = [0]

        def sb(stack, name, shape, dt):
            uid[0] += 1
            return stack.enter_context(nc.sbuf_tensor(f"{name}_{uid[0]}", list(shape), dt))

        chk_cnt = {}

        def chk(name):
            chk_cnt[name] = chk_cnt.get(name, 0) + 1
            if stop is None:
                return
            nm, _, n = stop.partition("#")
            if nm == name and chk_cnt[name] == int(n or 1):
                S.enabled = False

        def dump(name, ap, bufs):
            if name not in dumps or name in dbg_outs:
                return
            shp = list(ap.shape)
            dt_ = nc.dram_tensor("dbg_" + name, shp, ap.dtype, kind="ExternalOutput").ap()
            dbg_outs[name] = dt_
            S.dma("sp", ("dbg", name), [(dt_, ap)], reads=bufs)
            S.wait_all("sp", bufs)

        xT = sb(st, "xT_s", [128, 8, SEQ], F32)
        bx = [[S.newbuf() for _ in range(4)] for _ in range(8)]
        hT = sb(st, "hT_s", [128, 8, SEQ], BF16)
        bh = [[S.newbuf() for _ in range(4)] for _ in range(8)]
        pcols = sb(st, "pcols_s", [128, NL * NCOL], F32); b_pc = S.newbuf()
        brow = sb(st, "brow_s", [128, NBROW], F32); b_brow = S.newbuf()
        cfar = sb(st, "cfar_s", [128, 8], F32); b_cfar = S.newbuf()
        identf = sb(st, "identf", [128, 128], F32); b_idf = S.newbuf()
        bonesf = sb(st, "bonesf", [128, 128], F32); b_bof = S.newbuf()
        ident = sb(st, "ident_b", [128, 128], BF16); b_id = S.newbuf()
        bones = sb(st, "bones_b", [128, 128], BF16); b_bo = S.newbuf()
        ones = sb(st, "ones_b", [128, 128], BF16); b_ones = S.newbuf()
        epsc = sb(st, "epsc", [128, 1], F32); b_eps = S.newbuf()
        small = sb(st, "small", [128, 32], F32); b_small = S.newbuf()
        esink = sb(st, "esink", [128, 8], F32); b_esink = S.newbuf()
        G2 = sb(st, "G2", [128, 128], F32); b_G2 = S.newbuf()
        lamt = sb(st, "lamt", [128, 64], F32); b_lamt = S.newbuf()
        NWS = 4
        wslots = [sb(st, f"wslot{i}", [128, 2048], BF16) for i in range(NWS)]
        wbufs = [S.newbuf() for _ in range(NWS)]
        sqs = [sb(st, f"sq{i}", [128, 512], BF16) for i in range(2)]; b_sqs = [S.newbuf() for _ in range(2)]
        rts = [sb(st, f"rt{i}", [128, 512], F32) for i in range(2)]; b_rts = [S.newbuf() for _ in range(2)]

        ppair = [st.enter_context(nc.psum_tensor(f"pp{i}", [128, 1024], F32)) for i in range(4)]
        pbank = [ppair[i // 2][:, (i % 2) * 512:(i % 2 + 1) * 512] for i in range(8)]
        bbank = [S.newbuf() for _ in range(8)]
        for b_ in bbank:
            b_.excl = True
        rot = {"ps": 0, "sq": 0, "rt": 0, "accA": 0, "rndB": 0, "spair": 0, "fpair": 0}

        def ps_next():
            i = 3 + rot["ps"] % 5
            rot["ps"] += 1
            return pbank[i], bbank[i]

        def sq_next():
            i = rot["sq"] % 2
            rot["sq"] += 1
            return sqs[i], b_sqs[i]

        def rt_next():
            i = rot["rt"] % 2
            rot["rt"] += 1
            return rts[i], b_rts[i]

        def wview(slot, a, b):
            return slot[:, 0:a * b].rearrange("p (a b) -> p a b", a=a)

        def plan_layer(li):
            P = []

            def incols(c0, n):
                return w_in_d[li, :, c0:c0 + n].rearrange("(c p) n -> p c n", p=128)

            P.append((("qa", li, 0), lambda s: [(wview(s, 8, 256), incols(0, 256))]))
            P.append((("qa", li, 1), lambda s: [(wview(s, 8, 256), incols(256, 256))]))

            def kdup(s):
                v = s[:, 0:2048].rearrange("p (c k d e) -> p c k d e", c=8, k=2, d=2)
                return [(v[:, :, k, dd, :], incols(512 + k * 64, 64)) for k in range(2) for dd in range(2)]

            P.append((("ka", li), kdup))
            P.append((("va", li), lambda s: [(wview(s, 8, 128), incols(640, 128))]))
            for h in range(4):
                def qk(s, h=h):
                    v = wview(s, 8, 256)
                    return [(v[:, :, 0:128], incols(768 + h * 128, 128)),
                            (v[:, :, 128:256], incols(1280 + h * 128, 128))]
                P.append((("qkb", li, h), qk))
                P.append((("vb", li, h), lambda s, h=h: [(wview(s, 8, 128), incols(1792 + h * 128, 128))]))
            for ft in range(8):
                def gate(s, ft=ft):
                    v = wview(s, 8, 256)
                    g = lambda c0: w_gate_d[li, :, c0:c0 + 128].rearrange("(c p) n -> p c n", p=128)
                    return [(v[:, :, 0:128], g(ft * 128)), (v[:, :, 128:256], g(D + ft * 128))]
                P.append((("gate", li, ft), gate))

                def proj(s, ft=ft):
                    v = s[:, 0:1024].rearrange("p (m c n) -> p m c n", m=2, c=4)
                    a = w_a_d[li, :, ft * 128:(ft + 1) * 128].rearrange("(c p) n -> p c n", p=128)
                    b = w_b_d[li, :, ft * 128:(ft + 1) * 128].rearrange("(c p) n -> p c n", p=128)
                    return [(v[:, 0], a), (v[:, 1], b)]
                P.append((("proj", li, ft), proj))
            for f2 in range(4):
                P.append((("wo", li, f2), lambda s, f2=f2: [(wview(s, 8, 256), w_o_d[li, :, f2 * 256:(f2 + 1) * 256].rearrange("(c p) n -> p c n", p=128))]))
            for half in range(2):
                for i in range(11):
                    p = half * 11 + i

                    def up(s, p=p):
                        v = wview(s, 8, 256)
                        u = lambda c0: w_up_d[li, :, c0:c0 + 128].rearrange("(c p) n -> p c n", p=128)
                        return [(v[:, :, 0:128], u(p * 128)), (v[:, :, 128:256], u(D_FF + p * 128))]
                    P.append((("up", li, p), up))
                for fo in range(8):
                    def down(s, half=half, fo=fo):
                        v = wview(s, 11, 128)
                        src = w_dn_d[li, half * 1408:(half + 1) * 1408, fo * 128:(fo + 1) * 128].rearrange("(k p) n -> p k n", p=128)
                        return [(v, src)]
                    P.append((("down", li, half, fo), down))
            return P

        plan = []
        for li in range(NL):
            plan += plan_layer(li)
        wst = {"idx": 0, "issued": 0}
        AHEAD = NWS - 2

        def w_issue_upto(k):
            while wst["issued"] <= min(k, len(plan) - 1):
                t = wst["issued"]
                slot = t % NWS
                S.dma("pool", ("w", slot, plan[t][0][1]), plan[t][1](wslots[slot]), writes=[wbufs[slot]])
                wst["issued"] += 1

        def w_get(key):
            t = wst["idx"]
            assert plan[t][0] == key, (plan[t][0], key)
            w_issue_upto(t + AHEAD)
            wst["idx"] += 1
            return wslots[t % NWS], wbufs[t % NWS]

        for c in range(8):
            S.dma("sp", ("x", c), [(xT[:, c, :], xT_d[c * 128:(c + 1) * 128, :])], writes=bx[c])
        S.dma("sp", "c0", [(pcols[:], pcols_d)], writes=[b_pc])
        S.dma("sp", "c1", [(cfar[:], cfar_d.partition_broadcast(128))], writes=[b_cfar])
        S.dma("sp", "c2", [(identf[:], ident_d)], writes=[b_idf])
        S.dma("sp", "c3", [(bonesf[:], bones_d)], writes=[b_bof])
        S.op("dve", CP(ident[:], identf[:]), [b_idf], [b_id])
        S.op("dve", CP(bones[:], bonesf[:]), [b_bof], [b_bo])
        S.op("pool", MEMSET(ones[:], 1.0 / D), [], [b_ones])
        S.op("pool", MEMSET(epsc[:], EPS), [], [b_eps])
        w_issue_upto(AHEAD - 1)

        def pc(li, col, n=1):
            return pcols[:, li * NCOL + col: li * NCOL + col + n]

        def rstd_from_ms(pm, bpm, scale=1.0):
            rt, brt = rt_next()
            S.op("act", ACTF(rt[:], pm, AF.Ln, bias=epsc[:, 0:1], scale=scale), [bpm, b_eps], [brt])
            S.op("act", ACTF(rt[:], rt[:], AF.Exp, scale=-0.5), [brt], [brt])
            return rt, brt

        def rmsnorm_to_hT(li, gcol0):
            for G in range(4):
                ts = slice(G * 512, (G + 1) * 512)
                pm, bpm = ps_next()
                for c in range(8):
                    sq, bsq = sq_next()
                    S.op("act", ACTF(sq[:], xT[:, c, ts], AF.Square), [bx[c][G]], [bsq])
                    S.op("pe", MM(pm[:], ones[:], sq[:], start=(c == 0), stop=(c == 7)), [b_ones, bsq], [bpm])
                rt, brt = rstd_from_ms(pm[:], bpm)
                for c in range(8):
                    S.op("dve", STT(hT[:, c, ts], xT[:, c, ts], pc(li, gcol0 + c), rt[:], ALU.mult, ALU.mult),
                         [bx[c][G], b_pc, brt], [bh[c][G]])

        def proj_fm(wap_fn, G, kc, rhs_fn, rhs_bufs, wb):
            pz, bpz = ps_next()
            S.op("pe", [MM(pz[:], wap_fn(c), rhs_fn(c), start=(c == 0), stop=(c == kc - 1)) for c in range(kc)],
                 [wb] + rhs_bufs, [bpz])
            return pz, bpz

        def norm64_store(pz, bpz, gcol, dst, bdst):
            sq, bsq = sq_next()
            S.op("act", ACTF(sq[:], pz[:], AF.Square), [bpz], [bsq])
            pm, bpm = ps_next()
            S.op("pe", MM(pm[:], bones[:], sq[:]), [b_bo, bsq], [bpm])
            rt, brt = rstd_from_ms(pm[:], bpm)
            if isinstance(dst, tuple):
                for hf in range(2):
                    ps_ = slice(hf * 64, (hf + 1) * 64)
                    S.op("dve", STT(dst[hf], pz[ps_, :], gcol[ps_, :], rt[ps_, :], ALU.mult, ALU.mult), [bpz, b_pc, brt], [bdst])
            else:
                S.op("dve", STT(dst, pz[:], gcol, rt[:], ALU.mult, ALU.mult), [bpz, b_pc, brt], [bdst])

        def hT_rhs(G):
            return (lambda c: hT[:, c, G * 512:(G + 1) * 512]), [bh[c][G] for c in range(8)]

        dq = []

        def drain(n=1):
            for _ in range(n):
                if dq:
                    dq.pop(0)()

        def run_inproj(jobs):
            prev = None
            for (wfn, G, wb_, gcol, dst, bdst) in jobs:
                rf, rb = hT_rhs(G)
                pz, bpz = proj_fm(wfn, G, 8, rf, rb, wb_)
                if prev is not None:
                    norm64_store(*prev)
                prev = (pz, bpz, gcol, dst, bdst)
            if prev is not None:
                norm64_store(*prev)

        for li, l in enumerate(layers):
            lam_init = 0.8 - 0.6 * math.exp(-0.3 * l)
            S.dma("sp", "brow", [(brow[:], brow_d[li].partition_broadcast(128))], writes=[b_brow])
            S.op("dve", TT(lamt[:], brow[:, 136:200], brow[:, 200:264], ALU.mult), [b_brow], [b_lamt])
            S.op("dve", lambda e: e.reduce_sum(out=small[:, 0:1], in_=lamt[:], axis=AX.X), [b_lamt], [b_small])
            S.op("dve", TT(lamt[:], brow[:, 264:328], brow[:, 328:392], ALU.mult), [b_brow, b_lamt], [b_lamt])
            S.op("dve", lambda e: e.reduce_sum(out=small[:, 1:2], in_=lamt[:], axis=AX.X), [b_lamt, b_small], [b_small])
            S.op("act", ACTF(small[:, 2:4], small[:, 0:2], AF.Exp), [b_small], [b_small])
            S.op("dve", TT(small[:, 4:5], small[:, 3:4], small[:, 2:3], ALU.subtract), [b_small], [b_small])
            S.op("dve", TS(small[:, 4:5], small[:, 4:5], -lam_init, ALU.add), [b_small], [b_small])
            S.op("act", ACTF(esink[:], brow[:, 128:136], AF.Exp), [b_brow], [b_esink])
            S.op("dve", TS(G2[:], brow[:, 0:128], 1.0 - lam_init, ALU.mult), [b_brow], [b_G2])

            rmsnorm_to_hT(li, C_LN1)
            dump("hT", hT[:], [b for r in bh for b in r])
            chk("N1")

            with contextlib.ExitStack() as st_att:
                S.mark_barrier()
                oT = sb(st_att, "oT", [128, 8, SEQ], BF16)
                boT = [[S.newbuf() for _ in range(4)] for _ in range(8)]
                with contextlib.ExitStack() as st_ab:
                    pTsA = []; b_pTsA = []; sbtsA = []; b_sbtsA = []
                    rot["pT"] = 0; rot["sbt"] = 0

                    def pT_next():
                        i = rot["pT"] % 3
                        rot["pT"] += 1
                        return pTsA[i], b_pTsA[i]

                    def sbt_next():
                        i = rot["sbt"] % 2
                        rot["sbt"] += 1
                        return sbtsA[i], b_sbtsA[i]

                    with contextlib.ExitStack() as st_a:
                        pTsA += [sb(st_a, f"pTA{i}", [128, 512], BF16) for i in range(3)]; b_pTsA += [S.newbuf() for _ in range(3)]
                        sbtsA += [sb(st_a, f"sbtA{i}", [128, 512], F32) for i in range(2)]; b_sbtsA += [S.newbuf() for _ in range(2)]
                        qaT = sb(st_a, "qaT", [128, 4, SEQ], BF16); b_qa = [[S.newbuf() for _ in range(4)] for _ in range(4)]
                        kaT = sb(st_a, "kaT", [128, 2, SEQ], BF16); b_ka = [[S.newbuf() for _ in range(4)] for _ in range(2)]
                        vaA = sb(st_a, "vaA", [128, 16, 2, 66], BF16); b_va = [S.newbuf() for _ in range(16)]
                        biasA = sb(st_a, "biasA_s", [128, 2, 3, 512], F32); b_biasA = S.newbuf()
                        ostA = [sb(st_a, f"ostA{i}", [128, 512], BF16) for i in range(2)]; b_ostA = [[S.newbuf() for _ in range(8)] for _ in range(2)]
                        r4s = sb(st_a, "r4s", [128, 2, 4], F32); b_r4 = [S.newbuf() for _ in range(2)]
                        S.dma("sp", "biasA", [(biasA[:].rearrange("p a b c -> p (a b c)"), biasA_d)], writes=[b_biasA])
                        S.op("pool", MEMSET(vaA[:, :, :, 64:66], 1.0), [], b_va)
                        for half2 in range(2):
                            ws, wb = w_get(("qa", li, half2))
                            wv = wview(ws, 8, 256)
                            run_inproj([((lambda c, wv=wv, t2=t2: wv[:, c, t2 * 128:(t2 + 1) * 128]), G, wb, pc(li, C_GQA),
                                         qaT[:, half2 * 2 + t2, G * 512:(G + 1) * 512], b_qa[half2 * 2 + t2][G])
                                        for t2 in range(2) for G in range(4)])
                        ws, wb = w_get(("ka", li))
                        wv = wview(ws, 8, 256)
                        run_inproj([((lambda c, wv=wv, kap=kap: wv[:, c, kap * 128:(kap + 1) * 128]), G, wb, pc(li, C_GKA),
                                     kaT[:, kap, G * 512:(G + 1) * 512], b_ka[kap][G])
                                    for kap in range(2) for G in range(4)])
                        ws, wb = w_get(("va", li))
                        wv = wview(ws, 8, 128)
                        for t4 in range(4):
                            pv, bpv = ps_next()
                            for tq in range(4):
                                tt = t4 * 4 + tq
                                S.op("pe", [MM(pv[:, tq * 128:(tq + 1) * 128], hT[:, c, tt * 128:(tt + 1) * 128], wv[:, c, :],
                                               start=(c == 0), stop=(c == 7)) for c in range(8)],
                                     [wb] + [bh[c][t4] for c in range(8)], [bpv])
                            S.op("act", ACTF(vaA[:, t4 * 4:(t4 + 1) * 4, :, 0:64].rearrange("p t k e -> p (t k) e"),
                                             pv[:].rearrange("p (a e) -> p a e", e=64), AF.Copy),
                                 [bpv], b_va[t4 * 4:(t4 + 1) * 4])
                        dump("qaT", qaT[:], [b for r in b_qa for b in r])
                        dump("kaT", kaT[:], [b for r in b_ka for b in r])
                        dump("vaA", vaA[:], b_va)
                        chk("Ain")
                        stepsA = []
                        for i in range(16):
                            for kap in range(2):
                                js = [j for j in (i - 1, i, i + 1) if 0 <= j < 16]
                                for jn, j in enumerate(js):
                                    stepsA.append((i, kap, jn, j, len(js)))

                        def emit_qk_a(s_):
                            i, kap, jn, j, nj = stepsA[s_]
                            out = []
                            for hf in range(2):
                                pS, bpS = ps_next()
                                S.op("pe", MM(pS[:, 0:256],
                                              kaT[hf * 64:(hf + 1) * 64, kap, j * 128:(j + 1) * 128],
                                              qaT[hf * 64:(hf + 1) * 64, 2 * kap:2 * kap + 2, i * 128:(i + 1) * 128]),
                                     [b_ka[kap][j // 4], b_qa[2 * kap][i // 4], b_qa[2 * kap + 1][i // 4]], [bpS])
                                out.append((pS, bpS))
                            return out

                        def evacA_1(acc, bacc, kap):
                            def f():
                                accv = acc[:, 0:264].rearrange("p (a e) -> p a e", e=66)
                                r4 = r4s[:, kap, :]
                                S.op("dve", TT(r4, accv[:, :, 64], esink[:, kap * 4:(kap + 1) * 4], ALU.add), [bacc, b_esink], [b_r4[kap]])
                                S.op("dve", RECIP(r4, r4), [b_r4[kap]], [b_r4[kap]])
                            return f

                        def evacA_2(acc, bacc, kap, ost, bost):
                            def f():
                                accv = acc[:, 0:264].rearrange("p (a e) -> p a e", e=66)
                                for cb in range(4):
                                    h = 4 * kap + 2 * (cb % 2) + cb // 2
                                    S.op("act", ACTF(ost[:, h * 64:(h + 1) * 64], accv[:, cb, 0:64], AF.Identity, scale=r4s[:, kap, cb:cb + 1]),
                                         [bacc, b_r4[kap]], [bost[h]])
                            return f

                        def evacA_3(i, ost, bost):
                            def f():
                                ptr, bptr = ps_next()
                                S.op("pe", [MM(ptr[:, ft * 128:(ft + 1) * 128], ost[:, ft * 128:(ft + 1) * 128], ident[:]) for ft in range(4)],
                                     bost + [b_id], [bptr])
                                S.op("act", ACTF(oT[:, 0:4, i * 128:(i + 1) * 128], ptr[:].rearrange("p (a b) -> p a b", a=4), AF.Copy),
                                     [bptr], [boT[ft][i // 4] for ft in range(4)])
                            return f

                        pendA = {0: emit_qk_a(0)}
                        acc = bacc = None
                        for s_, (i, kap, jn, j, nj) in enumerate(stepsA):
                            if s_ + 1 < len(stepsA):
                                pendA[s_ + 1] = emit_qk_a(s_ + 1)
                            ost, bost = ostA[i % 2], b_ostA[i % 2]
                            if jn == 0:
                                acc, bacc = pbank[rot["accA"] % 3], bbank[rot["accA"] % 3]
                                rot["accA"] += 1
                            sbt, bsbt = sbt_next()
                            for hf, (pS, bpS) in enumerate(pendA.pop(s_)):
                                S.op("dve", STT(sbt[:, hf * 256:(hf + 1) * 256], pS[:, 0:256], 0.125,
                                                biasA[:, kap, j - i + 1, hf * 256:(hf + 1) * 256], ALU.mult, ALU.add),
                                     [bpS, b_biasA, bsbt], [bsbt])
                            pT, bpT = pT_next()
                            S.op("act", ACTF(pT[:], sbt[:], AF.Exp), [bsbt], [bpT])
                            S.op("pe", [MM(acc[:, cb * 66:(cb + 1) * 66], pT[:, cb * 128:(cb + 1) * 128], vaA[:, j, kap, :],
                                           start=(jn == 0 and cb == 0), stop=(jn == nj - 1))
                                        for cb in range(4)],
                                 [bpT, b_va[j]], [bacc])
                            drain(1)
                            if jn == nj - 1:
                                dq.append(evacA_1(acc, bacc, kap))
                                dq.append(evacA_2(acc, bacc, kap, ost, bost))
                                if kap == 1:
                                    dq.append(evacA_3(i, ost, bost))
                        drain(len(dq))
                    S.mark_barrier()
                    dump("oTa", oT[:, 0:4, :], [b for r in boT[0:4] for b in r])
                    chk("A")

                    with contextlib.ExitStack() as st_b:
                        pTs = [sb(st_b, f"pT{i}", [128, 1024], BF16) for i in range(3)]; b_pTs = [S.newbuf() for _ in range(3)]
                        stripHs = [sb(st_b, f"stripH{i}", [128, 1408], BF16) for i in range(2)]
                        stripLs = [sb(st_b, f"stripL{i}", [128, 1408], BF16) for i in range(2)]
                        b_stripHLs = [S.newbuf() for _ in range(2)]
                        qpad = sb(st_b, "qpad", [128, 2, SEQ], BF16)
                        kTb = sb(st_b, "kTb", [128, SEQ], BF16)
                        b_qk = [[S.newbuf() for _ in range(4)] for _ in range(2)]
                        vB = [sb(st_b, "vB0", [128, 16, 130], BF16)]
                        b_vB = [[S.newbuf() for _ in range(16)]]
                        strips = [sb(st_b, "strip0", [128, 1408], F32)]; b_strip = [S.newbuf()]
                        accS = sb(st_b, "accS", [128, 8, 130], F32); b_accS = S.newbuf()
                        o32 = sb(st_b, "o32", [128, 4, 128], F32); b_o32 = S.newbuf()
                        junk = sb(st_b, "junkB", [128, 128], F32); b_junk = S.newbuf()
                        ostB = [sb(st_b, f"ostB{i}", [128, 512], BF16) for i in range(2)]; b_ostB = [S.newbuf() for _ in range(2)]
                        sm = sb(st_b, "smB", [128, 32], F32); b_sm = S.newbuf()
                        S.op("pool", MEMSET(vB[0][:, :, 128:130], 1.0), [], b_vB[0])
                        S.op("pool", MEMSET(qpad[:].rearrange("p a t -> p (a t)"), 0.0), [], b_qk[0])
                        rnd = 0
                        for h in range(4):
                            par = 0
                            bqk, vb_, bvb = b_qk, vB[0], b_vB[0]
                            S.dma("sp", ("strip", 0), [(strips[0][:], strip_d[h])], writes=[b_strip[0]])
                            stripH, stripL, b_stripHL = stripHs[h % 2], stripLs[h % 2], b_stripHLs[h % 2]
                            S.op("act", ACTF(stripH[:], strips[0][:], AF.Identity, scale=8.0), [b_strip[0]], [b_stripHL])
                            S.op("dve", STT(stripL[:], strips[0][:], 8.0, stripH[:], ALU.mult, ALU.subtract), [b_strip[0], b_stripHL], [b_stripHL])
                            ws, wb = w_get(("qkb", li, h))
                            wv = wview(ws, 8, 256)
                            run_inproj([((lambda c, wv=wv, m=m: wv[:, c, m * 128:(m + 1) * 128]), G, wb, pc(li, C_GQB + m),
                                         ((qpad[0:64, 0, G * 512:(G + 1) * 512], qpad[64:128, 1, G * 512:(G + 1) * 512]) if m == 0
                                          else kTb[:, G * 512:(G + 1) * 512]), bqk[m][G])
                                        for m in range(2) for G in range(4)])
                            ws, wb = w_get(("vb", li, h))
                            wv = wview(ws, 8, 128)
                            for t4 in range(4):
                                pv, bpv = ps_next()
                                for tq in range(4):
                                    tt = t4 * 4 + tq
                                    S.op("pe", [MM(pv[:, tq * 128:(tq + 1) * 128], hT[:, c, tt * 128:(tt + 1) * 128], wv[:, c, :],
                                                   start=(c == 0), stop=(c == 7)) for c in range(8)],
                                         [wb] + [bh[c][t4] for c in range(8)], [bpv])
                                S.op("act", ACTF(vb_[:, t4 * 4:(t4 + 1) * 4, 0:128], pv[:].rearrange("p (a e) -> p a e", e=128), AF.Copy),
                                     [bpv], bvb[t4 * 4:(t4 + 1) * 4])
                            def evac_round(G, h=h):
                                S.op("act", ACTF(accS[:, 0:3, :], pbank[0][:, 0:390].rearrange("p (a e) -> p a e", e=130), AF.Copy), [bbank[0]], [b_accS])
                                S.op("dve", CP(accS[:, 3:6, :], pbank[1][:, 0:390].rearrange("p (a e) -> p a e", e=130)), [bbank[1], b_accS], [b_accS])
                                S.op("act", ACTF(accS[:, 6:8, :], pbank[2][:, 0:260].rearrange("p (a e) -> p a e", e=130), AF.Copy), [bbank[2], b_accS], [b_accS])
                                ost, bost = ostB[rot['rndB'] % 2], b_ostB[rot['rndB'] % 2]
                                rot['rndB'] += 1

                                def p1():
                                    S.op("dve", RECIP(sm[:, 0:8], accS[:, :, 128]), [b_accS, b_sm], [b_sm])
                                    S.op("dve", TS(sm[:, 8:12], sm[:, 4:8], small[:, 4:5], ALU.mult), [b_sm, b_small], [b_sm])

                                def p2():
                                    for b in range(4):
                                        S.op("pool", TS(o32[:, b, :], accS[:, b, 0:128], sm[:, b:b + 1], ALU.mult), [b_accS, b_sm, b_o32], [b_o32])

                                def p3():
                                    for b in range(4):
                                        S.op("dve", STT(o32[:, b, :], accS[:, 4 + b, 0:128], sm[:, 8 + b:9 + b], o32[:, b, :], ALU.mult, ALU.add),
                                             [b_accS, b_sm, b_o32], [b_o32])

                                def p4():
                                    for b in range(4):
                                        S.op("act", ACTF(junk[:], o32[:, b, :], AF.Square, accum_out=sm[:, 12 + b:13 + b]), [b_o32, b_junk, b_sm], [b_junk, b_sm])
                                    S.op("act", ACTF(sm[:, 16:20], sm[:, 12:16], AF.Ln, bias=epsc[:, 0:1], scale=1.0 / 128), [b_sm, b_eps], [b_sm])
                                    S.op("act", ACTF(sm[:, 16:20], sm[:, 16:20], AF.Exp, scale=-0.5), [b_sm], [b_sm])

                                def p5():
                                    for b in range(4):
                                        S.op("dve", STT(ost[:, b * 128:(b + 1) * 128], o32[:, b, :], sm[:, 16 + b:17 + b], G2[:], ALU.mult, ALU.mult),
                                             [b_o32, b_sm, b_G2], [bost])

                                def p6():
                                    ptr, bptr = pbank[3], bbank[3]
                                    S.op("pe", [MM(ptr[:, b * 128:(b + 1) * 128], ost[:, b * 128:(b + 1) * 128], ident[:]) for b in range(4)],
                                         [bost, b_id], [bptr])
                                    S.op("act", ACTF(oT[:, 4 + h, G * 512:(G + 1) * 512], ptr[:], AF.Copy), [bptr], [boT[4 + h][G]])

                                dq.extend([p1, p2, p3, p4, p5, p6])

                            pairs = [(G, cm, k) for G in range(4) for cm in range(2) for k in range(8)]

                            def emit_qk_pair(pi, bqk=bqk, stripH=stripH, stripL=stripL, b_stripHL=b_stripHL):
                                G, cm, k = pairs[pi]
                                buf = 2 + rot["spair"] % 2
                                rot["spair"] += 1
                                pp, bb = ppair[buf], [bbank[2 * buf], bbank[2 * buf + 1]]
                                dl0 = 2 * k - 4 * G
                                band = -2 <= dl0 <= 4
                                fns = []
                                for t in range(2):
                                    dst = pp[:, t * 512:(t + 1) * 512]
                                    fns.append(MM(dst, kTb[:, (2 * k + t) * 128:(2 * k + t + 1) * 128],
                                                  qpad[:, cm, G * 512:(G + 1) * 512], start=True, stop=not band))
                                    if band:
                                        w0 = (5 - (dl0 + t)) * 128
                                        fns.append(MM(dst, ident[:], stripH[:, w0:w0 + 512], start=False, stop=False))
                                        fns.append(MM(dst, ident[:], stripL[:, w0:w0 + 512], start=False, stop=True))
                                S.op("pe", fns, [bqk[1][k // 2], bqk[0][G]] + ([b_stripHL, b_id] if band else []), bb)
                                return pp, bb

                            pend = {0: emit_qk_pair(0)}
                            started = set()
                            for pi, (G, cm, k) in enumerate(pairs):
                                if pi + 1 < len(pairs):
                                    pend[pi + 1] = emit_qk_pair(pi + 1)
                                pp, bb = pend.pop(pi)
                                if cm == 0 and k == 0:
                                    started = set()
                                dl0 = 2 * k - 4 * G
                                ipt = rot["pT"] % 3
                                rot["pT"] += 1
                                pT2, bpT2 = pTs[ipt], b_pTs[ipt]
                                if -2 <= dl0 <= 4:
                                    S.op("act", ACTF(pT2[:], pp[:], AF.Exp, scale=0.125), bb, [bpT2])
                                else:
                                    col = h if dl0 < 0 else 4 + h
                                    S.op("act", ACTF(pT2[:], pp[:], AF.Exp, scale=0.125, bias=cfar[:, col:col + 1]), bb + [b_cfar], [bpT2])
                                fns = []
                                touched = []
                                for t in range(2):
                                    j = 2 * k + t
                                    for b in range(4):
                                        a = cm * 4 + b
                                        bank, off = a // 3, (a % 3) * 130
                                        fns.append(MM(pbank[bank][:, off:off + 130], pT2[:, t * 512 + b * 128:t * 512 + (b + 1) * 128], vb_[:, j, :],
                                                      start=(bank not in started), stop=(j == 15)))
                                        started.add(bank)
                                        if bbank[bank] not in touched:
                                            touched.append(bbank[bank])
                                S.op("pe", fns, [bpT2, bvb[2 * k], bvb[2 * k + 1]], touched)
                                if pi % 2 == 1:
                                    drain(1)
                                if cm == 1 and k == 7:
                                    assert not dq
                                    evac_round(G)
                        drain(len(dq))
                    S.mark_barrier()
                dump("oT", oT[:], [b for r in boT for b in r])
                chk("B")
                with contextlib.ExitStack() as st_m:
                    S.mark_barrier()
                    mixT = sb(st_m, "mixT", [128, 8, SEQ], BF16); b_mix = [[S.newbuf() for _ in range(4)] for _ in range(8)]
                    gts = [sb(st_m, f"gt{i}", [128, 512], F32) for i in range(4)]; b_gts = [S.newbuf() for _ in range(4)]
                    m1s = [sb(st_m, f"m1{i}", [128, 512], F32) for i in range(4)]; b_m1s = [S.newbuf() for _ in range(4)]
                    rr = 0
                    for ft in range(8):
                        wsg, wbg = w_get(("gate", li, ft))
                        wvg = wview(wsg, 8, 256)
                        wsp, wbp = w_get(("proj", li, ft))
                        wvp = wsp[:, 0:1024].rearrange("p (m c n) -> p m c n", m=2, c=4)
                        for G in range(4):
                            ts = slice(G * 512, (G + 1) * 512)
                            rf, rb = hT_rhs(G)
                            res = []
                            for m in range(2):
                                pg, bpg = proj_fm(lambda c: wvg[:, c, m * 128:(m + 1) * 128], G, 8, rf, rb, wbg)
                                gt, bgt = gts[rr % 4], b_gts[rr % 4]
                                S.op("act", ACTF(gt[:], pg[:], AF.Sigmoid, bias=pc(li, C_BG + m * 8 + ft)), [bpg, b_pc], [bgt])
                                dump("gt0", gt[:], [bgt])
                                pp, bpp = proj_fm(lambda c: wvp[:, m, c, :], G, 4, lambda c: oT[:, m * 4 + c, ts],
                                                  [boT[m * 4 + c][G] for c in range(4)], wbp)
                                m1, bm1 = m1s[rr % 4], b_m1s[rr % 4]
                                rr += 1
                                S.op("dve", TT(m1[:], pp[:], gt[:], ALU.mult), [bpp, bgt], [bm1])
                                res.append((m1, bm1))
                            S.op("dve", TT(mixT[:, ft, ts], res[0][0][:], res[1][0][:], ALU.add), [res[0][1], res[1][1]], [b_mix[ft][G]])
                    dump("mixT", mixT[:], [b for r in b_mix for b in r])
                    for f2 in range(4):
                        ws, wb = w_get(("wo", li, f2))
                        wv = wview(ws, 8, 256)
                        for t2 in range(2):
                            fo = f2 * 2 + t2
                            for G in range(4):
                                ts = slice(G * 512, (G + 1) * 512)
                                po, bpo = proj_fm(lambda c: wv[:, c, t2 * 128:(t2 + 1) * 128], G, 8, lambda c: mixT[:, c, ts],
                                                  [b_mix[c][G] for c in range(8)], wb)
                                S.op("dve", TT(xT[:, fo, ts], po[:], xT[:, fo, ts], ALU.add), [bpo, bx[fo][G]], [bx[fo][G]])
                S.mark_barrier()
            S.mark_barrier()

            dump("xmid", xT[:], [b for r in bx for b in r])
            chk("M")
            rmsnorm_to_hT(li, C_LN2)
            with contextlib.ExitStack() as st_f:
                S.mark_barrier()
                actT = sb(st_f, "actT", [128, 11, SEQ], BF16); b_act = [[S.newbuf() for _ in range(4)] for _ in range(11)]
                raws = [sb(st_f, f"raw{i}", [128, SEQ + 2], F32) for i in range(2)]; b_raws = [S.newbuf() for _ in range(2)]
                us = [sb(st_f, f"u{i}", [128, SEQ], F32) for i in range(2)]; b_us = [S.newbuf() for _ in range(2)]
                sg = sb(st_f, "sg", [128, SEQ], BF16); b_sg = S.newbuf()
                for i2 in range(2):
                    S.op("pool", MEMSET(raws[i2][:, 0:1], 0.0), [], [b_raws[i2]])
                    S.op("pool", MEMSET(raws[i2][:, SEQ + 1:SEQ + 2], 0.0), [b_raws[i2]], [b_raws[i2]])
                tcount = 0
                pending = [None]

                def make_conv(raw, braw, u, bu, ct, which, i):
                    def conv():
                        S.op("dve", STT(u[:], raw[:, 0:SEQ], pc(li, C_CW + ct), u[:], ALU.mult, ALU.add), [braw, bu, b_pc], [bu])
                        S.op("dve", STT(u[:], raw[:, 2:SEQ + 2], pc(li, C_CW + 88 + ct), u[:], ALU.mult, ALU.add), [braw, bu, b_pc], [bu])
                        if which == 1:
                            S.op("act", ACTF(sg[:], u[:], AF.Silu), [bu, b_sg], [b_sg])
                        else:
                            S.op("pool", TT(actT[:, i, :], sg[:], u[:], ALU.mult), [b_sg, bu], b_act[i])
                    return conv

                for half in range(2):
                    for i in range(11):
                        p = half * 11 + i
                        ws, wb = w_get(("up", li, p))
                        wv = wview(ws, 8, 256)
                        for which in (1, 0):
                            ct = p + 22 * which
                            raw, braw = raws[tcount % 2], b_raws[tcount % 2]
                            u, bu = us[tcount % 2], b_us[tcount % 2]
                            tcount += 1
                            for Gp in range(2):
                                kp = rot["fpair"] % 4
                                rot["fpair"] += 1
                                pp, bb = ppair[kp], [bbank[2 * kp], bbank[2 * kp + 1]]
                                for t in range(2):
                                    rf, rb = hT_rhs(2 * Gp + t)
                                    S.op("pe", [MM(pp[:, t * 512:(t + 1) * 512], wv[:, c, which * 128:(which + 1) * 128], rf(c),
                                                   start=(c == 0), stop=(c == 7)) for c in range(8)], [wb] + rb, [bb[t]])
                                S.op("act", ACTF(raw[:, 1 + Gp * 1024:1 + (Gp + 1) * 1024], pp[:], AF.Copy), bb + [braw], [braw])
                                S.op("act", ACTF(u[:, Gp * 1024:(Gp + 1) * 1024], pp[:], AF.Identity, scale=pc(li, C_CW + 44 + ct), bias=pc(li, C_CB + ct)),
                                     bb + [b_pc, bu], [bu])
                            if pending[0] is not None:
                                pending[0]()
                            pending[0] = make_conv(raw, braw, u, bu, ct, which, i)
                    if pending[0] is not None:
                        pending[0]()
                        pending[0] = None
                    for fo in range(8):
                        ws, wb = w_get(("down", li, half, fo))
                        wv = wview(ws, 11, 128)
                        for G in range(4):
                            ts = slice(G * 512, (G + 1) * 512)
                            po, bpo = proj_fm(lambda k: wv[:, k, :], G, 11, lambda k: actT[:, k, ts], [b_act[k][G] for k in range(11)], wb)
                            S.op("dve", TT(xT[:, fo, ts], po[:], xT[:, fo, ts], ALU.add), [bpo, bx[fo][G]], [bx[fo][G]])
            S.mark_barrier()

        S.enabled = True
        allx = [b for row in bx for b in row]
        for c in range(8):
            S.dma("sp", ("out", c), [(out_d[c * 128:(c + 1) * 128, :], xT[:, c, :])], reads=bx[c])
        S.wait_all("sp", allx)
        assert stop is not None or wst["idx"] == len(plan)
        S.emit()
    return nc


def _t5_bucket_np(rel):
    rel = np.asarray(rel, np.int64)
    half, max_exact = 16, 8
    ret = np.where(rel > 0, half, 0)
    n = np.abs(rel)
    nf = np.maximum(n, 1).astype(np.float32)
    large = max_exact + (np.log(nf / np.float32(max_exact)) / np.float32(math.log(128 / max_exact))
                         * np.float32(half - max_exact)).astype(np.int32)
    large = np.minimum(large, half - 1)
    return ret + np.where(n < max_exact, n, large)


def _host_layout(inp, layers):
    f32 = np.float32
    L = list(layers)
    g = lambda k: np.asarray(inp[k], f32)
    pcols = np.zeros((128, len(L) * NCOL), f32)
    brow = np.zeros((len(L), NBROW), f32)
    for li, l in enumerate(L):
        pcl = np.zeros((128, NCOL), f32)
        pcl[:, C_LN1:C_LN1 + 8] = g("ln1_g")[l].reshape(8, 128).T
        pcl[:, C_LN2:C_LN2 + 8] = g("ln2_g")[l].reshape(8, 128).T
        pcl[:, C_GQA] = np.tile(g("qn_a")[l], 2)
        pcl[:, C_GKA] = np.tile(g("kn_a")[l], 2)
        pcl[:, C_GQB] = np.tile(g("qn_b")[l], 2)
        pcl[:, C_GKB] = np.tile(g("kn_b")[l], 2)
        pcl[:, C_BG:C_BG + 16] = g("b_gate")[l].reshape(16, 128).T
        pcl[:, C_CW:C_CW + 132] = g("conv_w")[l].reshape(3, 44, 128).transpose(2, 0, 1).reshape(128, 132)
        pcl[:, C_CB:C_CB + 44] = g("conv_b")[l].reshape(44, 128).T
        pcols[:, li * NCOL:(li + 1) * NCOL] = pcl
        brow[li] = np.concatenate([g("subln_g")[l], g("sink")[l][SINK_PERM], g("lam_q1")[l], g("lam_k1")[l],
                                   g("lam_q2")[l], g("lam_k2")[l]])
    tab = g("rel_bias")
    cfar = np.concatenate([tab[15, 8:12], tab[31, 8:12]]).astype(f32)
    k = np.arange(128)[:, None]
    c = np.arange(1408)[None, :]
    bk = _t5_bucket_np(k - c + 640)
    stripB = np.stack([tab[bk, 8 + h] for h in range(4)]).astype(f32)
    q = np.arange(128)[None, :]
    biasA = np.zeros((128, 2, 3, 4, 128), f32)
    for di in range(3):
        rel = k + 128 * (di - 1) - q
        bkt = _t5_bucket_np(rel)
        ok = np.abs(rel) <= 128
        for kap in range(2):
            for cb in range(4):
                h = 4 * kap + 2 * (cb % 2) + cb // 2
                biasA[:, kap, di, cb, :] = np.where(ok, tab[bkt, h], f32(MASK_NEG))
    bones = np.zeros((128, 128), f32)
    bones[:64, :64] = 1.0 / 64
    bones[64:, 64:] = 1.0 / 64
    common = {
        "w_in": np.ascontiguousarray(g("w_in")[L]), "w_gate": np.ascontiguousarray(g("w_gate")[L]),
        "w_a_proj": np.ascontiguousarray(g("w_a_proj")[L]), "w_b_proj": np.ascontiguousarray(g("w_b_proj")[L]),
        "w_o": np.ascontiguousarray(g("w_o")[L]), "w_up": np.ascontiguousarray(g("w_up")[L]),
        "w_down": np.ascontiguousarray(g("w_down")[L]),
        "pcols": pcols, "brow": brow, "cfar": cfar, "stripB": stripB,
        "biasA": np.ascontiguousarray(biasA.reshape(128, 2 * 3 * 512)),
        "ident": np.eye(128, dtype=f32), "bones": bones,
    }
    return common


_PROGS = {}


def _run(layers, xT_list, inp):
    key = tuple(layers)
    if key not in _PROGS:
        _PROGS[key] = build_program(list(layers))
    nc = _PROGS[key]
    common = _host_layout(inp, layers)
    in_maps = [dict(common, xT=xT_list[b]) for b in range(NCORES)]
    res = run_bass_kernel_spmd(nc, in_maps, core_ids=list(range(NCORES)))
    return [np.asarray(r["outT"]) for r in res.results]


FUSED = True


def kernel(**inputs):
    x = np.asarray(inputs["x"], np.float32)
    xT = [np.ascontiguousarray(x[b].T) for b in range(NCORES)]
    if FUSED:
        xT = _run(range(DEPTH), xT, inputs)
    else:
        for l in range(DEPTH):
            xT = _run([l], xT, inputs)
    return np.stack([t.T for t in xT]).astype(np.float32)
```

```python
import contextlib
import math
import numpy as np
import concourse.bass as bass
import concourse.mybir as mybir
from concourse.bass_utils import run_bass_kernel_spmd

F32 = mybir.dt.float32
BF16 = mybir.dt.bfloat16
AF = mybir.ActivationFunctionType
ALU = mybir.AluOpType
AX = mybir.AxisListType

D = 1024
SEQ = 2048
DEPTH = 4
NCORES = 8
D_FF = 2816
IN_W = 2304
EPS = 1e-6
NCOL = 212
C_LN1, C_LN2, C_GQA, C_GKA, C_GQB, C_GKB, C_BG, C_CW, C_CB = 0, 8, 16, 17, 18, 19, 20, 36, 168
NBROW = 392
SINK_PERM = [0, 2, 1, 3, 4, 6, 5, 7]
MASK_NEG = -30000.0

EPOCH = 2000
ENGS = ("pe", "act", "dve", "pool", "sp")


class Buf:
    __slots__ = ("w", "r", "excl")

    def __init__(self, r=None):
        self.excl = False
        self.w = None
        self.r = dict(r) if r else {}


class Sched:
    def __init__(self, nc, stack):
        self.nc = nc
        self.stack = stack
        self.ops = {e: [] for e in ENGS}
        self.seq = {e: 0 for e in ENGS}
        self.sems = {}
        self.waited = {e: {} for e in ENGS}
        self.dma_cnt = {}
        self.barrier = {}
        self.enabled = True

    def sem(self, key):
        s = self.sems.get(key)
        if s is None:
            s = self.stack.enter_context(self.nc.semaphore("s_" + "_".join(str(k) for k in key)))
            self.sems[key] = s
        return s

    def newbuf(self):
        return Buf(self.barrier)

    def mark_barrier(self):
        b = {}
        for e in ENGS:
            n = self.seq[e]
            if n > 0:
                b[(e, (n - 1) // EPOCH)] = (n - 1) % EPOCH + 1
        for k, c in self.dma_cnt.items():
            b[k] = 16 * c
        self.barrier = b

    def _deps(self, eng, reads, writes):
        deps = {}
        for b in reads:
            if b.w is not None:
                k, v = b.w
                if deps.get(k, 0) < v:
                    deps[k] = v
        for b in writes:
            if b.w is not None:
                k, v = b.w
                if deps.get(k, 0) < v:
                    deps[k] = v
            for k, v in b.r.items():
                if deps.get(k, 0) < v:
                    deps[k] = v
        out = []
        wd = self.waited[eng]
        for k, v in deps.items():
            if eng == "pe" and k[0] == "pe":
                continue
            if wd.get(k, 0) >= v:
                continue
            wd[k] = v
            out.append((self.sem(k), v))
        return out

    def _commit(self, tok, reads, writes):
        k, v = tok
        for b in writes:
            b.w = tok
            b.r = {}
        for b in reads:
            if b.r.get(k, 0) < v:
                b.r[k] = v

    def op(self, eng, fns, reads=(), writes=()):
        if not self.enabled:
            return
        if not isinstance(fns, (list, tuple)):
            fns = [fns]
        if any(b.excl for b in reads):
            writes = list(writes) + [b for b in reads if b.excl]
            reads = [b for b in reads if not b.excl]
        waits = self._deps(eng, reads, writes)
        n = self.seq[eng]
        self.seq[eng] = n + 1
        key = (eng, n // EPOCH)
        tok = (key, n % EPOCH + 1)
        self.ops[eng].append((waits, fns, self.sem(key), 1))
        self._commit(tok, reads, writes)

    def dma(self, eng, slot, pairs, reads=(), writes=(), **kw):
        if not self.enabled:
            return
        waits = self._deps(eng, reads, writes)
        key = ("dma", slot)
        sem = self.sem(key)
        c = self.dma_cnt.get(key, 0)
        first = True
        for (o, i) in pairs:
            c += 1
            fn = (lambda e, o=o, i=i: e.dma_start(out=o, in_=i, **kw))
            self.ops[eng].append((waits if first else [], [fn], sem, 16))
            first = False
        self.dma_cnt[key] = c
        self._commit((key, 16 * c), reads, writes)

    def wait_all(self, eng, bufs):
        waits = self._deps(eng, bufs, bufs)
        self.ops[eng].append((waits, [], None, 0))

    def emit(self):
        objs = {"pe": "tensor", "act": "scalar", "dve": "vector", "pool": "gpsimd", "sp": "sync"}
        with self.nc.Block() as block:
            for e in ENGS:
                lst = self.ops[e]
                if not lst:
                    continue

                def body(engobj, lst=lst):
                    for waits, fns, sem, inc in lst:
                        for s, v in waits:
                            engobj.wait_ge(s, v)
                        for fn in fns[:-1]:
                            fn(engobj)
                        if fns:
                            fns[-1](engobj).then_inc(sem, inc)

                getattr(block, objs[e])(body)


def MM(out, lhsT, rhs, start=True, stop=True):
    return lambda e: e.matmul(out, lhsT=lhsT, rhs=rhs, start=start, stop=stop, skip_group_check=True)


def ACTF(out, in_, func, **kw):
    return lambda e: e.activation(out=out, in_=in_, func=func, **kw)


def STT(out, in0, scalar, in1, op0, op1):
    return lambda e: e.scalar_tensor_tensor(out=out, in0=in0, scalar=scalar, in1=in1, op0=op0, op1=op1)


def TT(out, in0, in1, op):
    return lambda e: e.tensor_tensor(out=out, in0=in0, in1=in1, op=op)


def TS(out, in0, s1, op0, s2=None, op1=None):
    if op1 is None:
        return lambda e: e.tensor_scalar(out=out, in0=in0, scalar1=s1, scalar2=None, op0=op0)
    return lambda e: e.tensor_scalar(out=out, in0=in0, scalar1=s1, scalar2=s2, op0=op0, op1=op1)


def CP(out, in_):
    return lambda e: e.tensor_copy(out=out, in_=in_)


def RECIP(out, in_):
    return lambda e: e.reciprocal(out=out, in_=in_)


def MEMSET(ap, v):
    return lambda e: e.memset(ap, v)


class _Stop(Exception):
    pass


def build_program(layers, stop=None, dumps=()):
    NL = len(layers)
    dbg_outs = {}
    nc = bass.Bass("TRN2", target_bir_lowering=False)

    def din(name, shape):
        return nc.dram_tensor(name, list(shape), F32, kind="ExternalInput").ap()

    xT_d = din("xT", [D, SEQ])
    w_in_d = din("w_in", [NL, D, IN_W])
    w_gate_d = din("w_gate", [NL, D, 2 * D])
    w_a_d = din("w_a_proj", [NL, 512, D])
    w_b_d = din("w_b_proj", [NL, 512, D])
    w_o_d = din("w_o", [NL, D, D])
    w_up_d = din("w_up", [NL, D, 2 * D_FF])
    w_dn_d = din("w_down", [NL, D_FF, D])
    pcols_d = din("pcols", [128, NL * NCOL])
    brow_d = din("brow", [NL, NBROW])
    cfar_d = din("cfar", [8])
    strip_d = din("stripB", [4, 128, 1408])
    biasA_d = din("biasA", [128, 2 * 3 * 512])
    ident_d = din("ident", [128, 128])
    bones_d = din("bones", [128, 128])
    out_d = nc.dram_tensor("outT", [D, SEQ], F32, kind="ExternalOutput").ap()

    with contextlib.ExitStack() as st:
        S = Sched(nc, st)

        uid = [0]

        def sb(stack, name, shape, dt):
            uid[0] += 1
            return stack.enter_context(nc.sbuf_tensor(f"{name}_{uid[0]}", list(shape), dt))

        chk_cnt = {}

        def chk(name):
            chk_cnt[name] = chk_cnt.get(name, 0) + 1
            if stop is None:
                return
            nm, _, n = stop.partition("#")
            if nm == name and chk_cnt[name] == int(n or 1):
                S.enabled = False

        def dump(name, ap, bufs):
            if name not in dumps or name in dbg_outs:
                return
            shp = list(ap.shape)
            dt_ = nc.dram_tensor("dbg_" + name, shp, ap.dtype, kind="ExternalOutput").ap()
            dbg_outs[name] = dt_
            S.dma("sp", ("dbg", name), [(dt_, ap)], reads=bufs)
            S.wait_all("sp", bufs)

        xT = sb(st, "xT_s", [128, 8, SEQ], F32)
        bx = [[S.newbuf() for _ in range(4)] for _ in range(8)]
        hT = sb(st, "hT_s", [128, 8, SEQ], BF16)
        bh = [[S.newbuf() for _ in range(4)] for _ in range(8)]
        pcols = sb(st, "pcols_s", [128, NL * NCOL], F32); b_pc = S.newbuf()
        brow = sb(st, "brow_s", [128, NBROW], F32); b_brow = S.newbuf()
        cfar = sb(st, "cfar_s", [128, 8], F32); b_cfar = S.newbuf()
        identf = sb(st, "identf", [128, 128], F32); b_idf = S.newbuf()
        bonesf = sb(st, "bonesf", [128, 128], F32); b_bof = S.newbuf()
        ident = sb(st, "ident_b", [128, 128], BF16); b_id = S.newbuf()
        bones = sb(st, "bones_b", [128, 128], BF16); b_bo = S.newbuf()
        ones = sb(st, "ones_b", [128, 128], BF16); b_ones = S.newbuf()
        epsc = sb(st, "epsc", [128, 1], F32); b_eps = S.newbuf()
        small = sb(st, "small", [128, 32], F32); b_small = S.newbuf()
        esink = sb(st, "esink", [128, 8], F32); b_esink = S.newbuf()
        G2 = sb(st, "G2", [128, 128], F32); b_G2 = S.newbuf()
        lamt = sb(st, "lamt", [128, 64], F32); b_lamt = S.newbuf()
        NWS = 4
        wslots = [sb(st, f"wslot{i}", [128, 2048], BF16) for i in range(NWS)]
        wbufs = [S.newbuf() for _ in range(NWS)]
        sqs = [sb(st, f"sq{i}", [128, 512], BF16) for i in range(2)]; b_sqs = [S.newbuf() for _ in range(2)]
        rts = [sb(st, f"rt{i}", [128, 512], F32) for i in range(2)]; b_rts = [S.newbuf() for _ in range(2)]

        ppair = [st.enter_context(nc.psum_tensor(f"pp{i}", [128, 1024], F32)) for i in range(4)]
        pbank = [ppair[i // 2][:, (i % 2) * 512:(i % 2 + 1) * 512] for i in range(8)]
        bbank = [S.newbuf() for _ in range(8)]
        for b_ in bbank:
            b_.excl = True
        rot = {"ps": 0, "sq": 0, "rt": 0, "accA": 0, "rndB": 0, "spair": 0, "fpair": 0}

        def ps_next():
            i = 3 + rot["ps"] % 5
            rot["ps"] += 1
            return pbank[i], bbank[i]

        def sq_next():
            i = rot["sq"] % 2
            rot["sq"] += 1
            return sqs[i], b_sqs[i]

        def rt_next():
            i = rot["rt"] % 2
            rot["rt"] += 1
            return rts[i], b_rts[i]

        def wview(slot, a, b):
            return slot[:, 0:a * b].rearrange("p (a b) -> p a b", a=a)

        def plan_layer(li):
            P = []

            def incols(c0, n):
                return w_in_d[li, :, c0:c0 + n].rearrange("(c p) n -> p c n", p=128)

            P.append((("qa", li, 0), lambda s: [(wview(s, 8, 256), incols(0, 256))]))
            P.append((("qa", li, 1), lambda s: [(wview(s, 8, 256), incols(256, 256))]))

            def kdup(s):
                v = s[:, 0:2048].rearrange("p (c k d e) -> p c k d e", c=8, k=2, d=2)
                return [(v[:, :, k, dd, :], incols(512 + k * 64, 64)) for k in range(2) for dd in range(2)]

            P.append((("ka", li), kdup))
            P.append((("va", li), lambda s: [(wview(s, 8, 128), incols(640, 128))]))
            for h in range(4):
                def qk(s, h=h):
                    v = wview(s, 8, 256)
                    return [(v[:, :, 0:128], incols(768 + h * 128, 128)),
                            (v[:, :, 128:256], incols(1280 + h * 128, 128))]
                P.append((("qkb", li, h), qk))
                P.append((("vb", li, h), lambda s, h=h: [(wview(s, 8, 128), incols(1792 + h * 128, 128))]))
            for ft in range(8):
                def gate(s, ft=ft):
                    v = wview(s, 8, 256)
                    g = lambda c0: w_gate_d[li, :, c0:c0 + 128].rearrange("(c p) n -> p c n", p=128)
                    return [(v[:, :, 0:128], g(ft * 128)), (v[:, :, 128:256], g(D + ft * 128))]
                P.append((("gate", li, ft), gate))

                def proj(s, ft=ft):
                    v = s[:, 0:1024].rearrange("p (m c n) -> p m c n", m=2, c=4)
                    a = w_a_d[li, :, ft * 128:(ft + 1) * 128].rearrange("(c p) n -> p c n", p=128)
                    b = w_b_d[li, :, ft * 128:(ft + 1) * 128].rearrange("(c p) n -> p c n", p=128)
                    return [(v[:, 0], a), (v[:, 1], b)]
                P.append((("proj", li, ft), proj))
            for f2 in range(4):
                P.append((("wo", li, f2), lambda s, f2=f2: [(wview(s, 8, 256), w_o_d[li, :, f2 * 256:(f2 + 1) * 256].rearrange("(c p) n -> p c n", p=128))]))
            for half in range(2):
                for i in range(11):
                    p = half * 11 + i

                    def up(s, p=p):
                        v = wview(s, 8, 256)
                        u = lambda c0: w_up_d[li, :, c0:c0 + 128].rearrange("(c p) n -> p c n", p=128)
                        return [(v[:, :, 0:128], u(p * 128)), (v[:, :, 128:256], u(D_FF + p * 128))]
                    P.append((("up", li, p), up))
                for fo in range(8):
                    def down(s, half=half, fo=fo):
                        v = wview(s, 11, 128)
                        src = w_dn_d[li, half * 1408:(half + 1) * 1408, fo * 128:(fo + 1) * 128].rearrange("(k p) n -> p k n", p=128)
                        return [(v, src)]
                    P.append((("down", li, half, fo), down))
            return P

        plan = []
        for li in range(NL):
            plan += plan_layer(li)
        wst = {"idx": 0, "issued": 0}
        AHEAD = NWS - 2

        def w_issue_upto(k):
            while wst["issued"] <= min(k, len(plan) - 1):
                t = wst["issued"]
                slot = t % NWS
                S.dma("pool", ("w", slot, plan[t][0][1]), plan[t][1](wslots[slot]), writes=[wbufs[slot]])
                wst["issued"] += 1

        def w_get(key):
            t = wst["idx"]
            assert plan[t][0] == key, (plan[t][0], key)
            w_issue_upto(t + AHEAD)
            wst["idx"] += 1
            return wslots[t % NWS], wbufs[t % NWS]

        for c in range(8):
            S.dma("sp", ("x", c), [(xT[:, c, :], xT_d[c * 128:(c + 1) * 128, :])], writes=bx[c])
        S.dma("sp", "c0", [(pcols[:], pcols_d)], writes=[b_pc])
        S.dma("sp", "c1", [(cfar[:], cfar_d.partition_broadcast(128))], writes=[b_cfar])
        S.dma("sp", "c2", [(identf[:], ident_d)], writes=[b_idf])
        S.dma("sp", "c3", [(bonesf[:], bones_d)], writes=[b_bof])
        S.op("dve", CP(ident[:], identf[:]), [b_idf], [b_id])
        S.op("dve", CP(bones[:], bonesf[:]), [b_bof], [b_bo])
        S.op("pool", MEMSET(ones[:], 1.0 / D), [], [b_ones])
        S.op("pool", MEMSET(epsc[:], EPS), [], [b_eps])
        w_issue_upto(AHEAD - 1)

        def pc(li, col, n=1):
            return pcols[:, li * NCOL + col: li * NCOL + col + n]

        def rstd_from_ms(pm, bpm, scale=1.0):
            rt, brt = rt_next()
            S.op("act", ACTF(rt[:], pm, AF.Ln, bias=epsc[:, 0:1], scale=scale), [bpm, b_eps], [brt])
            S.op("act", ACTF(rt[:], rt[:], AF.Exp, scale=-0.5), [brt], [brt])
            return rt, brt

        def rmsnorm_to_hT(li, gcol0):
            for G in range(4):
                ts = slice(G * 512, (G + 1) * 512)
                pm, bpm = ps_next()
                for c in range(8):
                    sq, bsq = sq_next()
                    S.op("act", ACTF(sq[:], xT[:, c, ts], AF.Square), [bx[c][G]], [bsq])
                    S.op("pe", MM(pm[:], ones[:], sq[:], start=(c == 0), stop=(c == 7)), [b_ones, bsq], [bpm])
                rt, brt = rstd_from_ms(pm[:], bpm)
                for c in range(8):
                    S.op("dve", STT(hT[:, c, ts], xT[:, c, ts], pc(li, gcol0 + c), rt[:], ALU.mult, ALU.mult),
                         [bx[c][G], b_pc, brt], [bh[c][G]])

        def proj_fm(wap_fn, G, kc, rhs_fn, rhs_bufs, wb):
            pz, bpz = ps_next()
            S.op("pe", [MM(pz[:], wap_fn(c), rhs_fn(c), start=(c == 0), stop=(c == kc - 1)) for c in range(kc)],
                 [wb] + rhs_bufs, [bpz])
            return pz, bpz

        def norm64_store(pz, bpz, gcol, dst, bdst):
            sq, bsq = sq_next()
            S.op("act", ACTF(sq[:], pz[:], AF.Square), [bpz], [bsq])
            pm, bpm = ps_next()
            S.op("pe", MM(pm[:], bones[:], sq[:]), [b_bo, bsq], [bpm])
            rt, brt = rstd_from_ms(pm[:], bpm)
            if isinstance(dst, tuple):
                for hf in range(2):
                    ps_ = slice(hf * 64, (hf + 1) * 64)
                    S.op("dve", STT(dst[hf], pz[ps_, :], gcol[ps_, :], rt[ps_, :], ALU.mult, ALU.mult), [bpz, b_pc, brt], [bdst])
            else:
                S.op("dve", STT(dst, pz[:], gcol, rt[:], ALU.mult, ALU.mult), [bpz, b_pc, brt], [bdst])

        def hT_rhs(G):
            return (lambda c: hT[:, c, G * 512:(G + 1) * 512]), [bh[c][G] for c in range(8)]

        dq = []

        def drain(n=1):
            for _ in range(n):
                if dq:
                    dq.pop(0)()

        def run_inproj(jobs):
            prev = None
            for (wfn, G, wb_, gcol, dst, bdst) in jobs:
                rf, rb = hT_rhs(G)
                pz, bpz = proj_fm(wfn, G, 8, rf, rb, wb_)
                if prev is not None:
                    norm64_store(*prev)
                prev = (pz, bpz, gcol, dst, bdst)
            if prev is not None:
                norm64_store(*prev)

        for li, l in enumerate(layers):
            lam_init = 0.8 - 0.6 * math.exp(-0.3 * l)
            S.dma("sp", "brow", [(brow[:], brow_d[li].partition_broadcast(128))], writes=[b_brow])
            S.op("dve", TT(lamt[:], brow[:, 136:200], brow[:, 200:264], ALU.mult), [b_brow], [b_lamt])
            S.op("dve", lambda e: e.reduce_sum(out=small[:, 0:1], in_=lamt[:], axis=AX.X), [b_lamt], [b_small])
            S.op("dve", TT(lamt[:], brow[:, 264:328], brow[:, 328:392], ALU.mult), [b_brow, b_lamt], [b_lamt])
            S.op("dve", lambda e: e.reduce_sum(out=small[:, 1:2], in_=lamt[:], axis=AX.X), [b_lamt, b_small], [b_small])
            S.op("act", ACTF(small[:, 2:4], small[:, 0:2], AF.Exp), [b_small], [b_small])
            S.op("dve", TT(small[:, 4:5], small[:, 3:4], small[:, 2:3], ALU.subtract), [b_small], [b_small])
            S.op("dve", TS(small[:, 4:5], small[:, 4:5], -lam_init, ALU.add), [b_small], [b_small])
            S.op("act", ACTF(esink[:], brow[:, 128:136], AF.Exp), [b_brow], [b_esink])
            S.op("dve", TS(G2[:], brow[:, 0:128], 1.0 - lam_init, ALU.mult), [b_brow], [b_G2])

            rmsnorm_to_hT(li, C_LN1)
            dump("hT", hT[:], [b for r in bh for b in r])
            chk("N1")

            with contextlib.ExitStack() as st_att:
                S.mark_barrier()
                oT = sb(st_att, "oT", [128, 8, SEQ], BF16)
                boT = [[S.newbuf() for _ in range(4)] for _ in range(8)]
                with contextlib.ExitStack() as st_ab:
                    pTsA = []; b_pTsA = []; sbtsA = []; b_sbtsA = []
                    rot["pT"] = 0; rot["sbt"] = 0

                    def pT_next():
                        i = rot["pT"] % 3
                        rot["pT"] += 1
                        return pTsA[i], b_pTsA[i]

                    def sbt_next():
                        i = rot["sbt"] % 2
                        rot["sbt"] += 1
                        return sbtsA[i], b_sbtsA[i]

                    with contextlib.ExitStack() as st_a:
                        pTsA += [sb(st_a, f"pTA{i}", [128, 512], BF16) for i in range(3)]; b_pTsA += [S.newbuf() for _ in range(3)]
                        sbtsA += [sb(st_a, f"sbtA{i}", [128, 512], F32) for i in range(2)]; b_sbtsA += [S.newbuf() for _ in range(2)]
                        qaT = sb(st_a, "qaT", [128, 4, SEQ], BF16); b_qa = [[S.newbuf() for _ in range(4)] for _ in range(4)]
                        kaT = sb(st_a, "kaT", [128, 2, SEQ], BF16); b_ka = [[S.newbuf() for _ in range(4)] for _ in range(2)]
                        vaA = sb(st_a, "vaA", [128, 16, 2, 66], BF16); b_va = [S.newbuf() for _ in range(16)]
                        biasA = sb(st_a, "biasA_s", [128, 2, 3, 512], F32); b_biasA = S.newbuf()
                        ostA = [sb(st_a, f"ostA{i}", [128, 512], BF16) for i in range(2)]; b_ostA = [[S.newbuf() for _ in range(8)] for _ in range(2)]
                        r4s = sb(st_a, "r4s", [128, 2, 4], F32); b_r4 = [S.newbuf() for _ in range(2)]
                        S.dma("sp", "biasA", [(biasA[:].rearrange("p a b c -> p (a b c)"), biasA_d)], writes=[b_biasA])
                        S.op("pool", MEMSET(vaA[:, :, :, 64:66], 1.0), [], b_va)
                        for half2 in range(2):
                            ws, wb = w_get(("qa", li, half2))
                            wv = wview(ws, 8, 256)
                            run_inproj([((lambda c, wv=wv, t2=t2: wv[:, c, t2 * 128:(t2 + 1) * 128]), G, wb, pc(li, C_GQA),
                                         qaT[:, half2 * 2 + t2, G * 512:(G + 1) * 512], b_qa[half2 * 2 + t2][G])
                                        for t2 in range(2) for G in range(4)])
                        ws, wb = w_get(("ka", li))
                        wv = wview(ws, 8, 256)
                        run_inproj([((lambda c, wv=wv, kap=kap: wv[:, c, kap * 128:(kap + 1) * 128]), G, wb, pc(li, C_GKA),
                                     kaT[:, kap, G * 512:(G + 1) * 512], b_ka[kap][G])
                                    for kap in range(2) for G in range(4)])
                        ws, wb = w_get(("va", li))
                        wv = wview(ws, 8, 128)
                        for t4 in range(4):
                            pv, bpv = ps_next()
                            for tq in range(4):
                                tt = t4 * 4 + tq
                                S.op("pe", [MM(pv[:, tq * 128:(tq + 1) * 128], hT[:, c, tt * 128:(tt + 1) * 128], wv[:, c, :],
                                               start=(c == 0), stop=(c == 7)) for c in range(8)],
                                     [wb] + [bh[c][t4] for c in range(8)], [bpv])
                            S.op("act", ACTF(vaA[:, t4 * 4:(t4 + 1) * 4, :, 0:64].rearrange("p t k e -> p (t k) e"),
                                             pv[:].rearrange("p (a e) -> p a e", e=64), AF.Copy),
                                 [bpv], b_va[t4 * 4:(t4 + 1) * 4])
                        dump("qaT", qaT[:], [b for r in b_qa for b in r])
                        dump("kaT", kaT[:], [b for r in b_ka for b in r])
                        dump("vaA", vaA[:], b_va)
                        chk("Ain")
                        stepsA = []
                        for i in range(16):
                            for kap in range(2):
                                js = [j for j in (i - 1, i, i + 1) if 0 <= j < 16]
                                for jn, j in enumerate(js):
                                    stepsA.append((i, kap, jn, j, len(js)))

                        def emit_qk_a(s_):
                            i, kap, jn, j, nj = stepsA[s_]
                            out = []
                            for hf in range(2):
                                pS, bpS = ps_next()
                                S.op("pe", MM(pS[:, 0:256],
                                              kaT[hf * 64:(hf + 1) * 64, kap, j * 128:(j + 1) * 128],
                                              qaT[hf * 64:(hf + 1) * 64, 2 * kap:2 * kap + 2, i * 128:(i + 1) * 128]),
                                     [b_ka[kap][j // 4], b_qa[2 * kap][i // 4], b_qa[2 * kap + 1][i // 4]], [bpS])
                                out.append((pS, bpS))
                            return out

                        def evacA_1(acc, bacc, kap):
                            def f():
                                accv = acc[:, 0:264].rearrange("p (a e) -> p a e", e=66)
                                r4 = r4s[:, kap, :]
                                S.op("dve", TT(r4, accv[:, :, 64], esink[:, kap * 4:(kap + 1) * 4], ALU.add), [bacc, b_esink], [b_r4[kap]])
                                S.op("dve", RECIP(r4, r4), [b_r4[kap]], [b_r4[kap]])
                            return f

                        def evacA_2(acc, bacc, kap, ost, bost):
                            def f():
                                accv = acc[:, 0:264].rearrange("p (a e) -> p a e", e=66)
                                for cb in range(4):
                                    h = 4 * kap + 2 * (cb % 2) + cb // 2
                                    if True:
                                        S.op("act", ACTF(ost[:, h * 64:(h + 1) * 64], accv[:, cb, 0:64], AF.Identity, scale=r4s[:, kap, cb:cb + 1]),
                                             [bacc, b_r4[kap]], [bost[h]])
                                    else:
                                        S.op("dve", TS(ost[:, h * 64:(h + 1) * 64], accv[:, cb, 0:64], r4s[:, kap, cb:cb + 1], ALU.mult),
                                             [bacc, b_r4[kap]], [bost[h]])
                            return f

                        def evacA_3(i, ost, bost):
                            def f():
                                ptr, bptr = ps_next()
                                S.op("pe", [MM(ptr[:, ft * 128:(ft + 1) * 128], ost[:, ft * 128:(ft + 1) * 128], ident[:]) for ft in range(4)],
                                     bost + [b_id], [bptr])
                                S.op("act", ACTF(oT[:, 0:4, i * 128:(i + 1) * 128], ptr[:].rearrange("p (a b) -> p a b", a=4), AF.Copy),
                                     [bptr], [boT[ft][i // 4] for ft in range(4)])
                            return f

                        pendA = {0: emit_qk_a(0)}
                        acc = bacc = None
                        for s_, (i, kap, jn, j, nj) in enumerate(stepsA):
                            if s_ + 1 < len(stepsA):
                                pendA[s_ + 1] = emit_qk_a(s_ + 1)
                            ost, bost = ostA[i % 2], b_ostA[i % 2]
                            if jn == 0:
                                acc, bacc = pbank[rot["accA"] % 3], bbank[rot["accA"] % 3]
                                rot["accA"] += 1
                            sbt, bsbt = sbt_next()
                            for hf, (pS, bpS) in enumerate(pendA.pop(s_)):
                                S.op("dve", STT(sbt[:, hf * 256:(hf + 1) * 256], pS[:, 0:256], 0.125,
                                                biasA[:, kap, j - i + 1, hf * 256:(hf + 1) * 256], ALU.mult, ALU.add),
                                     [bpS, b_biasA, bsbt], [bsbt])
                            pT, bpT = pT_next()
                            S.op("act", ACTF(pT[:], sbt[:], AF.Exp), [bsbt], [bpT])
                            S.op("pe", [MM(acc[:, cb * 66:(cb + 1) * 66], pT[:, cb * 128:(cb + 1) * 128], vaA[:, j, kap, :],
                                           start=(jn == 0 and cb == 0), stop=(jn == nj - 1))
                                        for cb in range(4)],
                                 [bpT, b_va[j]], [bacc])
                            drain(1)
                            if jn == nj - 1:
                                dq.append(evacA_1(acc, bacc, kap))
                                dq.append(evacA_2(acc, bacc, kap, ost, bost))
                                if kap == 1:
                                    dq.append(evacA_3(i, ost, bost))
                        drain(len(dq))
                    S.mark_barrier()
                    dump("oTa", oT[:, 0:4, :], [b for r in boT[0:4] for b in r])
                    chk("A")

                    with contextlib.ExitStack() as st_b:
                        pTs = [sb(st_b, f"pT{i}", [128, 1024], BF16) for i in range(3)]; b_pTs = [S.newbuf() for _ in range(3)]
                        stripHs = [sb(st_b, f"stripH{i}", [128, 1408], BF16) for i in range(2)]
                        stripLs = [sb(st_b, f"stripL{i}", [128, 1408], BF16) for i in range(2)]
                        b_stripHLs = [S.newbuf() for _ in range(2)]
                        qpad = sb(st_b, "qpad", [128, 2, SEQ], BF16)
                        kTb = sb(st_b, "kTb", [128, SEQ], BF16)
                        b_qk = [[S.newbuf() for _ in range(4)] for _ in range(2)]
                        vB = [sb(st_b, "vB0", [128, 16, 130], BF16)]
                        b_vB = [[S.newbuf() for _ in range(16)]]
                        strips = [sb(st_b, "strip0", [128, 1408], F32)]; b_strip = [S.newbuf()]
                        accS = sb(st_b, "accS", [128, 8, 130], F32); b_accS = S.newbuf()
                        o32 = sb(st_b, "o32", [128, 4, 128], F32); b_o32 = S.newbuf()
                        junk = sb(st_b, "junkB", [128, 128], F32); b_junk = S.newbuf()
                        ostB = [sb(st_b, f"ostB{i}", [128, 512], BF16) for i in range(2)]; b_ostB = [S.newbuf() for _ in range(2)]
                        sm = sb(st_b, "smB", [128, 32], F32); b_sm = S.newbuf()
                        S.op("pool", MEMSET(vB[0][:, :, 128:130], 1.0), [], b_vB[0])
                        S.op("pool", MEMSET(qpad[:].rearrange("p a t -> p (a t)"), 0.0), [], b_qk[0])
                        rnd = 0
                        for h in range(4):
                            par = 0
                            bqk, vb_, bvb = b_qk, vB[0], b_vB[0]
                            S.dma("sp", ("strip", 0), [(strips[0][:], strip_d[h])], writes=[b_strip[0]])
                            stripH, stripL, b_stripHL = stripHs[h % 2], stripLs[h % 2], b_stripHLs[h % 2]
                            S.op("act", ACTF(stripH[:], strips[0][:], AF.Identity, scale=8.0), [b_strip[0]], [b_stripHL])
                            S.op("dve", STT(stripL[:], strips[0][:], 8.0, stripH[:], ALU.mult, ALU.subtract), [b_strip[0], b_stripHL], [b_stripHL])
                            ws, wb = w_get(("qkb", li, h))
                            wv = wview(ws, 8, 256)
                            run_inproj([((lambda c, wv=wv, m=m: wv[:, c, m * 128:(m + 1) * 128]), G, wb, pc(li, C_GQB + m),
                                         ((qpad[0:64, 0, G * 512:(G + 1) * 512], qpad[64:128, 1, G * 512:(G + 1) * 512]) if m == 0
                                          else kTb[:, G * 512:(G + 1) * 512]), bqk[m][G])
                                        for m in range(2) for G in range(4)])
                            ws, wb = w_get(("vb", li, h))
                            wv = wview(ws, 8, 128)
                            for t4 in range(4):
                                pv, bpv = ps_next()
                                for tq in range(4):
                                    tt = t4 * 4 + tq
                                    S.op("pe", [MM(pv[:, tq * 128:(tq + 1) * 128], hT[:, c, tt * 128:(tt + 1) * 128], wv[:, c, :],
                                                   start=(c == 0), stop=(c == 7)) for c in range(8)],
                                         [wb] + [bh[c][t4] for c in range(8)], [bpv])
                                S.op("act", ACTF(vb_[:, t4 * 4:(t4 + 1) * 4, 0:128], pv[:].rearrange("p (a e) -> p a e", e=128), AF.Copy),
                                     [bpv], bvb[t4 * 4:(t4 + 1) * 4])
                            def evac_round(G, h=h):
                                S.op("act", ACTF(accS[:, 0:3, :], pbank[0][:, 0:390].rearrange("p (a e) -> p a e", e=130), AF.Copy), [bbank[0]], [b_accS])
                                S.op("dve", CP(accS[:, 3:6, :], pbank[1][:, 0:390].rearrange("p (a e) -> p a e", e=130)), [bbank[1], b_accS], [b_accS])
                                S.op("act", ACTF(accS[:, 6:8, :], pbank[2][:, 0:260].rearrange("p (a e) -> p a e", e=130), AF.Copy), [bbank[2], b_accS], [b_accS])
                                ost, bost = ostB[rot['rndB'] % 2], b_ostB[rot['rndB'] % 2]
                                rot['rndB'] += 1

                                def p1():
                                    S.op("dve", RECIP(sm[:, 0:8], accS[:, :, 128]), [b_accS, b_sm], [b_sm])
                                    S.op("dve", TS(sm[:, 8:12], sm[:, 4:8], small[:, 4:5], ALU.mult), [b_sm, b_small], [b_sm])

                                def p2():
                                    for b in range(4):
                                        S.op("dve", TS(o32[:, b, :], accS[:, b, 0:128], sm[:, b:b + 1], ALU.mult), [b_accS, b_sm, b_o32], [b_o32])

                                def p3():
                                    for b in range(4):
                                        S.op("dve", STT(o32[:, b, :], accS[:, 4 + b, 0:128], sm[:, 8 + b:9 + b], o32[:, b, :], ALU.mult, ALU.add),
                                             [b_accS, b_sm, b_o32], [b_o32])

                                def p4():
                                    for b in range(4):
                                        S.op("dve", lambda e, b=b: e.scalar_tensor_tensor(out=junk[:], in0=o32[:, b, :], scalar=1.0, in1=o32[:, b, :],
                                                                                          op0=ALU.mult, op1=ALU.mult, accum_out=sm[:, 12 + b:13 + b]),
                                             [b_o32, b_junk, b_sm], [b_junk, b_sm])
                                    S.op("act", ACTF(sm[:, 16:20], sm[:, 12:16], AF.Ln, bias=epsc[:, 0:1], scale=1.0 / 128), [b_sm, b_eps], [b_sm])
                                    S.op("act", ACTF(sm[:, 16:20], sm[:, 16:20], AF.Exp, scale=-0.5), [b_sm], [b_sm])

                                def p5():
                                    for b in range(4):
                                        S.op("dve", STT(ost[:, b * 128:(b + 1) * 128], o32[:, b, :], sm[:, 16 + b:17 + b], G2[:], ALU.mult, ALU.mult),
                                             [b_o32, b_sm, b_G2], [bost])

                                def p6():
                                    ptr, bptr = pbank[3], bbank[3]
                                    S.op("pe", [MM(ptr[:, b * 128:(b + 1) * 128], ost[:, b * 128:(b + 1) * 128], ident[:]) for b in range(4)],
                                         [bost, b_id], [bptr])
                                    S.op("dve", CP(oT[:, 4 + h, G * 512:(G + 1) * 512], ptr[:]), [bptr], [boT[4 + h][G]])

                                dq.extend([p1, p2, p3, p4, p5, p6])

                            pairs = [(G, cm, k) for G in range(4) for cm in range(2) for k in range(8)]

                            def emit_qk_pair(pi, bqk=bqk, stripH=stripH, stripL=stripL, b_stripHL=b_stripHL):
                                G, cm, k = pairs[pi]
                                buf = 2 + rot["spair"] % 2
                                rot["spair"] += 1
                                pp, bb = ppair[buf], [bbank[2 * buf], bbank[2 * buf + 1]]
                                dl0 = 2 * k - 4 * G
                                band = -2 <= dl0 <= 4
                                fns = []
                                for t in range(2):
                                    dst = pp[:, t * 512:(t + 1) * 512]
                                    fns.append(MM(dst, kTb[:, (2 * k + t) * 128:(2 * k + t + 1) * 128],
                                                  qpad[:, cm, G * 512:(G + 1) * 512], start=True, stop=not band))
                                    if band:
                                        w0 = (5 - (dl0 + t)) * 128
                                        fns.append(MM(dst, ident[:], stripH[:, w0:w0 + 512], start=False, stop=False))
                                        fns.append(MM(dst, ident[:], stripL[:, w0:w0 + 512], start=False, stop=True))
                                S.op("pe", fns, [bqk[1][k // 2], bqk[0][G]] + ([b_stripHL, b_id] if band else []), bb)
                                return pp, bb

                            pend = {0: emit_qk_pair(0)}
                            started = set()
                            for pi, (G, cm, k) in enumerate(pairs):
                                if pi + 1 < len(pairs):
                                    pend[pi + 1] = emit_qk_pair(pi + 1)
                                pp, bb = pend.pop(pi)
                                if cm == 0 and k == 0:
                                    started = set()
                                dl0 = 2 * k - 4 * G
                                ipt = rot["pT"] % 3
                                rot["pT"] += 1
                                pT2, bpT2 = pTs[ipt], b_pTs[ipt]
                                if -2 <= dl0 <= 4:
                                    S.op("act", ACTF(pT2[:], pp[:], AF.Exp, scale=0.125), bb, [bpT2])
                                else:
                                    col = h if dl0 < 0 else 4 + h
                                    S.op("act", ACTF(pT2[:], pp[:], AF.Exp, scale=0.125, bias=cfar[:, col:col + 1]), bb + [b_cfar], [bpT2])
                                fns = []
                                touched = []
                                for t in range(2):
                                    j = 2 * k + t
                                    for b in range(4):
                                        a = cm * 4 + b
                                        bank, off = a // 3, (a % 3) * 130
                                        fns.append(MM(pbank[bank][:, off:off + 130], pT2[:, t * 512 + b * 128:t * 512 + (b + 1) * 128], vb_[:, j, :],
                                                      start=(bank not in started), stop=(j == 15)))
                                        started.add(bank)
                                        if bbank[bank] not in touched:
                                            touched.append(bbank[bank])
                                S.op("pe", fns, [bpT2, bvb[2 * k], bvb[2 * k + 1]], touched)
                                if pi % 2 == 1:
                                    drain(1)
                                if cm == 1 and k == 7:
                                    assert not dq
                                    evac_round(G)
                        drain(len(dq))
                    S.mark_barrier()
                dump("oT", oT[:], [b for r in boT for b in r])
                chk("B")
                with contextlib.ExitStack() as st_m:
                    S.mark_barrier()
                    mixT = sb(st_m, "mixT", [128, 8, SEQ], BF16); b_mix = [[S.newbuf() for _ in range(4)] for _ in range(8)]
                    gts = [sb(st_m, f"gt{i}", [128, 512], F32) for i in range(4)]; b_gts = [S.newbuf() for _ in range(4)]
                    m1s = [sb(st_m, f"m1{i}", [128, 512], F32) for i in range(4)]; b_m1s = [S.newbuf() for _ in range(4)]
                    rr = 0
                    for ft in range(8):
                        wsg, wbg = w_get(("gate", li, ft))
                        wvg = wview(wsg, 8, 256)
                        wsp, wbp = w_get(("proj", li, ft))
                        wvp = wsp[:, 0:1024].rearrange("p (m c n) -> p m c n", m=2, c=4)
                        for G in range(4):
                            ts = slice(G * 512, (G + 1) * 512)
                            rf, rb = hT_rhs(G)
                            res = []
                            for m in range(2):
                                pg, bpg = proj_fm(lambda c: wvg[:, c, m * 128:(m + 1) * 128], G, 8, rf, rb, wbg)
                                gt, bgt = gts[rr % 4], b_gts[rr % 4]
                                S.op("act", ACTF(gt[:], pg[:], AF.Sigmoid, bias=pc(li, C_BG + m * 8 + ft)), [bpg, b_pc], [bgt])
                                dump("gt0", gt[:], [bgt])
                                pp, bpp = proj_fm(lambda c: wvp[:, m, c, :], G, 4, lambda c: oT[:, m * 4 + c, ts],
                                                  [boT[m * 4 + c][G] for c in range(4)], wbp)
                                m1, bm1 = m1s[rr % 4], b_m1s[rr % 4]
                                rr += 1
                                S.op("dve", TT(m1[:], pp[:], gt[:], ALU.mult), [bpp, bgt], [bm1])
                                res.append((m1, bm1))
                            S.op("dve", TT(mixT[:, ft, ts], res[0][0][:], res[1][0][:], ALU.add), [res[0][1], res[1][1]], [b_mix[ft][G]])
                    dump("mixT", mixT[:], [b for r in b_mix for b in r])
                    for f2 in range(4):
                        ws, wb = w_get(("wo", li, f2))
                        wv = wview(ws, 8, 256)
                        for t2 in range(2):
                            fo = f2 * 2 + t2
                            for G in range(4):
                                ts = slice(G * 512, (G + 1) * 512)
                                po, bpo = proj_fm(lambda c: wv[:, c, t2 * 128:(t2 + 1) * 128], G, 8, lambda c: mixT[:, c, ts],
                                                  [b_mix[c][G] for c in range(8)], wb)
                                S.op("dve", TT(xT[:, fo, ts], po[:], xT[:, fo, ts], ALU.add), [bpo, bx[fo][G]], [bx[fo][G]])
                S.mark_barrier()
            S.mark_barrier()

            dump("xmid", xT[:], [b for r in bx for b in r])
            chk("M")
            rmsnorm_to_hT(li, C_LN2)
            with contextlib.ExitStack() as st_f:
                S.mark_barrier()
                actT = sb(st_f, "actT", [128, 11, SEQ], BF16); b_act = [[S.newbuf() for _ in range(4)] for _ in range(11)]
                raws = [sb(st_f, f"raw{i}", [128, SEQ + 2], F32) for i in range(2)]; b_raws = [S.newbuf() for _ in range(2)]
                us = [sb(st_f, f"u{i}", [128, SEQ], F32) for i in range(2)]; b_us = [S.newbuf() for _ in range(2)]
                sg = sb(st_f, "sg", [128, SEQ], BF16); b_sg = S.newbuf()
                for i2 in range(2):
                    S.op("pool", MEMSET(raws[i2][:, 0:1], 0.0), [], [b_raws[i2]])
                    S.op("pool", MEMSET(raws[i2][:, SEQ + 1:SEQ + 2], 0.0), [b_raws[i2]], [b_raws[i2]])
                tcount = 0
                pending = [None]

                def make_conv(raw, braw, u, bu, ct, which, i):
                    def conv():
                        S.op("dve", STT(u[:], raw[:, 0:SEQ], pc(li, C_CW + ct), u[:], ALU.mult, ALU.add), [braw, bu, b_pc], [bu])
                        S.op("dve", STT(u[:], raw[:, 2:SEQ + 2], pc(li, C_CW + 88 + ct), u[:], ALU.mult, ALU.add), [braw, bu, b_pc], [bu])
                        if which == 1:
                            S.op("act", ACTF(sg[:], u[:], AF.Silu), [bu, b_sg], [b_sg])
                        else:
                            S.op("dve", TT(actT[:, i, :], sg[:], u[:], ALU.mult), [b_sg, bu], b_act[i])
                    return conv

                for half in range(2):
                    for i in range(11):
                        p = half * 11 + i
                        ws, wb = w_get(("up", li, p))
                        wv = wview(ws, 8, 256)
                        for which in (1, 0):
                            ct = p + 22 * which
                            raw, braw = raws[tcount % 2], b_raws[tcount % 2]
                            u, bu = us[tcount % 2], b_us[tcount % 2]
                            tcount += 1
                            for Gp in range(2):
                                kp = rot["fpair"] % 4
                                rot["fpair"] += 1
                                pp, bb = ppair[kp], [bbank[2 * kp], bbank[2 * kp + 1]]
                                for t in range(2):
                                    rf, rb = hT_rhs(2 * Gp + t)
                                    S.op("pe", [MM(pp[:, t * 512:(t + 1) * 512], wv[:, c, which * 128:(which + 1) * 128], rf(c),
                                                   start=(c == 0), stop=(c == 7)) for c in range(8)], [wb] + rb, [bb[t]])
                                S.op("act", ACTF(raw[:, 1 + Gp * 1024:1 + (Gp + 1) * 1024], pp[:], AF.Copy), bb + [braw], [braw])
                                S.op("act", ACTF(u[:, Gp * 1024:(Gp + 1) * 1024], pp[:], AF.Identity, scale=pc(li, C_CW + 44 + ct), bias=pc(li, C_CB + ct)),
                                     bb + [b_pc, bu], [bu])
                            if pending[0] is not None:
                                pending[0]()
                            pending[0] = make_conv(raw, braw, u, bu, ct, which, i)
                    if pending[0] is not None:
                        pending[0]()
                        pending[0] = None
                    for fo in range(8):
                        ws, wb = w_get(("down", li, half, fo))
                        wv = wview(ws, 11, 128)
                        for G in range(4):
                            ts = slice(G * 512, (G + 1) * 512)
                            po, bpo = proj_fm(lambda k: wv[:, k, :], G, 11, lambda k: actT[:, k, ts], [b_act[k][G] for k in range(11)], wb)
                            S.op("dve", TT(xT[:, fo, ts], po[:], xT[:, fo, ts], ALU.add), [bpo, bx[fo][G]], [bx[fo][G]])
            S.mark_barrier()

        S.enabled = True
        allx = [b for row in bx for b in row]
        for c in range(8):
            S.dma("sp", ("out", c), [(out_d[c * 128:(c + 1) * 128, :], xT[:, c, :])], reads=bx[c])
        S.wait_all("sp", allx)
        assert stop is not None or wst["idx"] == len(plan)
        S.emit()
    return nc


def _t5_bucket_np(rel):
    rel = np.asarray(rel, np.int64)
    half, max_exact = 16, 8
    ret = np.where(rel > 0, half, 0)
    n = np.abs(rel)
    nf = np.maximum(n, 1).astype(np.float32)
    large = max_exact + (np.log(nf / np.float32(max_exact)) / np.float32(math.log(128 / max_exact))
                         * np.float32(half - max_exact)).astype(np.int32)
    large = np.minimum(large, half - 1)
    return ret + np.where(n < max_exact, n, large)


def _host_layout(inp, layers):
    f32 = np.float32
    L = list(layers)
    g = lambda k: np.asarray(inp[k], f32)
    pcols = np.zeros((128, len(L) * NCOL), f32)
    brow = np.zeros((len(L), NBROW), f32)
    for li, l in enumerate(L):
        pcl = np.zeros((128, NCOL), f32)
        pcl[:, C_LN1:C_LN1 + 8] = g("ln1_g")[l].reshape(8, 128).T
        pcl[:, C_LN2:C_LN2 + 8] = g("ln2_g")[l].reshape(8, 128).T
        pcl[:, C_GQA] = np.tile(g("qn_a")[l], 2)
        pcl[:, C_GKA] = np.tile(g("kn_a")[l], 2)
        pcl[:, C_GQB] = np.tile(g("qn_b")[l], 2)
        pcl[:, C_GKB] = np.tile(g("kn_b")[l], 2)
        pcl[:, C_BG:C_BG + 16] = g("b_gate")[l].reshape(16, 128).T
        pcl[:, C_CW:C_CW + 132] = g("conv_w")[l].reshape(3, 44, 128).transpose(2, 0, 1).reshape(128, 132)
        pcl[:, C_CB:C_CB + 44] = g("conv_b")[l].reshape(44, 128).T
        pcols[:, li * NCOL:(li + 1) * NCOL] = pcl
        brow[li] = np.concatenate([g("subln_g")[l], g("sink")[l][SINK_PERM], g("lam_q1")[l], g("lam_k1")[l],
                                   g("lam_q2")[l], g("lam_k2")[l]])
    tab = g("rel_bias")
    cfar = np.concatenate([tab[15, 8:12], tab[31, 8:12]]).astype(f32)
    k = np.arange(128)[:, None]
    c = np.arange(1408)[None, :]
    bk = _t5_bucket_np(k - c + 640)
    stripB = np.stack([tab[bk, 8 + h] for h in range(4)]).astype(f32)
    q = np.arange(128)[None, :]
    biasA = np.zeros((128, 2, 3, 4, 128), f32)
    for di in range(3):
        rel = k + 128 * (di - 1) - q
        bkt = _t5_bucket_np(rel)
        ok = np.abs(rel) <= 128
        for kap in range(2):
            for cb in range(4):
                h = 4 * kap + 2 * (cb % 2) + cb // 2
                biasA[:, kap, di, cb, :] = np.where(ok, tab[bkt, h], f32(MASK_NEG))
    bones = np.zeros((128, 128), f32)
    bones[:64, :64] = 1.0 / 64
    bones[64:, 64:] = 1.0 / 64
    common = {
        "w_in": np.ascontiguousarray(g("w_in")[L]), "w_gate": np.ascontiguousarray(g("w_gate")[L]),
        "w_a_proj": np.ascontiguousarray(g("w_a_proj")[L]), "w_b_proj": np.ascontiguousarray(g("w_b_proj")[L]),
        "w_o": np.ascontiguousarray(g("w_o")[L]), "w_up": np.ascontiguousarray(g("w_up")[L]),
        "w_down": np.ascontiguousarray(g("w_down")[L]),
        "pcols": pcols, "brow": brow, "cfar": cfar, "stripB": stripB,
        "biasA": np.ascontiguousarray(biasA.reshape(128, 2 * 3 * 512)),
        "ident": np.eye(128, dtype=f32), "bones": bones,
    }
    return common


_PROGS = {}


def _run(layers, xT_list, inp):
    key = tuple(layers)
    if key not in _PROGS:
        _PROGS[key] = build_program(list(layers))
    nc = _PROGS[key]
    common = _host_layout(inp, layers)
    in_maps = [dict(common, xT=xT_list[b]) for b in range(NCORES)]
    res = run_bass_kernel_spmd(nc, in_maps, core_ids=list(range(NCORES)))
    return [np.asarray(r["outT"]) for r in res.results]


FUSED = True


def kernel(**inputs):
    x = np.asarray(inputs["x"], np.float32)
    xT = [np.ascontiguousarray(x[b].T) for b in range(NCORES)]
    if FUSED:
        xT = _run(range(DEPTH), xT, inputs)
    else:
        for l in range(DEPTH):
            xT = _run([l], xT, inputs)
    return np.stack([t.T for t in xT]).astype(np.float32)
```

```python
import contextlib
import math
import numpy as np
import concourse.bass as bass
import concourse.mybir as mybir
from concourse.bass_utils import run_bass_kernel_spmd

F32 = mybir.dt.float32
BF16 = mybir.dt.bfloat16
AF = mybir.ActivationFunctionType
ALU = mybir.AluOpType
AX = mybir.AxisListType

D = 1024
SEQ = 2048
DEPTH = 4
NCORES = 8
D_FF = 2816
IN_W = 2304
EPS = 1e-6
NCOL = 212
C_LN1, C_LN2, C_GQA, C_GKA, C_GQB, C_GKB, C_BG, C_CW, C_CB = 0, 8, 16, 17, 18, 19, 20, 36, 168
NBROW = 392
SINK_PERM = [0, 2, 1, 3, 4, 6, 5, 7]
MASK_NEG = -30000.0

EPOCH = 2000
ENGS = ("pe", "act", "dve", "pool", "sp")


class Buf:
    __slots__ = ("w", "r", "excl")

    def __init__(self, r=None):
        self.excl = False
        self.w = None
        self.r = dict(r) if r else {}


class Sched:
    def __init__(self, nc, stack):
        self.nc = nc
        self.stack = stack
        self.ops = {e: [] for e in ENGS}
        self.seq = {e: 0 for e in ENGS}
        self.sems = {}
        self.waited = {e: {} for e in ENGS}
        self.dma_cnt = {}
        self.barrier = {}
        self.enabled = True

    def sem(self, key):
        s = self.sems.get(key)
        if s is None:
            s = self.stack.enter_context(self.nc.semaphore("s_" + "_".join(str(k) for k in key)))
            self.sems[key] = s
        return s

    def newbuf(self):
        return Buf(self.barrier)

    def mark_barrier(self):
        b = {}
        for e in ENGS:
            n = self.seq[e]
            if n > 0:
                b[(e, (n - 1) // EPOCH)] = (n - 1) % EPOCH + 1
        for k, c in self.dma_cnt.items():
            b[k] = 16 * c
        self.barrier = b

    def _deps(self, eng, reads, writes):
        deps = {}
        for b in reads:
            if b.w is not None:
                k, v = b.w
                if deps.get(k, 0) < v:
                    deps[k] = v
        for b in writes:
            if b.w is not None:
                k, v = b.w
                if deps.get(k, 0) < v:
                    deps[k] = v
            for k, v in b.r.items():
                if deps.get(k, 0) < v:
                    deps[k] = v
        out = []
        wd = self.waited[eng]
        for k, v in deps.items():
            if eng == "pe" and k[0] == "pe":
                continue
            if wd.get(k, 0) >= v:
                continue
            wd[k] = v
            out.append((self.sem(k), v))
        return out

    def _commit(self, tok, reads, writes):
        k, v = tok
        for b in writes:
            b.w = tok
            b.r = {}
        for b in reads:
            if b.r.get(k, 0) < v:
                b.r[k] = v

    def op(self, eng, fns, reads=(), writes=()):
        if not self.enabled:
            return
        if not isinstance(fns, (list, tuple)):
            fns = [fns]
        if any(b.excl for b in reads):
            writes = list(writes) + [b for b in reads if b.excl]
            reads = [b for b in reads if not b.excl]
        waits = self._deps(eng, reads, writes)
        n = self.seq[eng]
        self.seq[eng] = n + 1
        key = (eng, n // EPOCH)
        tok = (key, n % EPOCH + 1)
        self.ops[eng].append((waits, fns, self.sem(key), 1))
        self._commit(tok, reads, writes)

    def dma(self, eng, slot, pairs, reads=(), writes=(), **kw):
        if not self.enabled:
            return
        waits = self._deps(eng, reads, writes)
        key = ("dma", slot)
        sem = self.sem(key)
        c = self.dma_cnt.get(key, 0)
        first = True
        for (o, i) in pairs:
            c += 1
            fn = (lambda e, o=o, i=i: e.dma_start(out=o, in_=i, **kw))
            self.ops[eng].append((waits if first else [], [fn], sem, 16))
            first = False
        self.dma_cnt[key] = c
        self._commit((key, 16 * c), reads, writes)

    def wait_all(self, eng, bufs):
        waits = self._deps(eng, bufs, bufs)
        self.ops[eng].append((waits, [], None, 0))

    def emit(self):
        objs = {"pe": "tensor", "act": "scalar", "dve": "vector", "pool": "gpsimd", "sp": "sync"}
        with self.nc.Block() as block:
            for e in ENGS:
                lst = self.ops[e]
                if not lst:
                    continue

                def body(engobj, lst=lst, e=e):
                    for waits, fns, sem, inc in lst:
                        attach = None
                        if fns and waits:
                            attach = waits[-1]
                            waits = waits[:-1]
                        for s, v in waits:
                            engobj.wait_ge(s, v)
                        n = len(fns)
                        for k, fn in enumerate(fns):
                            ins = fn(engobj)
                            if k == 0 and attach is not None:
                                ins._wait_ge(attach[0], attach[1])
                            if k == n - 1:
                                ins.then_inc(sem, inc)

                getattr(block, objs[e])(body)


def MM(out, lhsT, rhs, start=True, stop=True):
    return lambda e: e.matmul(out, lhsT=lhsT, rhs=rhs, start=start, stop=stop, skip_group_check=True)


def ACTF(out, in_, func, **kw):
    return lambda e: e.activation(out=out, in_=in_, func=func, **kw)


def STT(out, in0, scalar, in1, op0, op1):
    return lambda e: e.scalar_tensor_tensor(out=out, in0=in0, scalar=scalar, in1=in1, op0=op0, op1=op1)


def TT(out, in0, in1, op):
    return lambda e: e.tensor_tensor(out=out, in0=in0, in1=in1, op=op)


def TS(out, in0, s1, op0, s2=None, op1=None):
    if op1 is None:
        return lambda e: e.tensor_scalar(out=out, in0=in0, scalar1=s1, scalar2=None, op0=op0)
    return lambda e: e.tensor_scalar(out=out, in0=in0, scalar1=s1, scalar2=s2, op0=op0, op1=op1)


def CP(out, in_):
    return lambda e: e.tensor_copy(out=out, in_=in_)


def RECIP(out, in_):
    return lambda e: e.reciprocal(out=out, in_=in_)


def MEMSET(ap, v):
    return lambda e: e.memset(ap, v)


class _Stop(Exception):
    pass


def build_program(layers, stop=None, dumps=()):
    NL = len(layers)
    dbg_outs = {}
    nc = bass.Bass("TRN2", target_bir_lowering=False)

    def din(name, shape):
        return nc.dram_tensor(name, list(shape), F32, kind="ExternalInput").ap()

    xT_d = din("xT", [D, SEQ])
    w_in_d = din("w_in", [NL, D, IN_W])
    w_gate_d = din("w_gate", [NL, D, 2 * D])
    w_a_d = din("w_a_proj", [NL, 512, D])
    w_b_d = din("w_b_proj", [NL, 512, D])
    w_o_d = din("w_o", [NL, D, D])
    w_up_d = din("w_up", [NL, D, 2 * D_FF])
    w_dn_d = din("w_down", [NL, D_FF, D])
    pcols_d = din("pcols", [128, NL * NCOL])
    brow_d = din("brow", [NL, NBROW])
    cfar_d = din("cfar", [8])
    strip_d = din("stripB", [4, 128, 1408])
    biasA_d = din("biasA", [128, 2 * 3 * 512])
    ident_d = din("ident", [128, 128])
    bones_d = din("bones", [128, 128])
    out_d = nc.dram_tensor("outT", [D, SEQ], F32, kind="ExternalOutput").ap()

    with contextlib.ExitStack() as st:
        S = Sched(nc, st)

        uid = [0]

        def sb(stack, name, shape, dt):
            uid[0] += 1
            return stack.enter_context(nc.sbuf_tensor(f"{name}_{uid[0]}", list(shape), dt))

        chk_cnt = {}

        def chk(name):
            chk_cnt[name] = chk_cnt.get(name, 0) + 1
            if stop is None:
                return
            nm, _, n = stop.partition("#")
            if nm == name and chk_cnt[name] == int(n or 1):
                S.enabled = False

        def dump(name, ap, bufs):
            if name not in dumps or name in dbg_outs:
                return
            shp = list(ap.shape)
            dt_ = nc.dram_tensor("dbg_" + name, shp, ap.dtype, kind="ExternalOutput").ap()
            dbg_outs[name] = dt_
            S.dma("sp", ("dbg", name), [(dt_, ap)], reads=bufs)
            S.wait_all("sp", bufs)

        xT = sb(st, "xT_s", [128, 8, SEQ], F32)
        bx = [[S.newbuf() for _ in range(4)] for _ in range(8)]
        hT = sb(st, "hT_s", [128, 8, SEQ], BF16)
        bh = [[S.newbuf() for _ in range(4)] for _ in range(8)]
        pcols = sb(st, "pcols_s", [128, NL * NCOL], F32); b_pc = S.newbuf()
        brow = sb(st, "brow_s", [128, NBROW], F32); b_brow = S.newbuf()
        cfar = sb(st, "cfar_s", [128, 8], F32); b_cfar = S.newbuf()
        identf = sb(st, "identf", [128, 128], F32); b_idf = S.newbuf()
        bonesf = sb(st, "bonesf", [128, 128], F32); b_bof = S.newbuf()
        ident = sb(st, "ident_b", [128, 128], BF16); b_id = S.newbuf()
        bones = sb(st, "bones_b", [128, 128], BF16); b_bo = S.newbuf()
        ones = sb(st, "ones_b", [128, 128], BF16); b_ones = S.newbuf()
        epsc = sb(st, "epsc", [128, 1], F32); b_eps = S.newbuf()
        small = sb(st, "small", [128, 32], F32); b_small = S.newbuf()
        esink = sb(st, "esink", [128, 8], F32); b_esink = S.newbuf()
        G2 = sb(st, "G2", [128, 128], F32); b_G2 = S.newbuf()
        lamt = sb(st, "lamt", [128, 64], F32); b_lamt = S.newbuf()
        NWS = 4
        wslots = [sb(st, f"wslot{i}", [128, 2048], BF16) for i in range(NWS)]
        wbufs = [S.newbuf() for _ in range(NWS)]
        sqs = [sb(st, f"sq{i}", [128, 512], BF16) for i in range(2)]; b_sqs = [S.newbuf() for _ in range(2)]
        rts = [sb(st, f"rt{i}", [128, 512], F32) for i in range(2)]; b_rts = [S.newbuf() for _ in range(2)]

        ppair = [st.enter_context(nc.psum_tensor(f"pp{i}", [128, 1024], F32)) for i in range(4)]
        pbank = [ppair[i // 2][:, (i % 2) * 512:(i % 2 + 1) * 512] for i in range(8)]
        bbank = [S.newbuf() for _ in range(8)]
        for b_ in bbank:
            b_.excl = True
        rot = {"ps": 0, "sq": 0, "rt": 0, "accA": 0, "rndB": 0, "spair": 0, "fpair": 0}

        def ps_next():
            i = 3 + rot["ps"] % 5
            rot["ps"] += 1
            return pbank[i], bbank[i]

        def sq_next():
            i = rot["sq"] % 2
            rot["sq"] += 1
            return sqs[i], b_sqs[i]

        def rt_next():
            i = rot["rt"] % 2
            rot["rt"] += 1
            return rts[i], b_rts[i]

        def wview(slot, a, b):
            return slot[:, 0:a * b].rearrange("p (a b) -> p a b", a=a)

        def plan_layer(li):
            P = []

            def incols(c0, n):
                return w_in_d[li, :, c0:c0 + n].rearrange("(c p) n -> p c n", p=128)

            P.append((("qa", li, 0), lambda s: [(wview(s, 8, 256), incols(0, 256))]))
            P.append((("qa", li, 1), lambda s: [(wview(s, 8, 256), incols(256, 256))]))

            def kdup(s):
                v = s[:, 0:2048].rearrange("p (c k d e) -> p c k d e", c=8, k=2, d=2)
                return [(v[:, :, k, dd, :], incols(512 + k * 64, 64)) for k in range(2) for dd in range(2)]

            P.append((("ka", li), kdup))
            P.append((("va", li), lambda s: [(wview(s, 8, 128), incols(640, 128))]))
            for h in range(4):
                def qk(s, h=h):
                    v = wview(s, 8, 256)
                    return [(v[:, :, 0:128], incols(768 + h * 128, 128)),
                            (v[:, :, 128:256], incols(1280 + h * 128, 128))]
                P.append((("qkb", li, h), qk))
                P.append((("vb", li, h), lambda s, h=h: [(wview(s, 8, 128), incols(1792 + h * 128, 128))]))
            for ft in range(8):
                def gate(s, ft=ft):
                    v = wview(s, 8, 256)
                    g = lambda c0: w_gate_d[li, :, c0:c0 + 128].rearrange("(c p) n -> p c n", p=128)
                    return [(v[:, :, 0:128], g(ft * 128)), (v[:, :, 128:256], g(D + ft * 128))]
                P.append((("gate", li, ft), gate))

                def proj(s, ft=ft):
                    v = s[:, 0:1024].rearrange("p (m c n) -> p m c n", m=2, c=4)
                    a = w_a_d[li, :, ft * 128:(ft + 1) * 128].rearrange("(c p) n -> p c n", p=128)
                    b = w_b_d[li, :, ft * 128:(ft + 1) * 128].rearrange("(c p) n -> p c n", p=128)
                    return [(v[:, 0], a), (v[:, 1], b)]
                P.append((("proj", li, ft), proj))
            for f2 in range(4):
                P.append((("wo", li, f2), lambda s, f2=f2: [(wview(s, 8, 256), w_o_d[li, :, f2 * 256:(f2 + 1) * 256].rearrange("(c p) n -> p c n", p=128))]))
            for half in range(2):
                for i in range(11):
                    p = half * 11 + i

                    def up(s, p=p):
                        v = wview(s, 8, 256)
                        u = lambda c0: w_up_d[li, :, c0:c0 + 128].rearrange("(c p) n -> p c n", p=128)
                        return [(v[:, :, 0:128], u(p * 128)), (v[:, :, 128:256], u(D_FF + p * 128))]
                    P.append((("up", li, p), up))
                for fo in range(8):
                    def down(s, half=half, fo=fo):
                        v = wview(s, 11, 128)
                        src = w_dn_d[li, half * 1408:(half + 1) * 1408, fo * 128:(fo + 1) * 128].rearrange("(k p) n -> p k n", p=128)
                        return [(v, src)]
                    P.append((("down", li, half, fo), down))
            return P

        plan = []
        for li in range(NL):
            plan += plan_layer(li)
        wst = {"idx": 0, "issued": 0}
        AHEAD = NWS - 2

        def w_issue_upto(k):
            while wst["issued"] <= min(k, len(plan) - 1):
                t = wst["issued"]
                slot = t % NWS
                S.dma("pool", ("w", slot, plan[t][0][1]), plan[t][1](wslots[slot]), writes=[wbufs[slot]])
                wst["issued"] += 1

        def w_get(key):
            t = wst["idx"]
            assert plan[t][0] == key, (plan[t][0], key)
            w_issue_upto(t + AHEAD)
            wst["idx"] += 1
            return wslots[t % NWS], wbufs[t % NWS]

        for c in range(8):
            S.dma("sp", ("x", c), [(xT[:, c, :], xT_d[c * 128:(c + 1) * 128, :])], writes=bx[c])
        S.dma("sp", "c0", [(pcols[:], pcols_d)], writes=[b_pc])
        S.dma("sp", "c1", [(cfar[:], cfar_d.partition_broadcast(128))], writes=[b_cfar])
        S.dma("sp", "c2", [(identf[:], ident_d)], writes=[b_idf])
        S.dma("sp", "c3", [(bonesf[:], bones_d)], writes=[b_bof])
        S.op("dve", CP(ident[:], identf[:]), [b_idf], [b_id])
        S.op("dve", CP(bones[:], bonesf[:]), [b_bof], [b_bo])
        S.op("pool", MEMSET(ones[:], 1.0 / D), [], [b_ones])
        S.op("pool", MEMSET(epsc[:], EPS), [], [b_eps])
        w_issue_upto(AHEAD - 1)

        def pc(li, col, n=1):
            return pcols[:, li * NCOL + col: li * NCOL + col + n]

        def rstd_from_ms(pm, bpm, scale=1.0):
            rt, brt = rt_next()
            S.op("act", ACTF(rt[:], pm, AF.Ln, bias=epsc[:, 0:1], scale=scale), [bpm, b_eps], [brt])
            S.op("act", ACTF(rt[:], rt[:], AF.Exp, scale=-0.5), [brt], [brt])
            return rt, brt

        def rmsnorm_to_hT(li, gcol0):
            for G in range(4):
                ts = slice(G * 512, (G + 1) * 512)
                pm, bpm = ps_next()
                for c in range(8):
                    sq, bsq = sq_next()
                    S.op("act", ACTF(sq[:], xT[:, c, ts], AF.Square), [bx[c][G]], [bsq])
                    S.op("pe", MM(pm[:], ones[:], sq[:], start=(c == 0), stop=(c == 7)), [b_ones, bsq], [bpm])
                rt, brt = rstd_from_ms(pm[:], bpm)
                for c in range(8):
                    S.op("dve", STT(hT[:, c, ts], xT[:, c, ts], pc(li, gcol0 + c), rt[:], ALU.mult, ALU.mult),
                         [bx[c][G], b_pc, brt], [bh[c][G]])

        def proj_fm(wap_fn, G, kc, rhs_fn, rhs_bufs, wb):
            pz, bpz = ps_next()
            S.op("pe", [MM(pz[:], wap_fn(c), rhs_fn(c), start=(c == 0), stop=(c == kc - 1)) for c in range(kc)],
                 [wb] + rhs_bufs, [bpz])
            return pz, bpz

        def norm64_store(pz, bpz, gcol, dst, bdst):
            sq, bsq = sq_next()
            S.op("act", ACTF(sq[:], pz[:], AF.Square), [bpz], [bsq])
            pm, bpm = ps_next()
            S.op("pe", MM(pm[:], bones[:], sq[:]), [b_bo, bsq], [bpm])
            rt, brt = rstd_from_ms(pm[:], bpm)
            if isinstance(dst, tuple):
                for hf in range(2):
                    ps_ = slice(hf * 64, (hf + 1) * 64)
                    S.op("dve", STT(dst[hf], pz[ps_, :], gcol[ps_, :], rt[ps_, :], ALU.mult, ALU.mult), [bpz, b_pc, brt], [bdst])
            else:
                S.op("dve", STT(dst, pz[:], gcol, rt[:], ALU.mult, ALU.mult), [bpz, b_pc, brt], [bdst])

        def hT_rhs(G):
            return (lambda c: hT[:, c, G * 512:(G + 1) * 512]), [bh[c][G] for c in range(8)]

        dq = []

        def drain(n=1):
            for _ in range(n):
                if dq:
                    dq.pop(0)()

        def run_inproj(jobs):
            prev = None
            for (wfn, G, wb_, gcol, dst, bdst) in jobs:
                rf, rb = hT_rhs(G)
                pz, bpz = proj_fm(wfn, G, 8, rf, rb, wb_)
                if prev is not None:
                    norm64_store(*prev)
                prev = (pz, bpz, gcol, dst, bdst)
            if prev is not None:
                norm64_store(*prev)

        for li, l in enumerate(layers):
            lam_init = 0.8 - 0.6 * math.exp(-0.3 * l)
            S.dma("sp", "brow", [(brow[:], brow_d[li].partition_broadcast(128))], writes=[b_brow])
            S.op("dve", TT(lamt[:], brow[:, 136:200], brow[:, 200:264], ALU.mult), [b_brow], [b_lamt])
            S.op("dve", lambda e: e.reduce_sum(out=small[:, 0:1], in_=lamt[:], axis=AX.X), [b_lamt], [b_small])
            S.op("dve", TT(lamt[:], brow[:, 264:328], brow[:, 328:392], ALU.mult), [b_brow, b_lamt], [b_lamt])
            S.op("dve", lambda e: e.reduce_sum(out=small[:, 1:2], in_=lamt[:], axis=AX.X), [b_lamt, b_small], [b_small])
            S.op("act", ACTF(small[:, 2:4], small[:, 0:2], AF.Exp), [b_small], [b_small])
            S.op("dve", TT(small[:, 4:5], small[:, 3:4], small[:, 2:3], ALU.subtract), [b_small], [b_small])
            S.op("dve", TS(small[:, 4:5], small[:, 4:5], -lam_init, ALU.add), [b_small], [b_small])
            S.op("act", ACTF(esink[:], brow[:, 128:136], AF.Exp), [b_brow], [b_esink])
            S.op("dve", TS(G2[:], brow[:, 0:128], 1.0 - lam_init, ALU.mult), [b_brow], [b_G2])

            rmsnorm_to_hT(li, C_LN1)
            dump("hT", hT[:], [b for r in bh for b in r])
            chk("N1")

            with contextlib.ExitStack() as st_att:
                S.mark_barrier()
                oT = sb(st_att, "oT", [128, 8, SEQ], BF16)
                boT = [[S.newbuf() for _ in range(4)] for _ in range(8)]
                with contextlib.ExitStack() as st_ab:
                    pTsA = []; b_pTsA = []; sbtsA = []; b_sbtsA = []
                    rot["pT"] = 0; rot["sbt"] = 0

                    def pT_next():
                        i = rot["pT"] % 3
                        rot["pT"] += 1
                        return pTsA[i], b_pTsA[i]

                    def sbt_next():
                        i = rot["sbt"] % 2
                        rot["sbt"] += 1
                        return sbtsA[i], b_sbtsA[i]

                    with contextlib.ExitStack() as st_a:
                        pTsA += [sb(st_a, f"pTA{i}", [128, 512], BF16) for i in range(3)]; b_pTsA += [S.newbuf() for _ in range(3)]
                        sbtsA += [sb(st_a, f"sbtA{i}", [128, 512], F32) for i in range(2)]; b_sbtsA += [S.newbuf() for _ in range(2)]
                        qaT = sb(st_a, "qaT", [128, 4, SEQ], BF16); b_qa = [[S.newbuf() for _ in range(4)] for _ in range(4)]
                        kaT = sb(st_a, "kaT", [128, 2, SEQ], BF16); b_ka = [[S.newbuf() for _ in range(4)] for _ in range(2)]
                        vaA = sb(st_a, "vaA", [128, 16, 2, 66], BF16); b_va = [S.newbuf() for _ in range(16)]
                        biasA = sb(st_a, "biasA_s", [128, 2, 3, 512], F32); b_biasA = S.newbuf()
                        ostA = [sb(st_a, f"ostA{i}", [128, 512], BF16) for i in range(2)]; b_ostA = [[S.newbuf() for _ in range(8)] for _ in range(2)]
                        r4s = sb(st_a, "r4s", [128, 2, 4], F32); b_r4 = [S.newbuf() for _ in range(2)]
                        S.dma("sp", "biasA", [(biasA[:].rearrange("p a b c -> p (a b c)"), biasA_d)], writes=[b_biasA])
                        S.op("pool", MEMSET(vaA[:, :, :, 64:66], 1.0), [], b_va)
                        for half2 in range(2):
                            ws, wb = w_get(("qa", li, half2))
                            wv = wview(ws, 8, 256)
                            run_inproj([((lambda c, wv=wv, t2=t2: wv[:, c, t2 * 128:(t2 + 1) * 128]), G, wb, pc(li, C_GQA),
                                         qaT[:, half2 * 2 + t2, G * 512:(G + 1) * 512], b_qa[half2 * 2 + t2][G])
                                        for t2 in range(2) for G in range(4)])
                        ws, wb = w_get(("ka", li))
                        wv = wview(ws, 8, 256)
                        run_inproj([((lambda c, wv=wv, kap=kap: wv[:, c, kap * 128:(kap + 1) * 128]), G, wb, pc(li, C_GKA),
                                     kaT[:, kap, G * 512:(G + 1) * 512], b_ka[kap][G])
                                    for kap in range(2) for G in range(4)])
                        ws, wb = w_get(("va", li))
                        wv = wview(ws, 8, 128)
                        for t4 in range(4):
                            pv, bpv = ps_next()
                            for tq in range(4):
                                tt = t4 * 4 + tq
                                S.op("pe", [MM(pv[:, tq * 128:(tq + 1) * 128], hT[:, c, tt * 128:(tt + 1) * 128], wv[:, c, :],
                                               start=(c == 0), stop=(c == 7)) for c in range(8)],
                                     [wb] + [bh[c][t4] for c in range(8)], [bpv])
                            S.op("act", ACTF(vaA[:, t4 * 4:(t4 + 1) * 4, :, 0:64].rearrange("p t k e -> p (t k) e"),
                                             pv[:].rearrange("p (a e) -> p a e", e=64), AF.Copy),
                                 [bpv], b_va[t4 * 4:(t4 + 1) * 4])
                        dump("qaT", qaT[:], [b for r in b_qa for b in r])
                        dump("kaT", kaT[:], [b for r in b_ka for b in r])
                        dump("vaA", vaA[:], b_va)
                        chk("Ain")
                        stepsA = []
                        for i in range(16):
                            for kap in range(2):
                                js = [j for j in (i - 1, i, i + 1) if 0 <= j < 16]
                                for jn, j in enumerate(js):
                                    stepsA.append((i, kap, jn, j, len(js)))

                        def emit_qk_a(s_):
                            i, kap, jn, j, nj = stepsA[s_]
                            out = []
                            for hf in range(2):
                                pS, bpS = ps_next()
                                S.op("pe", MM(pS[:, 0:256],
                                              kaT[hf * 64:(hf + 1) * 64, kap, j * 128:(j + 1) * 128],
                                              qaT[hf * 64:(hf + 1) * 64, 2 * kap:2 * kap + 2, i * 128:(i + 1) * 128]),
                                     [b_ka[kap][j // 4], b_qa[2 * kap][i // 4], b_qa[2 * kap + 1][i // 4]], [bpS])
                                out.append((pS, bpS))
                            return out

                        def evacA_1(acc, bacc, kap):
                            def f():
                                accv = acc[:, 0:264].rearrange("p (a e) -> p a e", e=66)
                                r4 = r4s[:, kap, :]
                                S.op("dve", TT(r4, accv[:, :, 64], esink[:, kap * 4:(kap + 1) * 4], ALU.add), [bacc, b_esink], [b_r4[kap]])
                                S.op("dve", RECIP(r4, r4), [b_r4[kap]], [b_r4[kap]])
                            return f

                        def evacA_2(acc, bacc, kap, ost, bost):
                            def f():
                                accv = acc[:, 0:264].rearrange("p (a e) -> p a e", e=66)
                                for cb in range(4):
                                    h = 4 * kap + 2 * (cb % 2) + cb // 2
                                    if True:
                                        S.op("act", ACTF(ost[:, h * 64:(h + 1) * 64], accv[:, cb, 0:64], AF.Identity, scale=r4s[:, kap, cb:cb + 1]),
                                             [bacc, b_r4[kap]], [bost[h]])
                                    else:
                                        S.op("dve", TS(ost[:, h * 64:(h + 1) * 64], accv[:, cb, 0:64], r4s[:, kap, cb:cb + 1], ALU.mult),
                                             [bacc, b_r4[kap]], [bost[h]])
                            return f

                        def evacA_3(i, ost, bost):
                            def f():
                                ptr, bptr = ps_next()
                                S.op("pe", [MM(ptr[:, ft * 128:(ft + 1) * 128], ost[:, ft * 128:(ft + 1) * 128], ident[:]) for ft in range(4)],
                                     bost + [b_id], [bptr])
                                S.op("act", ACTF(oT[:, 0:4, i * 128:(i + 1) * 128], ptr[:].rearrange("p (a b) -> p a b", a=4), AF.Copy),
                                     [bptr], [boT[ft][i // 4] for ft in range(4)])
                            return f

                        pendA = {0: emit_qk_a(0)}
                        acc = bacc = None
                        for s_, (i, kap, jn, j, nj) in enumerate(stepsA):
                            if s_ + 1 < len(stepsA):
                                pendA[s_ + 1] = emit_qk_a(s_ + 1)
                            ost, bost = ostA[i % 2], b_ostA[i % 2]
                            if jn == 0:
                                acc, bacc = pbank[rot["accA"] % 3], bbank[rot["accA"] % 3]
                                rot["accA"] += 1
                            sbt, bsbt = sbt_next()
                            for hf, (pS, bpS) in enumerate(pendA.pop(s_)):
                                S.op("dve", STT(sbt[:, hf * 256:(hf + 1) * 256], pS[:, 0:256], 0.125,
                                                biasA[:, kap, j - i + 1, hf * 256:(hf + 1) * 256], ALU.mult, ALU.add),
                                     [bpS, b_biasA, bsbt], [bsbt])
                            pT, bpT = pT_next()
                            S.op("act", ACTF(pT[:], sbt[:], AF.Exp), [bsbt], [bpT])
                            S.op("pe", [MM(acc[:, cb * 66:(cb + 1) * 66], pT[:, cb * 128:(cb + 1) * 128], vaA[:, j, kap, :],
                                           start=(jn == 0 and cb == 0), stop=(jn == nj - 1))
                                        for cb in range(4)],
                                 [bpT, b_va[j]], [bacc])
                            drain(1)
                            if jn == nj - 1:
                                dq.append(evacA_1(acc, bacc, kap))
                                dq.append(evacA_2(acc, bacc, kap, ost, bost))
                                if kap == 1:
                                    dq.append(evacA_3(i, ost, bost))
                        drain(len(dq))
                    S.mark_barrier()
                    dump("oTa", oT[:, 0:4, :], [b for r in boT[0:4] for b in r])
                    chk("A")

                    with contextlib.ExitStack() as st_b:
                        pTs = [sb(st_b, f"pT{i}", [128, 1024], BF16) for i in range(3)]; b_pTs = [S.newbuf() for _ in range(3)]
                        stripHs = [sb(st_b, f"stripH{i}", [128, 1408], BF16) for i in range(2)]
                        stripLs = [sb(st_b, f"stripL{i}", [128, 1408], BF16) for i in range(2)]
                        b_stripHLs = [S.newbuf() for _ in range(2)]
                        qpad = sb(st_b, "qpad", [128, 2, SEQ], BF16)
                        kTb = sb(st_b, "kTb", [128, SEQ], BF16)
                        b_qk = [[S.newbuf() for _ in range(4)] for _ in range(2)]
                        vB = [sb(st_b, "vB0", [128, 16, 130], BF16)]
                        b_vB = [[S.newbuf() for _ in range(16)]]
                        strips = [sb(st_b, "strip0", [128, 1408], F32)]; b_strip = [S.newbuf()]
                        accS = sb(st_b, "accS", [128, 8, 130], F32); b_accS = S.newbuf()
                        o32 = sb(st_b, "o32", [128, 4, 128], F32); b_o32 = S.newbuf()
                        junk = sb(st_b, "junkB", [128, 128], F32); b_junk = S.newbuf()
                        ostB = [sb(st_b, f"ostB{i}", [128, 512], BF16) for i in range(2)]; b_ostB = [S.newbuf() for _ in range(2)]
                        sm = sb(st_b, "smB", [128, 32], F32); b_sm = S.newbuf()
                        S.op("pool", MEMSET(vB[0][:, :, 128:130], 1.0), [], b_vB[0])
                        S.op("pool", MEMSET(qpad[:].rearrange("p a t -> p (a t)"), 0.0), [], b_qk[0])
                        rnd = 0
                        for h in range(4):
                            par = 0
                            bqk, vb_, bvb = b_qk, vB[0], b_vB[0]
                            S.dma("sp", ("strip", 0), [(strips[0][:], strip_d[h])], writes=[b_strip[0]])
                            stripH, stripL, b_stripHL = stripHs[h % 2], stripLs[h % 2], b_stripHLs[h % 2]
                            S.op("act", ACTF(stripH[:], strips[0][:], AF.Identity, scale=8.0), [b_strip[0]], [b_stripHL])
                            S.op("dve", STT(stripL[:], strips[0][:], 8.0, stripH[:], ALU.mult, ALU.subtract), [b_strip[0], b_stripHL], [b_stripHL])
                            ws, wb = w_get(("qkb", li, h))
                            wv = wview(ws, 8, 256)
                            run_inproj([((lambda c, wv=wv, m=m: wv[:, c, m * 128:(m + 1) * 128]), G, wb, pc(li, C_GQB + m),
                                         ((qpad[0:64, 0, G * 512:(G + 1) * 512], qpad[64:128, 1, G * 512:(G + 1) * 512]) if m == 0
                                          else kTb[:, G * 512:(G + 1) * 512]), bqk[m][G])
                                        for m in range(2) for G in range(4)])
                            ws, wb = w_get(("vb", li, h))
                            wv = wview(ws, 8, 128)
                            for t4 in range(4):
                                pv, bpv = ps_next()
                                for tq in range(4):
                                    tt = t4 * 4 + tq
                                    S.op("pe", [MM(pv[:, tq * 128:(tq + 1) * 128], hT[:, c, tt * 128:(tt + 1) * 128], wv[:, c, :],
                                                   start=(c == 0), stop=(c == 7)) for c in range(8)],
                                         [wb] + [bh[c][t4] for c in range(8)], [bpv])
                                S.op("act", ACTF(vb_[:, t4 * 4:(t4 + 1) * 4, 0:128], pv[:].rearrange("p (a e) -> p a e", e=128), AF.Copy),
                                     [bpv], bvb[t4 * 4:(t4 + 1) * 4])
                            def evac_round(G, h=h):
                                S.op("act", ACTF(accS[:, 0:3, :], pbank[0][:, 0:390].rearrange("p (a e) -> p a e", e=130), AF.Copy), [bbank[0]], [b_accS])
                                S.op("dve", CP(accS[:, 3:6, :], pbank[1][:, 0:390].rearrange("p (a e) -> p a e", e=130)), [bbank[1], b_accS], [b_accS])
                                S.op("act", ACTF(accS[:, 6:8, :], pbank[2][:, 0:260].rearrange("p (a e) -> p a e", e=130), AF.Copy), [bbank[2], b_accS], [b_accS])
                                ost, bost = ostB[rot['rndB'] % 2], b_ostB[rot['rndB'] % 2]
                                rot['rndB'] += 1

                                def p1():
                                    S.op("dve", RECIP(sm[:, 0:8], accS[:, :, 128]), [b_accS, b_sm], [b_sm])
                                    S.op("dve", TS(sm[:, 8:12], sm[:, 4:8], small[:, 4:5], ALU.mult), [b_sm, b_small], [b_sm])

                                def p2():
                                    for b in range(4):
                                        S.op("dve", TS(o32[:, b, :], accS[:, b, 0:128], sm[:, b:b + 1], ALU.mult), [b_accS, b_sm, b_o32], [b_o32])

                                def p3():
                                    for b in range(4):
                                        S.op("dve", STT(o32[:, b, :], accS[:, 4 + b, 0:128], sm[:, 8 + b:9 + b], o32[:, b, :], ALU.mult, ALU.add),
                                             [b_accS, b_sm, b_o32], [b_o32])

                                def p4():
                                    for b in range(4):
                                        S.op("dve", lambda e, b=b: e.scalar_tensor_tensor(out=junk[:], in0=o32[:, b, :], scalar=1.0, in1=o32[:, b, :],
                                                                                          op0=ALU.mult, op1=ALU.mult, accum_out=sm[:, 12 + b:13 + b]),
                                             [b_o32, b_junk, b_sm], [b_junk, b_sm])
                                    S.op("act", ACTF(sm[:, 16:20], sm[:, 12:16], AF.Ln, bias=epsc[:, 0:1], scale=1.0 / 128), [b_sm, b_eps], [b_sm])
                                    S.op("act", ACTF(sm[:, 16:20], sm[:, 16:20], AF.Exp, scale=-0.5), [b_sm], [b_sm])

                                def p5():
                                    for b in range(4):
                                        S.op("dve", STT(ost[:, b * 128:(b + 1) * 128], o32[:, b, :], sm[:, 16 + b:17 + b], G2[:], ALU.mult, ALU.mult),
                                             [b_o32, b_sm, b_G2], [bost])

                                def p6():
                                    ptr, bptr = pbank[3], bbank[3]
                                    S.op("pe", [MM(ptr[:, b * 128:(b + 1) * 128], ost[:, b * 128:(b + 1) * 128], ident[:]) for b in range(4)],
                                         [bost, b_id], [bptr])
                                    S.op("dve", CP(oT[:, 4 + h, G * 512:(G + 1) * 512], ptr[:]), [bptr], [boT[4 + h][G]])

                                dq.extend([p1, p2, p3, p4, p5, p6])

                            pairs = [(G, cm, k) for G in range(4) for cm in range(2) for k in range(8)]

                            def emit_qk_pair(pi, bqk=bqk, stripH=stripH, stripL=stripL, b_stripHL=b_stripHL):
                                G, cm, k = pairs[pi]
                                buf = 2 + rot["spair"] % 2
                                rot["spair"] += 1
                                pp, bb = ppair[buf], [bbank[2 * buf], bbank[2 * buf + 1]]
                                dl0 = 2 * k - 4 * G
                                band = -2 <= dl0 <= 4
                                fns = []
                                for t in range(2):
                                    dst = pp[:, t * 512:(t + 1) * 512]
                                    fns.append(MM(dst, kTb[:, (2 * k + t) * 128:(2 * k + t + 1) * 128],
                                                  qpad[:, cm, G * 512:(G + 1) * 512], start=True, stop=not band))
                                    if band:
                                        w0 = (5 - (dl0 + t)) * 128
                                        fns.append(MM(dst, ident[:], stripH[:, w0:w0 + 512], start=False, stop=False))
                                        fns.append(MM(dst, ident[:], stripL[:, w0:w0 + 512], start=False, stop=True))
                                S.op("pe", fns, [bqk[1][k // 2], bqk[0][G]] + ([b_stripHL, b_id] if band else []), bb)
                                return pp, bb

                            pend = {0: emit_qk_pair(0)}
                            started = set()
                            for pi, (G, cm, k) in enumerate(pairs):
                                if pi + 1 < len(pairs):
                                    pend[pi + 1] = emit_qk_pair(pi + 1)
                                pp, bb = pend.pop(pi)
                                if cm == 0 and k == 0:
                                    started = set()
                                dl0 = 2 * k - 4 * G
                                ipt = rot["pT"] % 3
                                rot["pT"] += 1
                                pT2, bpT2 = pTs[ipt], b_pTs[ipt]
                                if -2 <= dl0 <= 4:
                                    S.op("act", ACTF(pT2[:], pp[:], AF.Exp, scale=0.125), bb, [bpT2])
                                else:
                                    col = h if dl0 < 0 else 4 + h
                                    S.op("act", ACTF(pT2[:], pp[:], AF.Exp, scale=0.125, bias=cfar[:, col:col + 1]), bb + [b_cfar], [bpT2])
                                fns = []
                                touched = []
                                for t in range(2):
                                    j = 2 * k + t
                                    for b in range(4):
                                        a = cm * 4 + b
                                        bank, off = a // 3, (a % 3) * 130
                                        fns.append(MM(pbank[bank][:, off:off + 130], pT2[:, t * 512 + b * 128:t * 512 + (b + 1) * 128], vb_[:, j, :],
                                                      start=(bank not in started), stop=(j == 15)))
                                        started.add(bank)
                                        if bbank[bank] not in touched:
                                            touched.append(bbank[bank])
                                S.op("pe", fns, [bpT2, bvb[2 * k], bvb[2 * k + 1]], touched)
                                if pi % 2 == 1:
                                    drain(1)
                                if cm == 1 and k == 7:
                                    assert not dq
                                    evac_round(G)
                        drain(len(dq))
                    S.mark_barrier()
                dump("oT", oT[:], [b for r in boT for b in r])
                chk("B")
                with contextlib.ExitStack() as st_m:
                    S.mark_barrier()
                    mixT = sb(st_m, "mixT", [128, 8, SEQ], BF16); b_mix = [[S.newbuf() for _ in range(4)] for _ in range(8)]
                    gts = [sb(st_m, f"gt{i}", [128, 512], F32) for i in range(4)]; b_gts = [S.newbuf() for _ in range(4)]
                    m1s = [sb(st_m, f"m1{i}", [128, 512], F32) for i in range(4)]; b_m1s = [S.newbuf() for _ in range(4)]
                    rr = 0
                    for ft in range(8):
                        wsg, wbg = w_get(("gate", li, ft))
                        wvg = wview(wsg, 8, 256)
                        wsp, wbp = w_get(("proj", li, ft))
                        wvp = wsp[:, 0:1024].rearrange("p (m c n) -> p m c n", m=2, c=4)
                        for G in range(4):
                            ts = slice(G * 512, (G + 1) * 512)
                            rf, rb = hT_rhs(G)
                            res = []
                            for m in range(2):
                                pg, bpg = proj_fm(lambda c: wvg[:, c, m * 128:(m + 1) * 128], G, 8, rf, rb, wbg)
                                gt, bgt = gts[rr % 4], b_gts[rr % 4]
                                S.op("act", ACTF(gt[:], pg[:], AF.Sigmoid, bias=pc(li, C_BG + m * 8 + ft)), [bpg, b_pc], [bgt])
                                dump("gt0", gt[:], [bgt])
                                pp, bpp = proj_fm(lambda c: wvp[:, m, c, :], G, 4, lambda c: oT[:, m * 4 + c, ts],
                                                  [boT[m * 4 + c][G] for c in range(4)], wbp)
                                m1, bm1 = m1s[rr % 4], b_m1s[rr % 4]
                                rr += 1
                                S.op("dve", TT(m1[:], pp[:], gt[:], ALU.mult), [bpp, bgt], [bm1])
                                res.append((m1, bm1))
                            S.op("dve", TT(mixT[:, ft, ts], res[0][0][:], res[1][0][:], ALU.add), [res[0][1], res[1][1]], [b_mix[ft][G]])
                    dump("mixT", mixT[:], [b for r in b_mix for b in r])
                    for f2 in range(4):
                        ws, wb = w_get(("wo", li, f2))
                        wv = wview(ws, 8, 256)
                        for t2 in range(2):
                            fo = f2 * 2 + t2
                            for G in range(4):
                                ts = slice(G * 512, (G + 1) * 512)
                                po, bpo = proj_fm(lambda c: wv[:, c, t2 * 128:(t2 + 1) * 128], G, 8, lambda c: mixT[:, c, ts],
                                                  [b_mix[c][G] for c in range(8)], wb)
                                S.op("dve", TT(xT[:, fo, ts], po[:], xT[:, fo, ts], ALU.add), [bpo, bx[fo][G]], [bx[fo][G]])
                S.mark_barrier()
            S.mark_barrier()

            dump("xmid", xT[:], [b for r in bx for b in r])
            chk("M")
            rmsnorm_to_hT(li, C_LN2)
            with contextlib.ExitStack() as st_f:
                S.mark_barrier()
                actT = sb(st_f, "actT", [128, 11, SEQ], BF16); b_act = [[S.newbuf() for _ in range(4)] for _ in range(11)]
                raws = [sb(st_f, f"raw{i}", [128, SEQ + 2], F32) for i in range(2)]; b_raws = [S.newbuf() for _ in range(2)]
                us = [sb(st_f, f"u{i}", [128, SEQ], F32) for i in range(2)]; b_us = [S.newbuf() for _ in range(2)]
                sg = sb(st_f, "sg", [128, SEQ], BF16); b_sg = S.newbuf()
                for i2 in range(2):
                    S.op("pool", MEMSET(raws[i2][:, 0:1], 0.0), [], [b_raws[i2]])
                    S.op("pool", MEMSET(raws[i2][:, SEQ + 1:SEQ + 2], 0.0), [b_raws[i2]], [b_raws[i2]])
                tcount = 0
                pending = [None]

                def make_conv(raw, braw, u, bu, ct, which, i):
                    def conv():
                        S.op("dve", STT(u[:], raw[:, 0:SEQ], pc(li, C_CW + ct), u[:], ALU.mult, ALU.add), [braw, bu, b_pc], [bu])
                        S.op("dve", STT(u[:], raw[:, 2:SEQ + 2], pc(li, C_CW + 88 + ct), u[:], ALU.mult, ALU.add), [braw, bu, b_pc], [bu])
                        if which == 1:
                            S.op("act", ACTF(sg[:], u[:], AF.Silu), [bu, b_sg], [b_sg])
                        else:
                            S.op("dve", TT(actT[:, i, :], sg[:], u[:], ALU.mult), [b_sg, bu], b_act[i])
                    return conv

                for half in range(2):
                    for i in range(11):
                        p = half * 11 + i
                        ws, wb = w_get(("up", li, p))
                        wv = wview(ws, 8, 256)
                        for which in (1, 0):
                            ct = p + 22 * which
                            raw, braw = raws[tcount % 2], b_raws[tcount % 2]
                            u, bu = us[tcount % 2], b_us[tcount % 2]
                            tcount += 1
                            for Gp in range(2):
                                kp = rot["fpair"] % 4
                                rot["fpair"] += 1
                                pp, bb = ppair[kp], [bbank[2 * kp], bbank[2 * kp + 1]]
                                for t in range(2):
                                    rf, rb = hT_rhs(2 * Gp + t)
                                    S.op("pe", [MM(pp[:, t * 512:(t + 1) * 512], wv[:, c, which * 128:(which + 1) * 128], rf(c),
                                                   start=(c == 0), stop=(c == 7)) for c in range(8)], [wb] + rb, [bb[t]])
                                S.op("act", ACTF(raw[:, 1 + Gp * 1024:1 + (Gp + 1) * 1024], pp[:], AF.Copy), bb + [braw], [braw])
                                S.op("act", ACTF(u[:, Gp * 1024:(Gp + 1) * 1024], pp[:], AF.Identity, scale=pc(li, C_CW + 44 + ct), bias=pc(li, C_CB + ct)),
                                     bb + [b_pc, bu], [bu])
                            if pending[0] is not None:
                                pending[0]()
                            pending[0] = make_conv(raw, braw, u, bu, ct, which, i)
                    if pending[0] is not None:
                        pending[0]()
                        pending[0] = None
                    for fo in range(8):
                        ws, wb = w_get(("down", li, half, fo))
                        wv = wview(ws, 11, 128)
                        for G in range(4):
                            ts = slice(G * 512, (G + 1) * 512)
                            po, bpo = proj_fm(lambda k: wv[:, k, :], G, 11, lambda k: actT[:, k, ts], [b_act[k][G] for k in range(11)], wb)
                            S.op("dve", TT(xT[:, fo, ts], po[:], xT[:, fo, ts], ALU.add), [bpo, bx[fo][G]], [bx[fo][G]])
            S.mark_barrier()

        S.enabled = True
        allx = [b for row in bx for b in row]
        for c in range(8):
            S.dma("sp", ("out", c), [(out_d[c * 128:(c + 1) * 128, :], xT[:, c, :])], reads=bx[c])
        S.wait_all("sp", allx)
        assert stop is not None or wst["idx"] == len(plan)
        S.emit()
    return nc


def _t5_bucket_np(rel):
    rel = np.asarray(rel, np.int64)
    half, max_exact = 16, 8
    ret = np.where(rel > 0, half, 0)
    n = np.abs(rel)
    nf = np.maximum(n, 1).astype(np.float32)
    large = max_exact + (np.log(nf / np.float32(max_exact)) / np.float32(math.log(128 / max_exact))
                         * np.float32(half - max_exact)).astype(np.int32)
    large = np.minimum(large, half - 1)
    return ret + np.where(n < max_exact, n, large)


def _host_layout(inp, layers):
    f32 = np.float32
    L = list(layers)
    g = lambda k: np.asarray(inp[k], f32)
    pcols = np.zeros((128, len(L) * NCOL), f32)
    brow = np.zeros((len(L), NBROW), f32)
    for li, l in enumerate(L):
        pcl = np.zeros((128, NCOL), f32)
        pcl[:, C_LN1:C_LN1 + 8] = g("ln1_g")[l].reshape(8, 128).T
        pcl[:, C_LN2:C_LN2 + 8] = g("ln2_g")[l].reshape(8, 128).T
        pcl[:, C_GQA] = np.tile(g("qn_a")[l], 2)
        pcl[:, C_GKA] = np.tile(g("kn_a")[l], 2)
        pcl[:, C_GQB] = np.tile(g("qn_b")[l], 2)
        pcl[:, C_GKB] = np.tile(g("kn_b")[l], 2)
        pcl[:, C_BG:C_BG + 16] = g("b_gate")[l].reshape(16, 128).T
        pcl[:, C_CW:C_CW + 132] = g("conv_w")[l].reshape(3, 44, 128).transpose(2, 0, 1).reshape(128, 132)
        pcl[:, C_CB:C_CB + 44] = g("conv_b")[l].reshape(44, 128).T
        pcols[:, li * NCOL:(li + 1) * NCOL] = pcl
        brow[li] = np.concatenate([g("subln_g")[l], g("sink")[l][SINK_PERM], g("lam_q1")[l], g("lam_k1")[l],
                                   g("lam_q2")[l], g("lam_k2")[l]])
    tab = g("rel_bias")
    cfar = np.concatenate([tab[15, 8:12], tab[31, 8:12]]).astype(f32)
    k = np.arange(128)[:, None]
    c = np.arange(1408)[None, :]
    bk = _t5_bucket_np(k - c + 640)
    stripB = np.stack([tab[bk, 8 + h] for h in range(4)]).astype(f32)
    q = np.arange(128)[None, :]
    biasA = np.zeros((128, 2, 3, 4, 128), f32)
    for di in range(3):
        rel = k + 128 * (di - 1) - q
        bkt = _t5_bucket_np(rel)
        ok = np.abs(rel) <= 128
        for kap in range(2):
            for cb in range(4):
                h = 4 * kap + 2 * (cb % 2) + cb // 2
                biasA[:, kap, di, cb, :] = np.where(ok, tab[bkt, h], f32(MASK_NEG))
    bones = np.zeros((128, 128), f32)
    bones[:64, :64] = 1.0 / 64
    bones[64:, 64:] = 1.0 / 64
    common = {
        "w_in": np.ascontiguousarray(g("w_in")[L]), "w_gate": np.ascontiguousarray(g("w_gate")[L]),
        "w_a_proj": np.ascontiguousarray(g("w_a_proj")[L]), "w_b_proj": np.ascontiguousarray(g("w_b_proj")[L]),
        "w_o": np.ascontiguousarray(g("w_o")[L]), "w_up": np.ascontiguousarray(g("w_up")[L]),
        "w_down": np.ascontiguousarray(g("w_down")[L]),
        "pcols": pcols, "brow": brow, "cfar": cfar, "stripB": stripB,
        "biasA": np.ascontiguousarray(biasA.reshape(128, 2 * 3 * 512)),
        "ident": np.eye(128, dtype=f32), "bones": bones,
    }
    return common


_PROGS = {}


def _run(layers, xT_list, inp):
    key = tuple(layers)
    if key not in _PROGS:
        _PROGS[key] = build_program(list(layers))
    nc = _PROGS[key]
    common = _host_layout(inp, layers)
    in_maps = [dict(common, xT=xT_list[b]) for b in range(NCORES)]
    res = run_bass_kernel_spmd(nc, in_maps, core_ids=list(range(NCORES)))
    return [np.asarray(r["outT"]) for r in res.results]


FUSED = True


def kernel(**inputs):
    x = np.asarray(inputs["x"], np.float32)
    xT = [np.ascontiguousarray(x[b].T) for b in range(NCORES)]
    if FUSED:
        xT = _run(range(DEPTH), xT, inputs)
    else:
        for l in range(DEPTH):
            xT = _run([l], xT, inputs)
    return np.stack([t.T for t in xT]).astype(np.float32)
```

```python
import contextlib
import math
import numpy as np
import concourse.bass as bass
import concourse.mybir as mybir
from concourse.bass_utils import run_bass_kernel_spmd

F32 = mybir.dt.float32
BF16 = mybir.dt.bfloat16
AF = mybir.ActivationFunctionType
ALU = mybir.AluOpType
AX = mybir.AxisListType

D = 1024
SEQ = 2048
DEPTH = 4
NCORES = 8
D_FF = 2816
IN_W = 2304
EPS = 1e-6
NCOL = 212
C_LN1, C_LN2, C_GQA, C_GKA, C_GQB, C_GKB, C_BG, C_CW, C_CB = 0, 8, 16, 17, 18, 19, 20, 36, 168
NBROW = 392
SINK_PERM = [0, 2, 1, 3, 4, 6, 5, 7]
MASK_NEG = -30000.0

EPOCH = 2000
ENGS = ("pe", "act", "dve", "pool", "sp")


class Buf:
    __slots__ = ("w", "r", "excl")

    def __init__(self, r=None):
        self.excl = False
        self.w = None
        self.r = dict(r) if r else {}


class Sched:
    def __init__(self, nc, stack):
        self.nc = nc
        self.stack = stack
        self.ops = {e: [] for e in ENGS}
        self.seq = {e: 0 for e in ENGS}
        self.sems = {}
        self.waited = {e: {} for e in ENGS}
        self.dma_cnt = {}
        self.barrier = {}
        self.enabled = True

    def sem(self, key):
        s = self.sems.get(key)
        if s is None:
            s = self.stack.enter_context(self.nc.semaphore("s_" + "_".join(str(k) for k in key)))
            self.sems[key] = s
        return s

    def newbuf(self):
        return Buf(self.barrier)

    def mark_barrier(self):
        b = {}
        for e in ENGS:
            n = self.seq[e]
            if n > 0:
                b[(e, (n - 1) // EPOCH)] = (n - 1) % EPOCH + 1
        for k, c in self.dma_cnt.items():
            b[k] = 16 * c
        self.barrier = b

    def _deps(self, eng, reads, writes):
        deps = {}
        for b in reads:
            if b.w is not None:
                k, v = b.w
                if deps.get(k, 0) < v:
                    deps[k] = v
        for b in writes:
            if b.w is not None:
                k, v = b.w
                if deps.get(k, 0) < v:
                    deps[k] = v
            for k, v in b.r.items():
                if deps.get(k, 0) < v:
                    deps[k] = v
        out = []
        wd = self.waited[eng]
        for k, v in deps.items():
            if eng == "pe" and k[0] == "pe":
                continue
            if wd.get(k, 0) >= v:
                continue
            wd[k] = v
            out.append((self.sem(k), v))
        return out

    def _commit(self, tok, reads, writes):
        k, v = tok
        for b in writes:
            b.w = tok
            b.r = {}
        for b in reads:
            if b.r.get(k, 0) < v:
                b.r[k] = v

    def op(self, eng, fns, reads=(), writes=()):
        if not self.enabled:
            return
        if not isinstance(fns, (list, tuple)):
            fns = [fns]
        if any(b.excl for b in reads):
            writes = list(writes) + [b for b in reads if b.excl]
            reads = [b for b in reads if not b.excl]
        waits = self._deps(eng, reads, writes)
        n = self.seq[eng]
        self.seq[eng] = n + 1
        key = (eng, n // EPOCH)
        tok = (key, n % EPOCH + 1)
        self.ops[eng].append((waits, fns, self.sem(key), 1))
        self._commit(tok, reads, writes)

    def dma(self, eng, slot, pairs, reads=(), writes=(), **kw):
        if not self.enabled:
            return
        waits = self._deps(eng, reads, writes)
        key = ("dma", slot)
        sem = self.sem(key)
        c = self.dma_cnt.get(key, 0)
        first = True
        for (o, i) in pairs:
            c += 1
            fn = (lambda e, o=o, i=i: e.dma_start(out=o, in_=i, **kw))
            self.ops[eng].append((waits if first else [], [fn], sem, 16))
            first = False
        self.dma_cnt[key] = c
        self._commit((key, 16 * c), reads, writes)

    def wait_all(self, eng, bufs):
        waits = self._deps(eng, bufs, bufs)
        self.ops[eng].append((waits, [], None, 0))

    def emit(self):
        objs = {"pe": "tensor", "act": "scalar", "dve": "vector", "pool": "gpsimd", "sp": "sync"}
        with self.nc.Block() as block:
            for e in ENGS:
                lst = self.ops[e]
                if not lst:
                    continue

                def body(engobj, lst=lst, e=e):
                    for waits, fns, sem, inc in lst:
                        attach = None
                        if fns and waits:
                            attach = waits[-1]
                            waits = waits[:-1]
                        for s, v in waits:
                            engobj.wait_ge(s, v)
                        n = len(fns)
                        for k, fn in enumerate(fns):
                            ins = fn(engobj)
                            if k == 0 and attach is not None:
                                ins._wait_ge(attach[0], attach[1])
                            if k == n - 1:
                                ins.then_inc(sem, inc)

                getattr(block, objs[e])(body)


def MM(out, lhsT, rhs, start=True, stop=True):
    return lambda e: e.matmul(out, lhsT=lhsT, rhs=rhs, start=start, stop=stop, skip_group_check=True)


def ACTF(out, in_, func, **kw):
    return lambda e: e.activation(out=out, in_=in_, func=func, **kw)


def STT(out, in0, scalar, in1, op0, op1):
    return lambda e: e.scalar_tensor_tensor(out=out, in0=in0, scalar=scalar, in1=in1, op0=op0, op1=op1)


def TT(out, in0, in1, op):
    return lambda e: e.tensor_tensor(out=out, in0=in0, in1=in1, op=op)


def TS(out, in0, s1, op0, s2=None, op1=None):
    if op1 is None:
        return lambda e: e.tensor_scalar(out=out, in0=in0, scalar1=s1, scalar2=None, op0=op0)
    return lambda e: e.tensor_scalar(out=out, in0=in0, scalar1=s1, scalar2=s2, op0=op0, op1=op1)


def CP(out, in_):
    return lambda e: e.tensor_copy(out=out, in_=in_)


def RECIP(out, in_):
    return lambda e: e.reciprocal(out=out, in_=in_)


def MEMSET(ap, v):
    return lambda e: e.memset(ap, v)


class _Stop(Exception):
    pass


def build_program(layers, stop=None, dumps=()):
    NL = len(layers)
    dbg_outs = {}
    nc = bass.Bass("TRN2", target_bir_lowering=False)

    def din(name, shape):
        return nc.dram_tensor(name, list(shape), F32, kind="ExternalInput").ap()

    xT_d = din("xT", [D, SEQ])
    w_in_d = din("w_in", [NL, D, IN_W])
    w_gate_d = din("w_gate", [NL, D, 2 * D])
    w_a_d = din("w_a_proj", [NL, 512, D])
    w_b_d = din("w_b_proj", [NL, 512, D])
    w_o_d = din("w_o", [NL, D, D])
    w_up_d = din("w_up", [NL, D, 2 * D_FF])
    w_dn_d = din("w_down", [NL, D_FF, D])
    pcols_d = din("pcols", [128, NL * NCOL])
    brow_d = din("brow", [NL, NBROW])
    cfar_d = din("cfar", [8])
    strip_d = din("stripB", [4, 128, 1408])
    biasA_d = din("biasA", [128, 2 * 3 * 512])
    ident_d = din("ident", [128, 128])
    bones_d = din("bones", [128, 128])
    out_d = nc.dram_tensor("outT", [D, SEQ], F32, kind="ExternalOutput").ap()

    with contextlib.ExitStack() as st:
        S = Sched(nc, st)

        uid = [0]

        def sb(stack, name, shape, dt):
            uid[0] += 1
            return stack.enter_context(nc.sbuf_tensor(f"{name}_{uid[0]}", list(shape), dt))

        chk_cnt = {}

        def chk(name):
            chk_cnt[name] = chk_cnt.get(name, 0) + 1
            if stop is None:
                return
            nm, _, n = stop.partition("#")
            if nm == name and chk_cnt[name] == int(n or 1):
                S.enabled = False

        def dump(name, ap, bufs):
            if name not in dumps or name in dbg_outs:
                return
            shp = list(ap.shape)
            dt_ = nc.dram_tensor("dbg_" + name, shp, ap.dtype, kind="ExternalOutput").ap()
            dbg_outs[name] = dt_
            S.dma("sp", ("dbg", name), [(dt_, ap)], reads=bufs)
            S.wait_all("sp", bufs)

        xT = sb(st, "xT_s", [128, 8, SEQ], F32)
        bx = [[S.newbuf() for _ in range(4)] for _ in range(8)]
        hT = sb(st, "hT_s", [128, 8, SEQ], BF16)
        bh = [[S.newbuf() for _ in range(4)] for _ in range(8)]
        pcols = sb(st, "pcols_s", [128, NL * NCOL], F32); b_pc = S.newbuf()
        brow = sb(st, "brow_s", [128, NBROW], F32); b_brow = S.newbuf()
        cfar = sb(st, "cfar_s", [128, 8], F32); b_cfar = S.newbuf()
        identf = sb(st, "identf", [128, 128], F32); b_idf = S.newbuf()
        bonesf = sb(st, "bonesf", [128, 128], F32); b_bof = S.newbuf()
        ident = sb(st, "ident_b", [128, 128], BF16); b_id = S.newbuf()
        bones = sb(st, "bones_b", [128, 128], BF16); b_bo = S.newbuf()
        ones = sb(st, "ones_b", [128, 128], BF16); b_ones = S.newbuf()
        epsc = sb(st, "epsc", [128, 1], F32); b_eps = S.newbuf()
        small = sb(st, "small", [128, 32], F32); b_small = S.newbuf()
        esink = sb(st, "esink", [128, 8], F32); b_esink = S.newbuf()
        G2 = sb(st, "G2", [128, 128], F32); b_G2 = S.newbuf()
        lamt = sb(st, "lamt", [128, 64], F32); b_lamt = S.newbuf()
        NWS = 4
        wslots = [sb(st, f"wslot{i}", [128, 2048], BF16) for i in range(NWS)]
        wbufs = [S.newbuf() for _ in range(NWS)]
        sqs = [sb(st, f"sq{i}", [128, 512], BF16) for i in range(2)]; b_sqs = [S.newbuf() for _ in range(2)]
        rts = [sb(st, f"rt{i}", [128, 512], F32) for i in range(2)]; b_rts = [S.newbuf() for _ in range(2)]

        ppair = [st.enter_context(nc.psum_tensor(f"pp{i}", [128, 1024], F32)) for i in range(4)]
        pbank = [ppair[i // 2][:, (i % 2) * 512:(i % 2 + 1) * 512] for i in range(8)]
        bbank = [S.newbuf() for _ in range(8)]
        for b_ in bbank:
            b_.excl = True
        rot = {"ps": 0, "sq": 0, "rt": 0, "accA": 0, "rndB": 0, "spair": 0, "fpair": 0}

        def ps_next():
            i = 3 + rot["ps"] % 5
            rot["ps"] += 1
            return pbank[i], bbank[i]

        def sq_next():
            i = rot["sq"] % 2
            rot["sq"] += 1
            return sqs[i], b_sqs[i]

        def rt_next():
            i = rot["rt"] % 2
            rot["rt"] += 1
            return rts[i], b_rts[i]

        def wview(slot, a, b):
            return slot[:, 0:a * b].rearrange("p (a b) -> p a b", a=a)

        def plan_layer(li):
            P = []

            def incols(c0, n):
                return w_in_d[li, :, c0:c0 + n].rearrange("(c p) n -> p c n", p=128)

            P.append((("qa", li, 0), lambda s: [(wview(s, 8, 256), incols(0, 256))]))
            P.append((("qa", li, 1), lambda s: [(wview(s, 8, 256), incols(256, 256))]))

            def kdup(s):
                v = s[:, 0:2048].rearrange("p (c k d e) -> p c k d e", c=8, k=2, d=2)
                return [(v[:, :, k, dd, :], incols(512 + k * 64, 64)) for k in range(2) for dd in range(2)]

            P.append((("ka", li), kdup))
            P.append((("va", li), lambda s: [(wview(s, 8, 128), incols(640, 128))]))
            for h in range(4):
                def qk(s, h=h):
                    v = wview(s, 8, 256)
                    return [(v[:, :, 0:128], incols(768 + h * 128, 128)),
                            (v[:, :, 128:256], incols(1280 + h * 128, 128))]
                P.append((("qkb", li, h), qk))
                P.append((("vb", li, h), lambda s, h=h: [(wview(s, 8, 128), incols(1792 + h * 128, 128))]))
            for ft in range(8):
                def gate(s, ft=ft):
                    v = wview(s, 8, 256)
                    g = lambda c0: w_gate_d[li, :, c0:c0 + 128].rearrange("(c p) n -> p c n", p=128)
                    return [(v[:, :, 0:128], g(ft * 128)), (v[:, :, 128:256], g(D + ft * 128))]
                P.append((("gate", li, ft), gate))

                def proj(s, ft=ft):
                    v = s[:, 0:1024].rearrange("p (m c n) -> p m c n", m=2, c=4)
                    a = w_a_d[li, :, ft * 128:(ft + 1) * 128].rearrange("(c p) n -> p c n", p=128)
                    b = w_b_d[li, :, ft * 128:(ft + 1) * 128].rearrange("(c p) n -> p c n", p=128)
                    return [(v[:, 0], a), (v[:, 1], b)]
                P.append((("proj", li, ft), proj))
            for f2 in range(4):
                P.append((("wo", li, f2), lambda s, f2=f2: [(wview(s, 8, 256), w_o_d[li, :, f2 * 256:(f2 + 1) * 256].rearrange("(c p) n -> p c n", p=128))]))
            for half in range(2):
                for i in range(11):
                    p = half * 11 + i

                    def up(s, p=p):
                        v = wview(s, 8, 256)
                        u = lambda c0: w_up_d[li, :, c0:c0 + 128].rearrange("(c p) n -> p c n", p=128)
                        return [(v[:, :, 0:128], u(p * 128)), (v[:, :, 128:256], u(D_FF + p * 128))]
                    P.append((("up", li, p), up))
                for fo in range(8):
                    def down(s, half=half, fo=fo):
                        v = wview(s, 11, 128)
                        src = w_dn_d[li, half * 1408:(half + 1) * 1408, fo * 128:(fo + 1) * 128].rearrange("(k p) n -> p k n", p=128)
                        return [(v, src)]
                    P.append((("down", li, half, fo), down))
            return P

        plan = []
        for li in range(NL):
            plan += plan_layer(li)
        wst = {"idx": 0, "issued": 0}
        AHEAD = NWS - 2

        def w_issue_upto(k):
            while wst["issued"] <= min(k, len(plan) - 1):
                t = wst["issued"]
                slot = t % NWS
                S.dma("pool", ("w", slot, plan[t][0][1]), plan[t][1](wslots[slot]), writes=[wbufs[slot]])
                wst["issued"] += 1

        def w_get(key):
            t = wst["idx"]
            assert plan[t][0] == key, (plan[t][0], key)
            w_issue_upto(t + AHEAD)
            wst["idx"] += 1
            return wslots[t % NWS], wbufs[t % NWS]

        for c in range(8):
            S.dma("sp", ("x", c), [(xT[:, c, :], xT_d[c * 128:(c + 1) * 128, :])], writes=bx[c])
        S.dma("sp", "c0", [(pcols[:], pcols_d)], writes=[b_pc])
        S.dma("sp", "c1", [(cfar[:], cfar_d.partition_broadcast(128))], writes=[b_cfar])
        S.dma("sp", "c2", [(identf[:], ident_d)], writes=[b_idf])
        S.dma("sp", "c3", [(bonesf[:], bones_d)], writes=[b_bof])
        S.op("dve", CP(ident[:], identf[:]), [b_idf], [b_id])
        S.op("dve", CP(bones[:], bonesf[:]), [b_bof], [b_bo])
        S.op("pool", MEMSET(ones[:], 1.0 / D), [], [b_ones])
        S.op("pool", MEMSET(epsc[:], EPS), [], [b_eps])
        w_issue_upto(AHEAD - 1)

        def pc(li, col, n=1):
            return pcols[:, li * NCOL + col: li * NCOL + col + n]

        def rstd_from_ms(pm, bpm, scale=1.0):
            rt, brt = rt_next()
            S.op("act", ACTF(rt[:], pm, AF.Ln, bias=epsc[:, 0:1], scale=scale), [bpm, b_eps], [brt])
            S.op("act", ACTF(rt[:], rt[:], AF.Exp, scale=-0.5), [brt], [brt])
            return rt, brt

        def rmsnorm_to_hT(li, gcol0):
            for G in range(4):
                ts = slice(G * 512, (G + 1) * 512)
                pm, bpm = ps_next()
                for c in range(8):
                    sq, bsq = sq_next()
                    S.op("act", ACTF(sq[:], xT[:, c, ts], AF.Square), [bx[c][G]], [bsq])
                    S.op("pe", MM(pm[:], ones[:], sq[:], start=(c == 0), stop=(c == 7)), [b_ones, bsq], [bpm])
                rt, brt = rstd_from_ms(pm[:], bpm)
                for c in range(8):
                    S.op("dve", STT(hT[:, c, ts], xT[:, c, ts], pc(li, gcol0 + c), rt[:], ALU.mult, ALU.mult),
                         [bx[c][G], b_pc, brt], [bh[c][G]])

        def proj_fm(wap_fn, G, kc, rhs_fn, rhs_bufs, wb):
            pz, bpz = ps_next()
            S.op("pe", [MM(pz[:], wap_fn(c), rhs_fn(c), start=(c == 0), stop=(c == kc - 1)) for c in range(kc)],
                 [wb] + rhs_bufs, [bpz])
            return pz, bpz

        def norm64_store(pz, bpz, gcol, dst, bdst):
            sq, bsq = sq_next()
            S.op("act", ACTF(sq[:], pz[:], AF.Square), [bpz], [bsq])
            pm, bpm = ps_next()
            S.op("pe", MM(pm[:], bones[:], sq[:]), [b_bo, bsq], [bpm])
            rt, brt = rstd_from_ms(pm[:], bpm)
            if isinstance(dst, tuple):
                for hf in range(2):
                    ps_ = slice(hf * 64, (hf + 1) * 64)
                    S.op("dve", STT(dst[hf], pz[ps_, :], gcol[ps_, :], rt[ps_, :], ALU.mult, ALU.mult), [bpz, b_pc, brt], [bdst])
            else:
                S.op("dve", STT(dst, pz[:], gcol, rt[:], ALU.mult, ALU.mult), [bpz, b_pc, brt], [bdst])

        def hT_rhs(G):
            return (lambda c: hT[:, c, G * 512:(G + 1) * 512]), [bh[c][G] for c in range(8)]

        dq = []

        def drain(n=1):
            for _ in range(n):
                if dq:
                    dq.pop(0)()

        def run_inproj(jobs):
            prev = None
            for (wfn, G, wb_, gcol, dst, bdst) in jobs:
                rf, rb = hT_rhs(G)
                pz, bpz = proj_fm(wfn, G, 8, rf, rb, wb_)
                if prev is not None:
                    norm64_store(*prev)
                prev = (pz, bpz, gcol, dst, bdst)
            if prev is not None:
                norm64_store(*prev)

        for li, l in enumerate(layers):
            lam_init = 0.8 - 0.6 * math.exp(-0.3 * l)
            S.dma("sp", "brow", [(brow[:], brow_d[li].partition_broadcast(128))], writes=[b_brow])
            S.op("dve", TT(lamt[:], brow[:, 136:200], brow[:, 200:264], ALU.mult), [b_brow], [b_lamt])
            S.op("dve", lambda e: e.reduce_sum(out=small[:, 0:1], in_=lamt[:], axis=AX.X), [b_lamt], [b_small])
            S.op("dve", TT(lamt[:], brow[:, 264:328], brow[:, 328:392], ALU.mult), [b_brow, b_lamt], [b_lamt])
            S.op("dve", lambda e: e.reduce_sum(out=small[:, 1:2], in_=lamt[:], axis=AX.X), [b_lamt, b_small], [b_small])
            S.op("act", ACTF(small[:, 2:4], small[:, 0:2], AF.Exp), [b_small], [b_small])
            S.op("dve", TT(small[:, 4:5], small[:, 3:4], small[:, 2:3], ALU.subtract), [b_small], [b_small])
            S.op("dve", TS(small[:, 4:5], small[:, 4:5], -lam_init, ALU.add), [b_small], [b_small])
            S.op("act", ACTF(esink[:], brow[:, 128:136], AF.Exp), [b_brow], [b_esink])
            S.op("dve", TS(G2[:], brow[:, 0:128], 1.0 - lam_init, ALU.mult), [b_brow], [b_G2])

            rmsnorm_to_hT(li, C_LN1)
            dump("hT", hT[:], [b for r in bh for b in r])
            chk("N1")

            with contextlib.ExitStack() as st_att:
                S.mark_barrier()
                oT = sb(st_att, "oT", [128, 8, SEQ], BF16)
                boT = [[S.newbuf() for _ in range(4)] for _ in range(8)]
                with contextlib.ExitStack() as st_ab:
                    pTsA = []; b_pTsA = []; sbtsA = []; b_sbtsA = []
                    rot["pT"] = 0; rot["sbt"] = 0

                    def pT_next():
                        i = rot["pT"] % 3
                        rot["pT"] += 1
                        return pTsA[i], b_pTsA[i]

                    def sbt_next():
                        i = rot["sbt"] % 2
                        rot["sbt"] += 1
                        return sbtsA[i], b_sbtsA[i]

                    with contextlib.ExitStack() as st_a:
                        pTsA += [sb(st_a, f"pTA{i}", [128, 512], BF16) for i in range(3)]; b_pTsA += [S.newbuf() for _ in range(3)]
                        sbtsA += [sb(st_a, f"sbtA{i}", [128, 512], F32) for i in range(2)]; b_sbtsA += [S.newbuf() for _ in range(2)]
                        qaT = sb(st_a, "qaT", [128, 4, SEQ], BF16); b_qa = [[S.newbuf() for _ in range(4)] for _ in range(4)]
                        kaT = sb(st_a, "kaT", [128, 2, SEQ], BF16); b_ka = [[S.newbuf() for _ in range(4)] for _ in range(2)]
                        vaA = sb(st_a, "vaA", [128, 16, 2, 66], BF16); b_va = [S.newbuf() for _ in range(16)]
                        biasA = sb(st_a, "biasA_s", [128, 2, 3, 512], F32); b_biasA = S.newbuf()
                        ostA = [sb(st_a, f"ostA{i}", [128, 512], BF16) for i in range(2)]; b_ostA = [[S.newbuf() for _ in range(8)] for _ in range(2)]
                        r4s = sb(st_a, "r4s", [128, 2, 4], F32); b_r4 = [S.newbuf() for _ in range(2)]
                        S.dma("sp", "biasA", [(biasA[:].rearrange("p a b c -> p (a b c)"), biasA_d)], writes=[b_biasA])
                        S.op("pool", MEMSET(vaA[:, :, :, 64:66], 1.0), [], b_va)
                        for half2 in range(2):
                            ws, wb = w_get(("qa", li, half2))
                            wv = wview(ws, 8, 256)
                            run_inproj([((lambda c, wv=wv, t2=t2: wv[:, c, t2 * 128:(t2 + 1) * 128]), G, wb, pc(li, C_GQA),
                                         qaT[:, half2 * 2 + t2, G * 512:(G + 1) * 512], b_qa[half2 * 2 + t2][G])
                                        for t2 in range(2) for G in range(4)])
                        ws, wb = w_get(("ka", li))
                        wv = wview(ws, 8, 256)
                        run_inproj([((lambda c, wv=wv, kap=kap: wv[:, c, kap * 128:(kap + 1) * 128]), G, wb, pc(li, C_GKA),
                                     kaT[:, kap, G * 512:(G + 1) * 512], b_ka[kap][G])
                                    for kap in range(2) for G in range(4)])
                        ws, wb = w_get(("va", li))
                        wv = wview(ws, 8, 128)
                        for t4 in range(4):
                            pv, bpv = ps_next()
                            for tq in range(4):
                                tt = t4 * 4 + tq
                                S.op("pe", [MM(pv[:, tq * 128:(tq + 1) * 128], hT[:, c, tt * 128:(tt + 1) * 128], wv[:, c, :],
                                               start=(c == 0), stop=(c == 7)) for c in range(8)],
                                     [wb] + [bh[c][t4] for c in range(8)], [bpv])
                            S.op("act", ACTF(vaA[:, t4 * 4:(t4 + 1) * 4, :, 0:64].rearrange("p t k e -> p (t k) e"),
                                             pv[:].rearrange("p (a e) -> p a e", e=64), AF.Copy),
                                 [bpv], b_va[t4 * 4:(t4 + 1) * 4])
                        dump("qaT", qaT[:], [b for r in b_qa for b in r])
                        dump("kaT", kaT[:], [b for r in b_ka for b in r])
                        dump("vaA", vaA[:], b_va)
                        chk("Ain")
                        stepsA = []
                        for i in range(16):
                            for kap in range(2):
                                js = [j for j in (i - 1, i, i + 1) if 0 <= j < 16]
                                for jn, j in enumerate(js):
                                    stepsA.append((i, kap, jn, j, len(js)))

                        def emit_qk_a(s_):
                            i, kap, jn, j, nj = stepsA[s_]
                            out = []
                            for hf in range(2):
                                pS, bpS = ps_next()
                                S.op("pe", MM(pS[:, 0:256],
                                              kaT[hf * 64:(hf + 1) * 64, kap, j * 128:(j + 1) * 128],
                                              qaT[hf * 64:(hf + 1) * 64, 2 * kap:2 * kap + 2, i * 128:(i + 1) * 128]),
                                     [b_ka[kap][j // 4], b_qa[2 * kap][i // 4], b_qa[2 * kap + 1][i // 4]], [bpS])
                                out.append((pS, bpS))
                            return out

                        def evacA_1(acc, bacc, kap):
                            def f():
                                accv = acc[:, 0:264].rearrange("p (a e) -> p a e", e=66)
                                r4 = r4s[:, kap, :]
                                S.op("dve", TT(r4, accv[:, :, 64], esink[:, kap * 4:(kap + 1) * 4], ALU.add), [bacc, b_esink], [b_r4[kap]])
                                S.op("dve", RECIP(r4, r4), [b_r4[kap]], [b_r4[kap]])
                            return f

                        def evacA_2(acc, bacc, kap, ost, bost):
                            def f():
                                accv = acc[:, 0:264].rearrange("p (a e) -> p a e", e=66)
                                for cb in range(4):
                                    h = 4 * kap + 2 * (cb % 2) + cb // 2
                                    if True:
                                        S.op("act", ACTF(ost[:, h * 64:(h + 1) * 64], accv[:, cb, 0:64], AF.Identity, scale=r4s[:, kap, cb:cb + 1]),
                                             [bacc, b_r4[kap]], [bost[h]])
                                    else:
                                        S.op("dve", TS(ost[:, h * 64:(h + 1) * 64], accv[:, cb, 0:64], r4s[:, kap, cb:cb + 1], ALU.mult),
                                             [bacc, b_r4[kap]], [bost[h]])
                            return f

                        def evacA_3(i, ost, bost):
                            def f():
                                ptr, bptr = ps_next()
                                S.op("pe", [MM(ptr[:, ft * 128:(ft + 1) * 128], ost[:, ft * 128:(ft + 1) * 128], ident[:]) for ft in range(4)],
                                     bost + [b_id], [bptr])
                                S.op("act", ACTF(oT[:, 0:4, i * 128:(i + 1) * 128], ptr[:].rearrange("p (a b) -> p a b", a=4), AF.Copy),
                                     [bptr], [boT[ft][i // 4] for ft in range(4)])
                            return f

                        pendA = {0: emit_qk_a(0)}
                        acc = bacc = None
                        for s_, (i, kap, jn, j, nj) in enumerate(stepsA):
                            if s_ + 1 < len(stepsA):
                                pendA[s_ + 1] = emit_qk_a(s_ + 1)
                            ost, bost = ostA[i % 2], b_ostA[i % 2]
                            if jn == 0:
                                acc, bacc = pbank[rot["accA"] % 3], bbank[rot["accA"] % 3]
                                rot["accA"] += 1
                            sbt, bsbt = sbt_next()
                            for hf, (pS, bpS) in enumerate(pendA.pop(s_)):
                                S.op("dve", STT(sbt[:, hf * 256:(hf + 1) * 256], pS[:, 0:256], 0.125,
                                                biasA[:, kap, j - i + 1, hf * 256:(hf + 1) * 256], ALU.mult, ALU.add),
                                     [bpS, b_biasA, bsbt], [bsbt])
                            pT, bpT = pT_next()
                            S.op("act", ACTF(pT[:], sbt[:], AF.Exp), [bsbt], [bpT])
                            S.op("pe", [MM(acc[:, cb * 66:(cb + 1) * 66], pT[:, cb * 128:(cb + 1) * 128], vaA[:, j, kap, :],
                                           start=(jn == 0 and cb == 0), stop=(jn == nj - 1))
                                        for cb in range(4)],
                                 [bpT, b_va[j]], [bacc])
                            drain(1)
                            if jn == nj - 1:
                                dq.append(evacA_1(acc, bacc, kap))
                                dq.append(evacA_2(acc, bacc, kap, ost, bost))
                                if kap == 1:
                                    dq.append(evacA_3(i, ost, bost))
                        drain(len(dq))
                    S.mark_barrier()
                    dump("oTa", oT[:, 0:4, :], [b for r in boT[0:4] for b in r])
                    chk("A")

                    with contextlib.ExitStack() as st_b:
                        pTs = [sb(st_b, f"pT{i}", [128, 1024], BF16) for i in range(3)]; b_pTs = [S.newbuf() for _ in range(3)]
                        stripHs = [sb(st_b, f"stripH{i}", [128, 1408], BF16) for i in range(2)]
                        stripLs = [sb(st_b, f"stripL{i}", [128, 1408], BF16) for i in range(2)]
                        b_stripHLs = [S.newbuf() for _ in range(2)]
                        qpad = sb(st_b, "qpad", [128, 2, SEQ], BF16)
                        kTb = sb(st_b, "kTb", [128, SEQ], BF16)
                        b_qk = [[S.newbuf() for _ in range(4)] for _ in range(2)]
                        vB = [sb(st_b, "vB0", [128, 16, 130], BF16)]
                        b_vB = [[S.newbuf() for _ in range(16)]]
                        strips = [sb(st_b, "strip0", [128, 1408], F32)]; b_strip = [S.newbuf()]
                        accS = sb(st_b, "accS", [128, 8, 130], F32); b_accS = S.newbuf()
                        o32 = sb(st_b, "o32", [128, 4, 128], F32); b_o32 = S.newbuf()
                        junk = sb(st_b, "junkB", [128, 128], F32); b_junk = S.newbuf()
                        ostB = [sb(st_b, f"ostB{i}", [128, 512], BF16) for i in range(2)]; b_ostB = [S.newbuf() for _ in range(2)]
                        sm = sb(st_b, "smB", [128, 32], F32); b_sm = S.newbuf()
                        S.op("pool", MEMSET(vB[0][:, :, 128:130], 1.0), [], b_vB[0])
                        S.op("pool", MEMSET(qpad[:].rearrange("p a t -> p (a t)"), 0.0), [], b_qk[0])
                        rnd = 0
                        for h in range(4):
                            par = 0
                            bqk, vb_, bvb = b_qk, vB[0], b_vB[0]
                            S.dma("sp", ("strip", 0), [(strips[0][:], strip_d[h])], writes=[b_strip[0]])
                            stripH, stripL, b_stripHL = stripHs[h % 2], stripLs[h % 2], b_stripHLs[h % 2]
                            S.op("act", ACTF(stripH[:], strips[0][:], AF.Identity, scale=8.0), [b_strip[0]], [b_stripHL])
                            S.op("dve", STT(stripL[:], strips[0][:], 8.0, stripH[:], ALU.mult, ALU.subtract), [b_strip[0], b_stripHL], [b_stripHL])
                            ws, wb = w_get(("qkb", li, h))
                            wv = wview(ws, 8, 256)
                            run_inproj([((lambda c, wv=wv, m=m: wv[:, c, m * 128:(m + 1) * 128]), G, wb, pc(li, C_GQB + m),
                                         ((qpad[0:64, 0, G * 512:(G + 1) * 512], qpad[64:128, 1, G * 512:(G + 1) * 512]) if m == 0
                                          else kTb[:, G * 512:(G + 1) * 512]), bqk[m][G])
                                        for m in range(2) for G in range(4)])
                            ws, wb = w_get(("vb", li, h))
                            wv = wview(ws, 8, 128)
                            for t4 in range(4):
                                pv, bpv = ps_next()
                                for tq in range(4):
                                    tt = t4 * 4 + tq
                                    S.op("pe", [MM(pv[:, tq * 128:(tq + 1) * 128], hT[:, c, tt * 128:(tt + 1) * 128], wv[:, c, :],
                                                   start=(c == 0), stop=(c == 7)) for c in range(8)],
                                         [wb] + [bh[c][t4] for c in range(8)], [bpv])
                                S.op("act", ACTF(vb_[:, t4 * 4:(t4 + 1) * 4, 0:128], pv[:].rearrange("p (a e) -> p a e", e=128), AF.Copy),
                                     [bpv], bvb[t4 * 4:(t4 + 1) * 4])
                            def evac_round(G, h=h):
                                S.op("act", ACTF(accS[:, 0:3, :], pbank[0][:, 0:390].rearrange("p (a e) -> p a e", e=130), AF.Copy), [bbank[0]], [b_accS])
                                S.op("dve", CP(accS[:, 3:6, :], pbank[1][:, 0:390].rearrange("p (a e) -> p a e", e=130)), [bbank[1], b_accS], [b_accS])
                                S.op("act", ACTF(accS[:, 6:8, :], pbank[2][:, 0:260].rearrange("p (a e) -> p a e", e=130), AF.Copy), [bbank[2], b_accS], [b_accS])
                                ost, bost = ostB[rot['rndB'] % 2], b_ostB[rot['rndB'] % 2]
                                rot['rndB'] += 1

                                def p1():
                                    S.op("dve", RECIP(sm[:, 0:8], accS[:, :, 128]), [b_accS, b_sm], [b_sm])
                                    S.op("dve", TS(sm[:, 8:12], sm[:, 4:8], small[:, 4:5], ALU.mult), [b_sm, b_small], [b_sm])

                                def p2():
                                    for b in range(4):
                                        S.op("dve", TS(o32[:, b, :], accS[:, b, 0:128], sm[:, b:b + 1], ALU.mult), [b_accS, b_sm, b_o32], [b_o32])

                                def p3():
                                    for b in range(4):
                                        S.op("dve", STT(o32[:, b, :], accS[:, 4 + b, 0:128], sm[:, 8 + b:9 + b], o32[:, b, :], ALU.mult, ALU.add),
                                             [b_accS, b_sm, b_o32], [b_o32])

                                def p4():
                                    for b in range(4):
                                        S.op("dve", lambda e, b=b: e.scalar_tensor_tensor(out=junk[:], in0=o32[:, b, :], scalar=1.0, in1=o32[:, b, :],
                                                                                          op0=ALU.mult, op1=ALU.mult, accum_out=sm[:, 12 + b:13 + b]),
                                             [b_o32, b_junk, b_sm], [b_junk, b_sm])
                                    S.op("act", ACTF(sm[:, 16:20], sm[:, 12:16], AF.Ln, bias=epsc[:, 0:1], scale=1.0 / 128), [b_sm, b_eps], [b_sm])
                                    S.op("act", ACTF(sm[:, 16:20], sm[:, 16:20], AF.Exp, scale=-0.5), [b_sm], [b_sm])

                                def p5():
                                    for b in range(4):
                                        S.op("dve", STT(ost[:, b * 128:(b + 1) * 128], o32[:, b, :], sm[:, 16 + b:17 + b], G2[:], ALU.mult, ALU.mult),
                                             [b_o32, b_sm, b_G2], [bost])

                                def p6():
                                    ptr, bptr = pbank[3], bbank[3]
                                    S.op("pe", [MM(ptr[:, b * 128:(b + 1) * 128], ost[:, b * 128:(b + 1) * 128], ident[:]) for b in range(4)],
                                         [bost, b_id], [bptr])
                                    S.op("dve", CP(oT[:, 4 + h, G * 512:(G + 1) * 512], ptr[:]), [bptr], [boT[4 + h][G]])

                                dq.extend([p1, p2, p3, p4, p5, p6])

                            pairs = [(G, cm, k) for G in range(4) for cm in range(2) for k in range(8)]

                            def emit_qk_pair(pi, bqk=bqk, stripH=stripH, stripL=stripL, b_stripHL=b_stripHL):
                                G, cm, k = pairs[pi]
                                buf = 2 + rot["spair"] % 2
                                rot["spair"] += 1
                                pp, bb = ppair[buf], [bbank[2 * buf], bbank[2 * buf + 1]]
                                dl0 = 2 * k - 4 * G
                                band = -2 <= dl0 <= 4
                                fns = []
                                for t in range(2):
                                    dst = pp[:, t * 512:(t + 1) * 512]
                                    fns.append(MM(dst, kTb[:, (2 * k + t) * 128:(2 * k + t + 1) * 128],
                                                  qpad[:, cm, G * 512:(G + 1) * 512], start=True, stop=not band))
                                    if band:
                                        w0 = (5 - (dl0 + t)) * 128
                                        fns.append(MM(dst, ident[:], stripH[:, w0:w0 + 512], start=False, stop=True))
                                S.op("pe", fns, [bqk[1][k // 2], bqk[0][G]] + ([b_stripHL, b_id] if band else []), bb)
                                return pp, bb

                            pend = {0: emit_qk_pair(0)}
                            started = set()
                            for pi, (G, cm, k) in enumerate(pairs):
                                if pi + 1 < len(pairs):
                                    pend[pi + 1] = emit_qk_pair(pi + 1)
                                pp, bb = pend.pop(pi)
                                if cm == 0 and k == 0:
                                    started = set()
                                dl0 = 2 * k - 4 * G
                                ipt = rot["pT"] % 3
                                rot["pT"] += 1
                                pT2, bpT2 = pTs[ipt], b_pTs[ipt]
                                if -2 <= dl0 <= 4:
                                    S.op("act", ACTF(pT2[:], pp[:], AF.Exp, scale=0.125), bb, [bpT2])
                                else:
                                    col = h if dl0 < 0 else 4 + h
                                    S.op("act", ACTF(pT2[:], pp[:], AF.Exp, scale=0.125, bias=cfar[:, col:col + 1]), bb + [b_cfar], [bpT2])
                                fns = []
                                touched = []
                                for t in range(2):
                                    j = 2 * k + t
                                    for b in range(4):
                                        a = cm * 4 + b
                                        bank, off = a // 3, (a % 3) * 130
                                        fns.append(MM(pbank[bank][:, off:off + 130], pT2[:, t * 512 + b * 128:t * 512 + (b + 1) * 128], vb_[:, j, :],
                                                      start=(bank not in started), stop=(j == 15)))
                                        started.add(bank)
                                        if bbank[bank] not in touched:
                                            touched.append(bbank[bank])
                                S.op("pe", fns, [bpT2, bvb[2 * k], bvb[2 * k + 1]], touched)
                                if pi % 2 == 1:
                                    drain(1)
                                if cm == 1 and k == 7:
                                    assert not dq
                                    evac_round(G)
                        drain(len(dq))
                    S.mark_barrier()
                dump("oT", oT[:], [b for r in boT for b in r])
                chk("B")
                with contextlib.ExitStack() as st_m:
                    S.mark_barrier()
                    mixT = sb(st_m, "mixT", [128, 8, SEQ], BF16); b_mix = [[S.newbuf() for _ in range(4)] for _ in range(8)]
                    gts = [sb(st_m, f"gt{i}", [128, 512], F32) for i in range(4)]; b_gts = [S.newbuf() for _ in range(4)]
                    m1s = [sb(st_m, f"m1{i}", [128, 512], F32) for i in range(4)]; b_m1s = [S.newbuf() for _ in range(4)]
                    rr = 0
                    for ft in range(8):
                        wsg, wbg = w_get(("gate", li, ft))
                        wvg = wview(wsg, 8, 256)
                        wsp, wbp = w_get(("proj", li, ft))
                        wvp = wsp[:, 0:1024].rearrange("p (m c n) -> p m c n", m=2, c=4)
                        for G in range(4):
                            ts = slice(G * 512, (G + 1) * 512)
                            rf, rb = hT_rhs(G)
                            res = []
                            for m in range(2):
                                pg, bpg = proj_fm(lambda c: wvg[:, c, m * 128:(m + 1) * 128], G, 8, rf, rb, wbg)
                                gt, bgt = gts[rr % 4], b_gts[rr % 4]
                                S.op("act", ACTF(gt[:], pg[:], AF.Sigmoid, bias=pc(li, C_BG + m * 8 + ft)), [bpg, b_pc], [bgt])
                                dump("gt0", gt[:], [bgt])
                                pp, bpp = proj_fm(lambda c: wvp[:, m, c, :], G, 4, lambda c: oT[:, m * 4 + c, ts],
                                                  [boT[m * 4 + c][G] for c in range(4)], wbp)
                                m1, bm1 = m1s[rr % 4], b_m1s[rr % 4]
                                rr += 1
                                S.op("dve", TT(m1[:], pp[:], gt[:], ALU.mult), [bpp, bgt], [bm1])
                                res.append((m1, bm1))
                            S.op("dve", TT(mixT[:, ft, ts], res[0][0][:], res[1][0][:], ALU.add), [res[0][1], res[1][1]], [b_mix[ft][G]])
                    dump("mixT", mixT[:], [b for r in b_mix for b in r])
                    for f2 in range(4):
                        ws, wb = w_get(("wo", li, f2))
                        wv = wview(ws, 8, 256)
                        for t2 in range(2):
                            fo = f2 * 2 + t2
                            for G in range(4):
                                ts = slice(G * 512, (G + 1) * 512)
                                po, bpo = proj_fm(lambda c: wv[:, c, t2 * 128:(t2 + 1) * 128], G, 8, lambda c: mixT[:, c, ts],
                                                  [b_mix[c][G] for c in range(8)], wb)
                                S.op("dve", TT(xT[:, fo, ts], po[:], xT[:, fo, ts], ALU.add), [bpo, bx[fo][G]], [bx[fo][G]])
                S.mark_barrier()
            S.mark_barrier()

            dump("xmid", xT[:], [b for r in bx for b in r])
            chk("M")
            rmsnorm_to_hT(li, C_LN2)
            with contextlib.ExitStack() as st_f:
                S.mark_barrier()
                actT = sb(st_f, "actT", [128, 11, SEQ], BF16); b_act = [[S.newbuf() for _ in range(4)] for _ in range(11)]
                raws = [sb(st_f, f"raw{i}", [128, SEQ + 2], F32) for i in range(2)]; b_raws = [S.newbuf() for _ in range(2)]
                us = [sb(st_f, f"u{i}", [128, SEQ], F32) for i in range(2)]; b_us = [S.newbuf() for _ in range(2)]
                sg = sb(st_f, "sg", [128, SEQ], BF16); b_sg = S.newbuf()
                for i2 in range(2):
                    S.op("pool", MEMSET(raws[i2][:, 0:1], 0.0), [], [b_raws[i2]])
                    S.op("pool", MEMSET(raws[i2][:, SEQ + 1:SEQ + 2], 0.0), [b_raws[i2]], [b_raws[i2]])
                tcount = 0
                pending = [None]

                def make_conv(raw, braw, u, bu, ct, which, i):
                    def conv():
                        S.op("dve", STT(u[:], raw[:, 0:SEQ], pc(li, C_CW + ct), u[:], ALU.mult, ALU.add), [braw, bu, b_pc], [bu])
                        S.op("dve", STT(u[:], raw[:, 2:SEQ + 2], pc(li, C_CW + 88 + ct), u[:], ALU.mult, ALU.add), [braw, bu, b_pc], [bu])
                        if which == 1:
                            S.op("act", ACTF(sg[:], u[:], AF.Silu), [bu, b_sg], [b_sg])
                        else:
                            S.op("dve", TT(actT[:, i, :], sg[:], u[:], ALU.mult), [b_sg, bu], b_act[i])
                    return conv

                for half in range(2):
                    for i in range(11):
                        p = half * 11 + i
                        ws, wb = w_get(("up", li, p))
                        wv = wview(ws, 8, 256)
                        for which in (1, 0):
                            ct = p + 22 * which
                            raw, braw = raws[tcount % 2], b_raws[tcount % 2]
                            u, bu = us[tcount % 2], b_us[tcount % 2]
                            tcount += 1
                            for Gp in range(2):
                                kp = rot["fpair"] % 4
                                rot["fpair"] += 1
                                pp, bb = ppair[kp], [bbank[2 * kp], bbank[2 * kp + 1]]
                                for t in range(2):
                                    rf, rb = hT_rhs(2 * Gp + t)
                                    S.op("pe", [MM(pp[:, t * 512:(t + 1) * 512], wv[:, c, which * 128:(which + 1) * 128], rf(c),
                                                   start=(c == 0), stop=(c == 7)) for c in range(8)], [wb] + rb, [bb[t]])
                                S.op("act", ACTF(raw[:, 1 + Gp * 1024:1 + (Gp + 1) * 1024], pp[:], AF.Copy), bb + [braw], [braw])
                                S.op("act", ACTF(u[:, Gp * 1024:(Gp + 1) * 1024], pp[:], AF.Identity, scale=pc(li, C_CW + 44 + ct), bias=pc(li, C_CB + ct)),
                                     bb + [b_pc, bu], [bu])
                            if pending[0] is not None:
                                pending[0]()
                            pending[0] = make_conv(raw, braw, u, bu, ct, which, i)
                    if pending[0] is not None:
                        pending[0]()
                        pending[0] = None
                    for fo in range(8):
                        ws, wb = w_get(("down", li, half, fo))
                        wv = wview(ws, 11, 128)
                        if fo == 0:
                            opened = []
                            for G in range(4):
                                ts = slice(G * 512, (G + 1) * 512)
                                po, bpo = ps_next()
                                opened.append((po, bpo))
                                S.op("pe", [MM(po[:], wv[:, k, :], actT[:, k, ts], start=(k == 0), stop=False) for k in range(10)],
                                     [wb] + [b_act[k][G] for k in range(10)], [bpo])
                            for G in range(4):
                                ts = slice(G * 512, (G + 1) * 512)
                                po, bpo = opened[G]
                                S.op("pe", MM(po[:], wv[:, 10, :], actT[:, 10, ts], start=False, stop=True), [wb, b_act[10][G]], [bpo])
                                S.op("dve", TT(xT[:, fo, ts], po[:], xT[:, fo, ts], ALU.add), [bpo, bx[fo][G]], [bx[fo][G]])
                            continue
                        for G in range(4):
                            ts = slice(G * 512, (G + 1) * 512)
                            po, bpo = proj_fm(lambda k: wv[:, k, :], G, 11, lambda k: actT[:, k, ts], [b_act[k][G] for k in range(11)], wb)
                            S.op("dve", TT(xT[:, fo, ts], po[:], xT[:, fo, ts], ALU.add), [bpo, bx[fo][G]], [bx[fo][G]])
            S.mark_barrier()

        S.enabled = True
        allx = [b for row in bx for b in row]
        for c in range(8):
            S.dma("sp", ("out", c), [(out_d[c * 128:(c + 1) * 128, :], xT[:, c, :])], reads=bx[c])
        S.wait_all("sp", allx)
        assert stop is not None or wst["idx"] == len(plan)
        S.emit()
    return nc


def _t5_bucket_np(rel):
    rel = np.asarray(rel, np.int64)
    half, max_exact = 16, 8
    ret = np.where(rel > 0, half, 0)
    n = np.abs(rel)
    nf = np.maximum(n, 1).astype(np.float32)
    large = max_exact + (np.log(nf / np.float32(max_exact)) / np.float32(math.log(128 / max_exact))
                         * np.float32(half - max_exact)).astype(np.int32)
    large = np.minimum(large, half - 1)
    return ret + np.where(n < max_exact, n, large)


def _host_layout(inp, layers):
    f32 = np.float32
    L = list(layers)
    g = lambda k: np.asarray(inp[k], f32)
    pcols = np.zeros((128, len(L) * NCOL), f32)
    brow = np.zeros((len(L), NBROW), f32)
    for li, l in enumerate(L):
        pcl = np.zeros((128, NCOL), f32)
        pcl[:, C_LN1:C_LN1 + 8] = g("ln1_g")[l].reshape(8, 128).T
        pcl[:, C_LN2:C_LN2 + 8] = g("ln2_g")[l].reshape(8, 128).T
        pcl[:, C_GQA] = np.tile(g("qn_a")[l], 2)
        pcl[:, C_GKA] = np.tile(g("kn_a")[l], 2)
        pcl[:, C_GQB] = np.tile(g("qn_b")[l], 2)
        pcl[:, C_GKB] = np.tile(g("kn_b")[l], 2)
        pcl[:, C_BG:C_BG + 16] = g("b_gate")[l].reshape(16, 128).T
        pcl[:, C_CW:C_CW + 132] = g("conv_w")[l].reshape(3, 44, 128).transpose(2, 0, 1).reshape(128, 132)
        pcl[:, C_CB:C_CB + 44] = g("conv_b")[l].reshape(44, 128).T
        pcols[:, li * NCOL:(li + 1) * NCOL] = pcl
        brow[li] = np.concatenate([g("subln_g")[l], g("sink")[l][SINK_PERM], g("lam_q1")[l], g("lam_k1")[l],
                                   g("lam_q2")[l], g("lam_k2")[l]])
    tab = g("rel_bias")
    cfar = np.concatenate([tab[15, 8:12], tab[31, 8:12]]).astype(f32)
    k = np.arange(128)[:, None]
    c = np.arange(1408)[None, :]
    bk = _t5_bucket_np(k - c + 640)
    stripB = np.stack([tab[bk, 8 + h] for h in range(4)]).astype(f32)
    q = np.arange(128)[None, :]
    biasA = np.zeros((128, 2, 3, 4, 128), f32)
    for di in range(3):
        rel = k + 128 * (di - 1) - q
        bkt = _t5_bucket_np(rel)
        ok = np.abs(rel) <= 128
        for kap in range(2):
            for cb in range(4):
                h = 4 * kap + 2 * (cb % 2) + cb // 2
                biasA[:, kap, di, cb, :] = np.where(ok, tab[bkt, h], f32(MASK_NEG))
    bones = np.zeros((128, 128), f32)
    bones[:64, :64] = 1.0 / 64
    bones[64:, 64:] = 1.0 / 64
    common = {
        "w_in": np.ascontiguousarray(g("w_in")[L]), "w_gate": np.ascontiguousarray(g("w_gate")[L]),
        "w_a_proj": np.ascontiguousarray(g("w_a_proj")[L]), "w_b_proj": np.ascontiguousarray(g("w_b_proj")[L]),
        "w_o": np.ascontiguousarray(g("w_o")[L]), "w_up": np.ascontiguousarray(g("w_up")[L]),
        "w_down": np.ascontiguousarray(g("w_down")[L]),
        "pcols": pcols, "brow": brow, "cfar": cfar, "stripB": stripB,
        "biasA": np.ascontiguousarray(biasA.reshape(128, 2 * 3 * 512)),
        "ident": np.eye(128, dtype=f32), "bones": bones,
    }
    return common


_PROGS = {}


def _run(layers, xT_list, inp):
    key = tuple(layers)
    if key not in _PROGS:
        _PROGS[key] = build_program(list(layers))
    nc = _PROGS[key]
    common = _host_layout(inp, layers)
    in_maps = [dict(common, xT=xT_list[b]) for b in range(NCORES)]
    res = run_bass_kernel_spmd(nc, in_maps, core_ids=list(range(NCORES)))
    return [np.asarray(r["outT"]) for r in res.results]


FUSED = True


def kernel(**inputs):
    x = np.asarray(inputs["x"], np.float32)
    xT = [np.ascontiguousarray(x[b].T) for b in range(NCORES)]
    if FUSED:
        xT = _run(range(DEPTH), xT, inputs)
    else:
        for l in range(DEPTH):
            xT = _run([l], xT, inputs)
    return np.stack([t.T for t in xT]).astype(np.float32)
```
